# Optimizing a Trainium2 kernel written in Bass

```python
import jax, jax.numpy as jnp
from jax import lax
import numpy as np

D_MODEL = 2048
BATCH = 8
SEQ = 4096
DEPTH = 1
DEC_BATCH = 16
DEC_SEQ = 2048
PAST_LEN = 128

N_MEM = 256
W_BR = D_MODEL
N_BRANCH = 3
NH_M = 8
HD_M = W_BR // NH_M
CHUNK = 128
CONV_W = 3
NH_A = 4
HD_A = W_BR // NH_A
N_IN = 5 * W_BR + 4 * NH_M + 4 * W_BR + 2 * W_BR + N_BRANCH * D_MODEL
EPS = 1e-6

kernel_name = 'bidir_mlstm_shortconv_memattn_hybrid'


def rmsnorm(x, g):
    xf = x.astype(jnp.float32)
    y = xf * lax.rsqrt(jnp.mean(xf * xf, axis=-1, keepdims=True) + EPS) * g.astype(jnp.float32)
    return y.astype(x.dtype)


def split_in(p):
    sizes = [W_BR] * 5 + [4 * NH_M] + [W_BR] * 4 + [W_BR] * 2 + [N_BRANCH * D_MODEL]
    idx = np.cumsum(sizes)[:-1].tolist()
    return jnp.split(p, idx, axis=-1)


def mlstm_one_direction(q, k, v, i_pre, f_pre):
    B, H, S, d = q.shape
    nc = S // CHUNK

    def to_chunks(t):
        return jnp.moveaxis(t.reshape((B, H, nc, CHUNK) + t.shape[3:]), 2, 0)

    xs = (to_chunks(q), to_chunks(k), to_chunks(v), to_chunks(i_pre), to_chunks(jax.nn.log_sigmoid(f_pre)))
    tril = jnp.tril(jnp.ones((CHUNK, CHUNK), dtype=bool))

    def body(carry, xc):
        C, n, m = carry
        qc, kc, vc, ic, lfc = xc
        b = jnp.cumsum(lfc, axis=-1)
        dmat = b[..., :, None] - b[..., None, :] + ic[..., None, :]
        dmat = jnp.where(tril, dmat, -jnp.inf)
        inter = b + m[..., None]
        m_t = jnp.maximum(jnp.max(dmat, axis=-1), inter)
        s = jnp.einsum('bhtd,bhsd->bhts', qc, kc) * jnp.exp(dmat - m_t[..., None])
        a = jnp.exp(inter - m_t)
        num = jnp.einsum('bhts,bhsd->bhtd', s, vc) + a[..., None] * jnp.einsum('bhtd,bhde->bhte', qc, C)
        den = jnp.sum(s, axis=-1) + a * jnp.einsum('bhtd,bhd->bht', qc, n)
        h = num / jnp.maximum(jnp.abs(den), jnp.exp(-m_t))[..., None]
        bl = b[..., -1]
        wlog = bl[..., None] - b + ic
        m_new = jnp.maximum(bl + m, jnp.max(wlog, axis=-1))
        w = jnp.exp(wlog - m_new[..., None])
        decay = jnp.exp(bl + m - m_new)
        kw = kc * w[..., None]
        C_new = decay[..., None, None] * C + jnp.einsum('bhsd,bhse->bhde', kw, vc)
        n_new = decay[..., None] * n + jnp.sum(kw, axis=2)
        return (C_new, n_new, m_new), h

    init = (jnp.zeros((B, H, d, d), jnp.float32), jnp.zeros((B, H, d), jnp.float32),
            jnp.zeros((B, H), jnp.float32))
    _, hs = lax.scan(body, init, xs)
    return jnp.moveaxis(hs, 0, 2).reshape(B, H, S, d)


def mlstm_branch(q_m, k_m, v_m, o_m, z_m, gates, b_if, mh_g):
    B, S, _ = q_m.shape
    f32 = jnp.float32

    def heads(t):
        return t.reshape(B, S, NH_M, HD_M).transpose(0, 2, 1, 3).astype(f32)

    q = heads(q_m)
    k = heads(k_m) * (HD_M ** -0.5)
    v = heads(v_m)
    g = (gates.astype(f32) + b_if.astype(f32)).transpose(0, 2, 1)
    i_f, i_b, f_f, f_b = jnp.split(g, 4, axis=1)
    h_fwd = mlstm_one_direction(q, k, v, i_f, f_f)
    fl = lambda t: jnp.flip(t, axis=2)
    h_bwd = fl(mlstm_one_direction(fl(q), fl(k), fl(v), fl(i_b), fl(f_b)))
    h = h_fwd + h_bwd
    mu = jnp.mean(h, axis=-1, keepdims=True)
    var = jnp.mean((h - mu) ** 2, axis=-1, keepdims=True)
    h = (h - mu) * lax.rsqrt(var + EPS)
    h = h.transpose(0, 2, 1, 3).reshape(B, S, W_BR) * mh_g.astype(f32)
    h = h.astype(q_m.dtype)
    return h * jax.nn.sigmoid(o_m) * jax.nn.silu(z_m)


def shortconv_branch(cb, cc, cx, z_c, conv_w):
    u = cc * cx
    up = jnp.pad(u, ((0, 0), (1, 1), (0, 0)))
    y = conv_w[0] * up[:, :-2] + conv_w[1] * up[:, 1:-1] + conv_w[2] * up[:, 2:]
    return cb * y * jax.nn.silu(z_c)


def memattn_branch(q_a, z_a, mem, mem_g, w_kv):
    B, S, _ = q_a.shape
    kv = rmsnorm(mem, mem_g) @ w_kv
    km, vm = jnp.split(kv, 2, axis=-1)
    q = q_a.reshape(B, S, NH_A, HD_A)
    km = km.reshape(B, N_MEM, NH_A, HD_A)
    vm = vm.reshape(B, N_MEM, NH_A, HD_A)
    s = jnp.einsum('bshd,bmhd->bhsm', q, km).astype(jnp.float32) * (HD_A ** -0.5)
    p = jax.nn.softmax(s, axis=-1).astype(vm.dtype)
    o = jnp.einsum('bhsm,bmhd->bshd', p, vm).reshape(B, S, W_BR)
    return o * jax.nn.silu(z_a)


def hybrid_layer(x, mem, norm_g, w_in, b_if, conv_w, mem_norm_g, w_kv_mem, mh_norm_g, w_branch, w_out):
    B, S, _ = x.shape
    h = rmsnorm(x, norm_g)
    (q_m, k_m, v_m, o_m, z_m, gates, cb, cc, cx, z_c, q_a, z_a, mg) = split_in(h @ w_in)
    y_m = mlstm_branch(q_m, k_m, v_m, o_m, z_m, gates, b_if, mh_norm_g)
    y_c = shortconv_branch(cb, cc, cx, z_c, conv_w)
    y_a = memattn_branch(q_a, z_a, mem, mem_norm_g, w_kv_mem)
    mg = jax.nn.sigmoid(mg.reshape(B, S, N_BRANCH, D_MODEL))
    merged = (mg[:, :, 0] * (y_m @ w_branch[0])
              + mg[:, :, 1] * (y_c @ w_branch[1])
              + mg[:, :, 2] * (y_a @ w_branch[2]))
    return x + merged @ w_out


def setup_inputs(seed: int = 0) -> dict:
    key = jax.random.key(seed)
    ks = jax.random.split(key, 16)
    nrm = jax.random.normal
    x_prompt = nrm(ks[0], (BATCH, SEQ, D_MODEL), jnp.float32)
    x_sample = nrm(ks[1], (DEC_BATCH, DEC_SEQ, D_MODEL), jnp.float32)
    mem_prompt = nrm(ks[2], (BATCH, N_MEM, D_MODEL), jnp.float32)
    mem_sample = nrm(ks[3], (DEC_BATCH, N_MEM, D_MODEL), jnp.float32)
    norm_g = 1.0 + 0.02 * nrm(ks[4], (DEPTH, D_MODEL), jnp.float32)
    w_in = nrm(ks[5], (DEPTH, D_MODEL, N_IN), jnp.float32) * (D_MODEL ** -0.5)
    b_i = 0.1 * nrm(ks[6], (DEPTH, 2 * NH_M), jnp.float32)
    b_f = jnp.tile(jnp.linspace(3.0, 6.0, NH_M, dtype=jnp.float32), 2)[None] + 0.1 * nrm(ks[7], (DEPTH, 2 * NH_M), jnp.float32)
    b_if = jnp.concatenate([b_i, b_f], axis=-1)
    conv_w = nrm(ks[8], (DEPTH, CONV_W, W_BR), jnp.float32) * (CONV_W ** -0.5)
    mem_norm_g = 1.0 + 0.02 * nrm(ks[9], (DEPTH, D_MODEL), jnp.float32)
    w_kv_mem = nrm(ks[10], (DEPTH, D_MODEL, 2 * W_BR), jnp.float32) * (D_MODEL ** -0.5)
    mh_norm_g = 1.0 + 0.02 * nrm(ks[11], (DEPTH, W_BR), jnp.float32)
    w_branch = nrm(ks[12], (DEPTH, N_BRANCH, W_BR, D_MODEL), jnp.float32) * (W_BR ** -0.5)
    w_out = nrm(ks[13], (DEPTH, D_MODEL, D_MODEL), jnp.float32) * (D_MODEL ** -0.5)
    final_norm_g = 1.0 + 0.02 * nrm(ks[14], (D_MODEL,), jnp.float32)
    return {'x_prompt': x_prompt, 'x_sample': x_sample, 'mem_prompt': mem_prompt, 'mem_sample': mem_sample,
            'norm_g': norm_g, 'w_in': w_in, 'b_if': b_if, 'conv_w': conv_w, 'mem_norm_g': mem_norm_g,
            'w_kv_mem': w_kv_mem, 'mh_norm_g': mh_norm_g, 'w_branch': w_branch, 'w_out': w_out,
            'final_norm_g': final_norm_g}


def reference(x_prompt, x_sample, mem_prompt, mem_sample, norm_g, w_in, b_if, conv_w, mem_norm_g,
              w_kv_mem, mh_norm_g, w_branch, w_out, final_norm_g):
    def trunk(x, mem):
        for l in range(DEPTH):
            x = hybrid_layer(x, mem, norm_g[l], w_in[l], b_if[l], conv_w[l], mem_norm_g[l], w_kv_mem[l],
                             mh_norm_g[l], w_branch[l], w_out[l])
        return rmsnorm(x, final_norm_g)

    y_prompt = trunk(x_prompt, mem_prompt)
    y_sample = trunk(x_sample, mem_sample)
    return (y_prompt, y_sample)
```

```python
import contextlib
import numpy as np
import concourse.bass as bass
import concourse.mybir as mybir
from concourse.bass_utils import run_bass_kernel_spmd

F32 = mybir.dt.float32
BF16 = mybir.dt.bfloat16
AF = mybir.ActivationFunctionType
ALU = mybir.AluOpType

D = 2048
KT = 16
TB = 256
NCH = TB // 128
NH = 8
HD = 256
NHA = 4
HDA = 512
NMEM = 256
EPS = 1e-6
VS = 264
NSLOT = 5
import os
VARIANT = os.environ.get("KVARIANT", "")
INTERLEAVE_BC = os.environ.get("KBC", "0") == "1"
CPIPE = os.environ.get("KCPIPE", "1") == "1"
DIAG_ENG = os.environ.get("KDIAG", "dve")
FIN = os.environ.get("KFIN", "dve4")
CONV_ENG = os.environ.get("KCONV", "actpool")
SQD = float(np.sqrt(D))

C_ID, C_ONE, C_TF, C_TB, C_G, C_GM, C_CW, C_BIF, C_MHG, C_FG = 0, 128, 256, 384, 512, 528, 544, 592, 624, 640
C_ZERO = 2688
C_END = 2944


def unit_catalog():
    units = []
    idx = {}

    def add(key, src, cols):
        idx[key] = len(units)
        units.append((src, np.asarray(cols, dtype=np.int64)))

    r = np.arange
    OQ, OK_, OV, OO, OZ = 0, 2048, 4096, 6144, 8192
    OCB, OCC, OCX, OZC = 10272, 12320, 14368, 16416
    OQA, OZA, OMG = 18464, 20512, 22560
    for h in range(NH):
        add(("q", h), "in", OQ + h * 256 + r(256))
        add(("k", h), "in", OK_ + h * 256 + r(256))
        add(("v", h), "in", OV + h * 256 + r(256))
    for u in range(8):
        add(("kk", u), "kv", u * 256 + r(256))
    for u in range(8):
        add(("kv", u), "kv", 2048 + u * 256 + r(256))
    for ft in range(16):
        add(("b1", ft), "in", np.concatenate([OCB + ft * 128 + r(128), OCC + ft * 128 + r(128)]))
        add(("b2", ft), "in", np.concatenate([OCX + ft * 128 + r(128), OZC + ft * 128 + r(128)]))
    for h in range(NH):
        add(("o", h), "in", OO + h * 256 + r(256))
        add(("z", h), "in", OZ + h * 256 + r(256))
    for h in range(NHA):
        for j in range(2):
            add(("qa", h, j), "in", OQA + h * 512 + j * 256 + r(256))
        for j in range(2):
            add(("za", h, j), "in", OZA + h * 512 + j * 256 + r(256))
    for pr in range(8):
        for b in range(3):
            add(("mg", pr, b), "in", OMG + b * 2048 + pr * 256 + r(256))
        for b in range(3):
            add(("wb", pr, b), "br%d" % b, pr * 256 + r(256))
    for u in range(8):
        add(("out", u), "out", u * 256 + r(256))
    return units, idx


class Buf:
    __slots__ = ("name", "last_w", "readers", "const")

    def __init__(self, name, const=False):
        self.name = name
        self.last_w = None
        self.readers = []
        self.const = const


class Op:
    __slots__ = ("eng", "fns", "deps", "signal", "ev", "dma", "pre")

    def __init__(self, eng, fns, dma):
        self.eng = eng
        self.fns = fns
        self.dma = dma
        self.deps = []
        self.signal = False
        self.ev = None
        self.pre = None


class Prog:
    ENG = ["pe", "act", "dve", "pool", "sp"]

    def __init__(self):
        self.ops = {e: [] for e in self.ENG}
        self.nops = 0

    def op(self, eng, fns, reads=(), writes=(), dma=False, extra=()):
        if not isinstance(fns, (list, tuple)):
            fns = [fns]
        o = Op(eng, list(fns), dma)
        deps = {}
        for b in reads:
            if b.last_w is not None:
                deps[id(b.last_w)] = b.last_w
        for b in writes:
            if b.last_w is not None:
                deps[id(b.last_w)] = b.last_w
            for rd in b.readers:
                deps[id(rd)] = rd
        for x in extra:
            deps[id(x)] = x
        deps.pop(id(o), None)
        for d in deps.values():
            if eng == "pe" and d.eng == "pe" and not d.dma:
                continue
            d.signal = True
            o.deps.append(d)
        for b in reads:
            if not b.const:
                b.readers.append(o)
        for b in writes:
            b.last_w = o
            b.readers = []
        self.ops[eng].append(o)
        self.nops += 1
        return o

    def emit(self, nc, stack):
        esem = {e: stack.enter_context(nc.semaphore("es_" + e)) for e in ["pe", "act", "dve", "pool"]}
        npool = {"sp": int(os.environ.get("KSPQ", "4")), "pool": 4, "act": 4}
        dsem = {e: [stack.enter_context(nc.semaphore("ds_%s%d" % (e, i))) for i in range(n)] for e, n in npool.items()}
        for e, lst in self.ops.items():
            cnt = 0
            rr = 0
            use = [0] * npool.get(e, 0)
            for o in lst:
                if o.dma:
                    k = rr % len(use)
                    rr += 1
                    o.pre = (dsem[e][k], use[k] * 16)
                    use[k] += 1
                    o.ev = (dsem[e][k], use[k] * 16)
                elif o.signal:
                    cnt += 1
                    o.ev = (esem[e], cnt)
        block = stack.enter_context(nc.Block())
        ops = self.ops

        def make(e_name):
            def body(E):
                waited = {}
                for o in ops[e_name]:
                    need = [d.ev for d in o.deps]
                    if o.dma and o.pre[1] > 0:
                        need.append(o.pre)
                    for (s, v) in need:
                        k = id(s)
                        if waited.get(k, 0) < v:
                            E.wait_ge(s, v)
                            waited[k] = v
                    ins = None
                    for fn in o.fns:
                        ins = fn(E)
                    if o.dma:
                        ins.then_inc(o.ev[0], 16)
                    elif o.signal:
                        ins.then_inc(o.ev[0], 1)
            return body

        block.tensor(make("pe"))
        block.scalar(make("act"))
        block.vector(make("dve"))
        block.gpsimd(make("pool"))
        block.sync(make("sp"))


class Rot:
    def __init__(self, items):
        self.items = items
        self.i = 0

    def get(self):
        it = self.items[self.i % len(self.items)]
        self.i += 1
        return it


class _Stop(Exception):
    pass


def build_program(seq_lens, stop=None):
    units, uidx = unit_catalog()
    NU = len(units)
    NSEQ = len(seq_lens)
    NTOK = int(sum(seq_lens))
    seq_off = [int(sum(seq_lens[:i])) for i in range(NSEQ)]
    nblk = [L // TB for L in seq_lens]
    blk_base = [int(sum(nblk[:i])) for i in range(NSEQ)]
    NBLK = int(sum(nblk))
    halo_idx = {}
    for s in range(NSEQ):
        for bi in range(nblk[s] - 1):
            halo_idx[(s, bi)] = len(halo_idx)
    NHALO = 32
    assert len(halo_idx) <= NHALO

    nc = bass.Bass("TRN2", target_bir_lowering=False)
    dt_ = nc.dram_tensor
    x_tok = dt_("x_tok", [NTOK, D], F32, kind="ExternalInput").ap()
    x_T = dt_("x_T", [NBLK, 128, KT * TB], F32, kind="ExternalInput").ap()
    m_tok = dt_("m_tok", [NSEQ * NMEM, D], F32, kind="ExternalInput").ap()
    m_T = dt_("m_T", [NSEQ, 128, KT * NMEM], F32, kind="ExternalInput").ap()
    h_tok = dt_("h_tok", [NHALO, D], F32, kind="ExternalInput").ap()
    h_T = dt_("h_T", [128, KT * NHALO], F32, kind="ExternalInput").ap()
    ws32 = dt_("ws32", [NU, 128, KT * 256], F32, kind="ExternalInput").ap()
    wg32 = dt_("wg32", [128, KT * 32], F32, kind="ExternalInput").ap()
    cst_d = dt_("cst_in", [128, C_END], F32, kind="ExternalInput").ap()
    y_out = dt_("y", [NTOK, D], F32, kind="ExternalOutput").ap()
    ws16 = dt_("ws16", [NU, 128, KT * 256], BF16, kind="Internal").ap()
    hb_d = dt_("hb", [NTOK, D], F32, kind="Internal").ap()
    km_d = dt_("km", [NSEQ, 128, KT * NMEM], BF16, kind="Internal").ap()
    vm_d = dt_("vm", [NSEQ, 128, 2 * D], BF16, kind="Internal").ap()

    P = Prog()
    stack = contextlib.ExitStack()
    with stack:
        def sb(name, shape, dtype):
            return stack.enter_context(nc.sbuf_tensor(name, shape, dtype))

        cst = sb("cst", [128, C_END], F32)
        cstb = Buf("cst", const=True)
        identb = sb("identb", [128, 128], BF16)
        identbb = Buf("identb", const=True)
        wg = sb("wg", [128, KT * 32], BF16)
        wgb = Buf("wg", const=True)
        xtok = [sb("xtok%d" % c, [128, D], F32) for c in range(NCH)]
        xtokb = [Buf("xtok%d" % c) for c in range(NCH)]
        xts = Rot([(sb("xts%d" % i, [128, 4 * TB], F32), Buf("xts%d" % i)) for i in range(2)])
        class _H:
            pass
        H = _H()
        hTs = [(sb("hT0", [128, KT * TB], BF16), Buf("hT0")), (sb("hT1", [128, KT * TB], BF16), Buf("hT1"))]
        H.t, H.b = hTs[0]
        ssq8 = sb("ssq8", [128, 8], F32)
        ssq8b = Buf("ssq8")
        junk = sb("junk", [128, D], BF16)
        junkb = Buf("junk")
        ssq = sb("ssq", [128, 4], F32)
        ssqb = Buf("ssq")
        rstd = sb("rstd", [128, 4], F32)
        rstdb = Buf("rstd")
        lnt = sb("lnt", [128, 4], F32)
        lntb = Buf("lnt")
        diag = sb("diag", [128, 256], F32)
        diagb = Buf("diag")
        rbc = sb("rbc", [128, TB], F32)
        rbcb = Buf("rbc")
        ymT = sb("ymT", [128, KT * TB], BF16)
        ymTb = [Buf("ymT%d" % h) for h in range(NH)]
        ycT = sb("ycT", [128, KT * TB], BF16)
        ycTb = [Buf("ycT%d" % i) for i in range(KT)]
        yaT = sb("yaT", [128, KT * TB], BF16)
        yaTb = [Buf("yaT%d" % h) for h in range(NHA)]
        mgT = sb("mgT", [128, KT * TB], BF16)
        mgTb = [Buf("mgT%d" % i) for i in range(KT)]
        U = sb("U", [128, NH * 2 * VS], F32)
        Ub = [Buf("U%d" % h) for h in range(NH)]
        cbf = Rot([(sb("cbf%d" % i, [128, 2 * VS], BF16), Buf("cbf%d" % i)) for i in range(2)])
        GT = sb("GT", [128, (NCH + 1) * 24], F32)
        GTb = [Buf("GT%d" % i) for i in range(NCH + 1)]
        gsb = sb("gsb", [128, 16], F32)
        gsbb = Buf("gsb")
        e1 = sb("e1", [128, 8], F32)
        e1b = Buf("e1")
        nlf = sb("nlf", [128, 8], F32)
        nlfb = Buf("nlf")
        ipn = sb("ipn", [128, 8], F32)
        ipnb = Buf("ipn")
        uhalo = sb("uhalo", [128, KT * NHALO], F32)
        uhalob = Buf("uhalo")
        ulast = sb("ulast", [128, KT], F32)
        ulastb = [Buf("ulast%d" % i) for i in range(KT)]
        sm = Rot([(sb("sm%d" % i, [128, 8], F32), Buf("sm%d" % i)) for i in range(12)])
        qTs = Rot([(sb("qT%d" % i, [128, 2 * TB], BF16), Buf("qT%d" % i)) for i in range(2)])
        kTs = Rot([(sb("kT%d" % i, [128, 2 * TB], BF16), Buf("kT%d" % i)) for i in range(2)])
        kts = Rot([(sb("ktok%d" % i, [128, NCH * 256], BF16), Buf("ktok%d" % i)) for i in range(2)])
        vxs = Rot([(sb("vext%d" % i, [128, NCH * VS], BF16), Buf("vext%d" % i)) for i in range(2)])
        gozs = Rot([(sb("goz%d" % i, [128, NCH * 256], F32), Buf("goz%d" % i)) for i in range(2)])
        qas = Rot([(sb("qaT%d" % i, [128, 4 * TB], BF16), Buf("qaT%d" % i)) for i in range(2)])
        pTs = Rot([(sb("pT%d" % i, [128, 2 * TB], BF16), Buf("pT%d" % i)) for i in range(2)])
        kmh = Rot([(sb("kmh%d" % i, [128, 4 * NMEM], BF16), Buf("kmh%d" % i)) for i in range(2)])
        vmh = Rot([(sb("vmh%d" % i, [128, 2 * HDA], BF16), Buf("vmh%d" % i)) for i in range(2)])
        tF = Rot([(sb("tF%d" % i, [128, 512], F32), Buf("tF%d" % i)) for i in range(7)])
        tH = Rot([(sb("tH%d" % i, [128, 512], BF16), Buf("tH%d" % i)) for i in range(3)])
        tFr = Rot([(sb("tFr%d" % i, [128, 512], F32), Buf("tFr%d" % i)) for i in range(4)])
        tHr = Rot([(sb("tHr%d" % i, [128, 256], BF16), Buf("tHr%d" % i)) for i in range(4)])
        smr = Rot([(sb("smr%d" % i, [128, 8], F32), Buf("smr%d" % i)) for i in range(16)])
        wr = [sb("wr%d" % i, [128, KT * 256], BF16) for i in range(NSLOT)]
        wrb = [Buf("wr%d" % i) for i in range(NSLOT)]
        wring = Rot(list(zip(wr, wrb)))
        NPF = 6
        psf = stack.enter_context(nc.psum_tensor("psf", [128, NPF * 512], F32))
        psfb = [Buf("psf%d" % i) for i in range(NPF)]
        psb = stack.enter_context(nc.psum_tensor("psb", [128, 2 * 1024], BF16))
        psbb = [Buf("psb%d" % i) for i in range(2)]
        ps_i = [0]
        psb_i = [0]

        def bank():
            i = ps_i[0] % NPF
            ps_i[0] += 1
            return psf[:, i * 512:(i + 1) * 512], psfb[i]

        def bankb():
            i = psb_i[0] % 2
            psb_i[0] += 1
            return psb[:, i * 1024:(i + 1) * 1024], psbb[i]

        wsb = [Buf("ws%d" % u, const=True) for u in range(NU)]
        hbb = {}
        km_ops = [[] for _ in range(NSEQ)]
        vm_ops = [[] for _ in range(NSEQ)]
        out_ops = []

        def cc(a, b):
            return cst[:, a:b]

        def dma(eng, out, in_, reads, writes, extra=()):
            return P.op(eng, lambda E: E.dma_start(out=out, in_=in_), reads, writes, dma=True, extra=extra)

        def mm(out, pairs, reads, writes):
            n = len(pairs)
            fns = [(lambda E, l=l, r=r, i=i: E.matmul(out, lhsT=l, rhs=r, start=(i == 0), stop=(i == n - 1)))
                   for i, (l, r) in enumerate(pairs)]
            return P.op("pe", fns, reads, writes)

        def act(out, in_, func, reads, writes, scale=1.0, bias=0.0, accum=None):
            if accum is None:
                return P.op("act", lambda E: E.activation(out=out, in_=in_, func=func, bias=bias, scale=scale), reads, writes)
            return P.op("act", lambda E: E.activation(out=out, in_=in_, func=func, bias=bias, scale=scale, accum_out=accum), reads, writes)

        def tt(eng, out, a, b, op, reads, writes):
            return P.op(eng, lambda E: E.tensor_tensor(out=out, in0=a, in1=b, op=op), reads, writes)

        def ts(eng, out, a, s1, s2, op0, op1, reads, writes):
            if s2 is None:
                return P.op(eng, lambda E: E.tensor_scalar(out=out, in0=a, scalar1=s1, scalar2=None, op0=op0), reads, writes)
            return P.op(eng, lambda E: E.tensor_scalar(out=out, in0=a, scalar1=s1, scalar2=s2, op0=op0, op1=op1), reads, writes)

        def stt(eng, out, a, s, b, op0, op1, reads, writes):
            if eng == "pool":
                P.op(eng, lambda E: E.tensor_scalar(out=out, in0=a, scalar1=s, scalar2=None, op0=op0), reads, writes)
                return P.op(eng, lambda E: E.tensor_tensor(out=out, in0=out, in1=b, op=op1), list(reads) + list(writes), writes)
            return P.op(eng, lambda E: E.scalar_tensor_tensor(out=out, in0=a, scalar=s, in1=b, op0=op0, op1=op1), reads, writes)

        def load_unit(key):
            u = uidx[key]
            w, b = wring.get()
            dma("sp", w[:, :], ws16[u], [wsb[u]], [b])
            return w, b

        def chk(tag):
            if stop == tag:
                raise _Stop()

        dma("sp", cst[:, :], cst_d[:, :], [], [cstb])
        act(identb[:, :], cst[:, C_ID:C_ID + 128], AF.Copy, [cstb], [identbb])
        st0, st0b = xts.get()
        dma("sp", st0[:, 0:KT * 32], wg32[:, :], [], [st0b])
        act(wg[:, :], st0[:, 0:KT * 32], AF.Copy, [st0b], [wgb])
        P.op("pool", lambda E: E.memset(U[:, :], 0.0), [], Ub)
        for u in range(NU):
            dma("pool", ws16[u], ws32[u], [], [wsb[u]])

        def norm_block(tok_aps, xT_ap, nT, gcol):
            pa, pb = bank()
            for c, (tap, rows) in enumerate(tok_aps):
                dma("sp", xtok[c][0:rows, :], tap, [], [xtokb[c]])
                act(junk[0:rows, :], xtok[c][0:rows, :], AF.Square, [xtokb[c]], [junkb, ssqb], accum=ssq[0:rows, c:c + 1])
                act(lnt[0:rows, c:c + 1], ssq[0:rows, c:c + 1], AF.Ln, [ssqb], [lntb], bias=float(D * EPS))
                act(rstd[0:rows, c:c + 1], lnt[0:rows, c:c + 1], AF.Exp, [lntb], [rstdb], scale=-0.5)
                stt("dve", diag[0:rows, c * 128:c * 128 + rows], cst[0:rows, C_ID:C_ID + rows], rstd[0:rows, c:c + 1],
                    cst[0:rows, C_ZERO:C_ZERO + rows], ALU.mult, ALU.add, [rstdb, cstb], [diagb])
            for c, (tap, rows) in enumerate(tok_aps):
                mm(pa[:, c * 128:c * 128 + rows], [(cst[0:rows, C_ONE:C_ONE + 128], diag[0:rows, c * 128:c * 128 + rows])], [diagb, cstb], [pb])
            act(rbc[:, 0:nT], pa[:, 0:nT], AF.Copy, [pb], [rbcb], scale=SQD)
            chk("norm1")
            for q in range(4):
                st, stb = xts.get()
                if nT == TB:
                    dma("sp", st[:, :], xT_ap[:, q * 4 * nT:(q + 1) * 4 * nT], [], [stb])
                else:
                    dma("sp", st[:, 0:4 * nT], xT_ap[:, q * 4 * nT:(q + 1) * 4 * nT], [], [stb])
                for j in range(4):
                    kt = q * 4 + j
                    stt("dve", H.t[:, kt * TB:kt * TB + nT], st[:, j * nT:(j + 1) * nT], cst[:, gcol + kt:gcol + kt + 1],
                        rbc[:, 0:nT], ALU.mult, ALU.mult, [stb, rbcb, cstb], [H.b])

        def norm_pre(tok0, xT_ap, dst_t, dst_b):
            loads = []
            for c in range(NCH):
                for p in range(4):
                    loads.append(("x", c, p))
            for q in range(4):
                loads.append(("t", q, 0))
            slots = {}

            def issue(i):
                kind, a, b = loads[i]
                st, stb = xts.get()
                if kind == "x":
                    dma("sp", st[:, 0:512], x_tok[tok0 + a * 128:tok0 + (a + 1) * 128, b * 512:(b + 1) * 512], [], [stb])
                else:
                    dma("sp", st[:, :], xT_ap[:, a * 4 * TB:(a + 1) * 4 * TB], [], [stb])
                slots[i] = (st, stb)

            issue(0)
            for i, (kind, a, b) in enumerate(loads):
                if i + 1 < len(loads):
                    issue(i + 1)
                st, stb = slots.pop(i)
                if kind == "x":
                    c, p = a, b
                    jk, jkb = tH.get()
                    act(jk[:, :], st[:, 0:512], AF.Square, [stb], [jkb, ssq8b], accum=ssq8[:, c * 4 + p:c * 4 + p + 1])
                    if p == 3:
                        P.op("dve", lambda E, c=c: E.reduce_sum(out=ssq[:, c:c + 1], in_=ssq8[:, c * 4:(c + 1) * 4], axis=mybir.AxisListType.X),
                             [ssq8b], [ssqb])
                        act(lnt[:, c:c + 1], ssq[:, c:c + 1], AF.Ln, [ssqb], [lntb], bias=float(D * EPS))
                        act(rstd[:, c:c + 1], lnt[:, c:c + 1], AF.Exp, [lntb], [rstdb], scale=-0.5)
                        stt("dve", diag[:, c * 128:(c + 1) * 128], cst[:, C_ID:C_ID + 128], rstd[:, c:c + 1],
                            cst[:, C_ZERO:C_ZERO + 128], ALU.mult, ALU.add, [rstdb, cstb], [diagb])
                        if c == NCH - 1:
                            pa, pb = bank()
                            for c2 in range(NCH):
                                mm(pa[:, c2 * 128:(c2 + 1) * 128], [(cst[:, C_ONE:C_ONE + 128], diag[:, c2 * 128:(c2 + 1) * 128])],
                                   [diagb, cstb], [pb])
                            act(rbc[:, 0:TB], pa[:, 0:TB], AF.Copy, [pb], [rbcb], scale=SQD)
                    yield
                else:
                    q = a
                    for j in range(4):
                        kt = q * 4 + j
                        stt("dve", dst_t[:, kt * TB:(kt + 1) * TB], st[:, j * TB:(j + 1) * TB], cst[:, C_G + kt:C_G + kt + 1],
                            rbc[:, 0:TB], ALU.mult, ALU.mult, [stb, rbcb, cstb], [dst_b])
                        if j % 2 == 1:
                            yield

        def kv_prologue(s):
          if True:
            norm_block([(m_tok[s * NMEM + c * 128:s * NMEM + (c + 1) * 128, :], 128) for c in range(2)], m_T[s], NMEM, C_GM)
            chk("norm")
            for u in range(8):
                w, wb_ = load_unit(("kk", u))
                pa, pb = bank()
                for j in range(2):
                    mm(pa[:, j * 256:(j + 1) * 256],
                       [(w[:, kt * 256 + j * 128:kt * 256 + (j + 1) * 128], H.t[:, kt * TB:(kt + 1) * TB]) for kt in range(KT)],
                       [wb_, H.b], [pb])
                t, tb_ = tH.get()
                act(t[:, :], pa, AF.Copy, [pb], [tb_])
                km_ops[s].append(dma("pool", km_d[s][:, 2 * u * NMEM:(2 * u + 2) * NMEM], t[:, :], [tb_], []))
            chk("km")
            for u in range(8):
                w, wb_ = load_unit(("kv", u))
                pa, pb = bank()
                for mt in range(2):
                    mm(pa[:, mt * 256:(mt + 1) * 256],
                       [(H.t[:, kt * TB + mt * 128:kt * TB + (mt + 1) * 128], w[:, kt * 256:(kt + 1) * 256]) for kt in range(KT)],
                       [wb_, H.b], [pb])
                t, tb_ = tH.get()
                act(t[:, :], pa, AF.Copy, [pb], [tb_])
                for mt in range(2):
                    vm_ops[s].append(dma("pool", vm_d[s][:, mt * D + u * 256:mt * D + (u + 1) * 256], t[:, mt * 256:(mt + 1) * 256], [tb_], []))

        def halo_prepass():
          norm_block([(h_tok[:, :], NHALO)], h_T, NHALO, C_G)
          for ft in range(16):
            w1, w1b = load_unit(("b1", ft))
            w2, w2b = load_unit(("b2", ft))
            pa, pb = bank()
            mm(pa[:, 0:NHALO], [(w1[:, kt * 256 + 128:kt * 256 + 256], H.t[:, kt * TB:kt * TB + NHALO]) for kt in range(KT)],
               [w1b, H.b], [pb])
            mm(pa[:, 256:256 + NHALO], [(w2[:, kt * 256:kt * 256 + 128], H.t[:, kt * TB:kt * TB + NHALO]) for kt in range(KT)],
               [w2b, H.b], [pb])
            t, tb_ = tF.get()
            act(t[:, 0:NHALO], pa[:, 0:NHALO], AF.Copy, [pb], [tb_])
            tt("dve", uhalo[:, ft * NHALO:(ft + 1) * NHALO], t[:, 0:NHALO], pa[:, 256:256 + NHALO], ALU.mult, [tb_, pb], [uhalob])

        def gates(ci, d, slot):
            pa, pb = bank()
            mm(pa[:, 0:16], [(H.t[:, kt * TB + ci * 128:kt * TB + (ci + 1) * 128], wg[:, kt * 32 + d * 16:kt * 32 + (d + 1) * 16])
                             for kt in range(KT)], [H.b, wgb], [pb])
            yield
            tt("dve", gsb[:, :], pa[:, 0:16], cst[:, C_BIF + d * 16:C_BIF + (d + 1) * 16], ALU.add, [pb, cstb], [gsbb])
            act(e1[:, :], gsb[:, 8:16], AF.Exp, [gsbb], [e1b], scale=-1.0)
            act(nlf[:, :], e1[:, :], AF.Ln, [e1b], [nlfb], bias=1.0)
            tri = C_TF if d == 0 else C_TB
            pc, pcb = bank()
            mm(pc[:, 0:8], [(cst[:, tri:tri + 128], nlf[:, :])], [nlfb, cstb], [pcb])
            mm(pc[:, 8:16], [(cst[:, C_ONE:C_ONE + 128], nlf[:, :])], [nlfb, cstb], [pcb])
            yield
            tt("dve", ipn[:, :], gsb[:, 0:8], pc[:, 0:8], ALU.add, [gsbb, pcb], [ipnb])
            g0 = slot * 24
            act(GT[:, g0:g0 + 8], ipn[:, :], AF.Exp, [ipnb], [GTb[slot]])
            act(GT[:, g0 + 8:g0 + 16], pc[:, 0:8], AF.Exp, [pcb], [GTb[slot]])
            act(GT[:, g0 + 16:g0 + 24], pc[:, 8:16], AF.Exp, [pcb], [GTb[slot]], scale=-1.0)

        def mk_ctx(s, bi, d, h, order, first_blk, last_blk, full):
            return dict(s=s, bi=bi, d=d, h=h, order=order, first_blk=first_blk, last_blk=last_blk, full=full)

        def proj_a(c):
            h, order, full = c["h"], c["order"], c["full"]
            wq, wqb = load_unit(("q", h))
            wk, wkb = load_unit(("k", h))
            qT, qTb = qTs.get()
            kT, kTb = kTs.get()
            ktk, ktkb = kts.get()
            vx, vxb = vxs.get()
            c.update(qT=qT, qTb=qTb, kT=kT, kTb=kTb, ktk=ktk, ktkb=ktkb, vx=vx, vxb=vxb)
            pa, pb = bank()
            for j in range(2):
                mm(pa[:, j * 256:(j + 1) * 256],
                   [(wq[:, kt * 256 + j * 128:kt * 256 + (j + 1) * 128], H.t[:, kt * TB:(kt + 1) * TB]) for kt in range(KT)],
                   [wqb, H.b], [pb])
            act(qT[:, :], pa, AF.Copy, [pb], [qTb])
            yield
            pa, pb = bank()
            for j in range(2):
                mm(pa[:, j * 256:(j + 1) * 256],
                   [(wk[:, kt * 256 + j * 128:kt * 256 + (j + 1) * 128], H.t[:, kt * TB:(kt + 1) * TB]) for kt in range(KT)],
                   [wkb, H.b], [pb])
            act(kT[:, :], pa, AF.Copy, [pb], [kTb], scale=float(HD ** -0.5))
            yield
            wv, wvb = load_unit(("v", h))
            pv, pvb = bank()
            for ci in order:
                mm(pv[:, ci * 256:(ci + 1) * 256],
                   [(H.t[:, kt * TB + ci * 128:kt * TB + (ci + 1) * 128], wv[:, kt * 256:(kt + 1) * 256]) for kt in range(KT)],
                   [wvb, H.b], [pvb])
            pt, ptb = bankb()
            for ci in order:
                for j in range(2):
                    P.op("pe", lambda E, j=j, ci=ci, pt=pt: E.transpose(out=pt[:, ci * 256 + j * 128:ci * 256 + (j + 1) * 128],
                                                                       in_=kT[:, j * TB + ci * 128:j * TB + (ci + 1) * 128],
                                                                       identity=identb[:, :]),
                         [kTb, identbb], [ptb])
            act(ktk[:, :], pt[:, 0:NCH * 256], AF.Copy, [ptb], [ktkb])
            for oi, ci in enumerate(order):
                g0 = (oi + 1) * 24
                ts("dve", vx[:, ci * VS:ci * VS + 256], pv[:, ci * 256:(ci + 1) * 256], GT[:, g0 + h:g0 + h + 1], None, ALU.mult, None,
                   [pvb, GTb[oi + 1]], [vxb])
                P.op("pool", lambda E, ci=ci, g0=g0: E.tensor_copy(out=vx[:, ci * VS + 256:ci * VS + 257], in_=GT[:, g0 + h:g0 + h + 1]),
                     [GTb[oi + 1]], [vxb])
            yield
            if full:
                wo, wob = load_unit(("o", h))
                wz, wzb = load_unit(("z", h))
                goz, gozb = gozs.get()
                c.update(goz=goz, gozb=gozb)
                for ci in order:
                    poz, pozb = bank()
                    mm(poz[:, 0:256], [(H.t[:, kt * TB + ci * 128:kt * TB + (ci + 1) * 128], wo[:, kt * 256:(kt + 1) * 256]) for kt in range(KT)],
                       [wob, H.b], [pozb])
                    mm(poz[:, 256:512], [(H.t[:, kt * TB + ci * 128:kt * TB + (ci + 1) * 128], wz[:, kt * 256:(kt + 1) * 256]) for kt in range(KT)],
                       [wzb, H.b], [pozb])
                    toz, tozb = tF.get()
                    act(toz[:, :], poz, AF.Tanh, [pozb], [tozb], scale=0.5)
                    t1, t1b = tF.get()
                    stt("dve", t1[:, 0:256], toz[:, 256:512], 1.0, poz[:, 256:512], ALU.add, ALU.mult, [tozb, pozb], [t1b])
                    stt("dve", goz[:, ci * 256:(ci + 1) * 256], toz[:, 0:256], 1.0, t1[:, 0:256], ALU.add, ALU.mult, [tozb, t1b], [gozb])
                    yield

        def recur_a(c):
            s, bi, d, h, order, full = c["s"], c["bi"], c["d"], c["h"], c["order"], c["full"]
            qT, qTb, kT, kTb, ktk, ktkb, vx, vxb = c["qT"], c["qTb"], c["kT"], c["kTb"], c["ktk"], c["ktkb"], c["vx"], c["vxb"]
            tok0 = seq_off[s] + bi * TB
            tri = C_TF if d == 0 else C_TB
            u0 = h * 2 * VS
            for oi, ci in enumerate(order):
                slot = oi + 1
                g0 = slot * 24
                gp = (slot - 1) * 24
                first = c["first_blk"] and oi == 0
                last = c["last_blk"] and oi == len(order) - 1
                rows = slice(tok0 + ci * 128, tok0 + (ci + 1) * 128)
                key = (s, bi, ci, h)
                if not first:
                    cb_, cbb = cbf.get()
                    act(cb_[:, :], U[:, u0:u0 + 2 * VS], AF.Copy, [Ub[h], GTb[slot - 1]], [cbb], scale=GT[:, gp + 16 + h:gp + 17 + h])
                hd, hdb = tFr.get()
                if full and VARIANT != "v2":
                    dma("sp", hd[:, 256:512], hb_d[rows, h * 256:(h + 1) * 256], [hbb[key]], [hdb])
                pS, pSb = bank()
                mm(pS[:, 0:128], [(kT[:, j * TB + ci * 128:j * TB + (ci + 1) * 128], qT[:, j * TB + ci * 128:j * TB + (ci + 1) * 128])
                                  for j in range(2)], [kTb, qTb], [pSb])
                Pm, Pmb = tHr.get()
                tt("dve", Pm[:, 0:128], pS[:, 0:128], cst[:, tri:tri + 128], ALU.mult, [pSb, cstb], [Pmb])
                if not last:
                    for j in range(2):
                        pC, pCb = bank()
                        mm(pC[:, 0:257], [(ktk[:, ci * 256 + j * 128:ci * 256 + (j + 1) * 128], vx[:, ci * VS:ci * VS + 257])],
                           [ktkb, vxb], [pCb])
                        uo = U[:, u0 + j * VS:u0 + j * VS + 257]
                        if first:
                            P.op("dve", lambda E, uo=uo, pC=pC: E.tensor_copy(out=uo, in_=pC[:, 0:257]), [pCb], [Ub[h]])
                        else:
                            stt("dve", uo, uo, GT[:, gp + 16 + h:gp + 17 + h], pC[:, 0:257], ALU.mult, ALU.add,
                                [pCb, Ub[h], GTb[slot - 1]], [Ub[h]])
                yield
                pN, pNb = bank()
                pairs = [(Pm[:, 0:128], vx[:, ci * VS:ci * VS + 257])]
                rd = [Pmb, vxb]
                if not first:
                    for j in range(2):
                        pairs.append((qT[:, j * TB + ci * 128:j * TB + (ci + 1) * 128], cb_[:, j * VS:j * VS + 257]))
                    rd += [qTb, cbb]
                mm(pN[:, 0:257], pairs, rd, [pNb])
                r0, r0b = smr.get()
                tt("dve", r0[:, 0:1], pN[:, 256:257], GT[:, g0 + 8 + h:g0 + 9 + h], ALU.max, [pNb, GTb[slot]], [r0b])
                r1, r1b = smr.get()
                stt("dve", r1[:, 0:1], pN[:, 256:257], -1.0, r0[:, 0:1], ALU.mult, ALU.max, [pNb, r0b], [r1b])
                r2, r2b = smr.get()
                P.op("dve", lambda E, r1=r1, r2=r2: E.reciprocal(out=r2[:, 0:1], in_=r1[:, 0:1]), [r1b], [r2b])
                act(hd[:, 0:256], pN[:, 0:256], AF.Copy, [pNb, r2b], [hdb], scale=r2[:, 0:1])
                if not full:
                    hbb[key] = Buf("hb")
                    dma("pool", hb_d[rows, h * 256:(h + 1) * 256], hd[:, 0:256], [hdb], [hbb[key]])
                    yield
                    continue
                goz, gozb = c["goz"], c["gozb"]
                if VARIANT == "v2":
                    dma("sp", hd[:, 256:512], hb_d[rows, h * 256:(h + 1) * 256], [hbb[key]], [hdb])
                hs, hsb = tFr.get()
                tt("dve", hs[:, 0:256], hd[:, 0:256], hd[:, 256:512], ALU.add, [hdb], [hsb])
                st6, st6b = smr.get()
                P.op("dve", lambda E, st6=st6, hs=hs: E.bn_stats(out=st6[:, 0:6], in_=hs[:, 0:256]), [hsb], [st6b])
                mv, mvb = smr.get()
                P.op("dve", lambda E, st6=st6, mv=mv: E.bn_aggr(out=mv[:, 0:2], in_=st6[:, 0:6]), [st6b], [mvb])
                l1, l1b = smr.get()
                act(l1[:, 0:1], mv[:, 1:2], AF.Ln, [mvb], [l1b], bias=float(EPS))
                rs_, rsb = smr.get()
                act(rs_[:, 0:1], l1[:, 0:1], AF.Exp, [l1b], [rsb], scale=-0.5, bias=float(np.log(0.25)))
                stt("dve", hs[:, 256:512], hs[:, 0:256], mv[:, 0:1], goz[:, ci * 256:(ci + 1) * 256], ALU.subtract, ALU.mult,
                    [hsb, mvb, gozb], [hsb])
                ym, ymb = tHr.get()
                stt("dve", ym[:, 0:256], hs[:, 256:512], rs_[:, 0:1], cst[:, C_ZERO:C_ZERO + 256], ALU.mult, ALU.add,
                    [hsb, rsb, cstb], [ymb])
                yield
                pt, ptb = bankb()
                for j in range(2):
                    P.op("pe", lambda E, j=j, pt=pt, ym=ym: E.transpose(out=pt[:, j * 128:(j + 1) * 128],
                                                                       in_=ym[:, j * 128:(j + 1) * 128], identity=identb[:, :]),
                         [ymb, identbb], [ptb])
                for j in range(2):
                    act(ymT[:, (2 * h + j) * TB + ci * 128:(2 * h + j) * TB + (ci + 1) * 128], pt[:, j * 128:(j + 1) * 128], AF.Copy,
                        [ptb, cstb], [ymTb[h]], scale=(1.0 if VARIANT == "v1" else cst[:, C_MHG + 2 * h + j:C_MHG + 2 * h + j + 1]))
                yield

        def phase_b(s, bi):
            for ft in range(16):
                w1, w1b = load_unit(("b1", ft))
                w2, w2b = load_unit(("b2", ft))
                p1, p1b = bank()
                for half in range(2):
                    mm(p1[:, half * 256:(half + 1) * 256],
                       [(w1[:, kt * 256 + half * 128:kt * 256 + (half + 1) * 128], H.t[:, kt * TB:(kt + 1) * TB]) for kt in range(KT)],
                       [w1b, H.b], [p1b])
                p2, p2b = bank()
                for half in range(2):
                    mm(p2[:, half * 256:(half + 1) * 256],
                       [(w2[:, kt * 256 + half * 128:kt * 256 + (half + 1) * 128], H.t[:, kt * TB:(kt + 1) * TB]) for kt in range(KT)],
                       [w2b, H.b], [p2b])
                if CONV_ENG == "actpool":
                    ccs, ccsb = tF.get()
                    act(ccs[:, 0:TB], p1[:, 256:512], AF.Copy, [p1b], [ccsb])
                    u, ub = tF.get()
                    tt("dve", u[:, 1:TB + 1], ccs[:, 0:TB], p2[:, 0:256], ALU.mult, [ccsb, p2b], [ub])
                    if bi == 0:
                        P.op("pool", lambda E, u=u: E.memset(u[:, 0:1], 0.0), [], [ub])
                    else:
                        P.op("pool", lambda E, u=u, ft=ft: E.tensor_copy(out=u[:, 0:1], in_=ulast[:, ft:ft + 1]), [ulastb[ft]], [ub])
                    if bi == nblk[s] - 1:
                        P.op("pool", lambda E, u=u: E.memset(u[:, TB + 1:TB + 2], 0.0), [], [ub])
                    else:
                        hi = halo_idx[(s, bi)]
                        P.op("pool", lambda E, u=u, ft=ft, hi=hi: E.tensor_copy(out=u[:, TB + 1:TB + 2],
                                                                               in_=uhalo[:, ft * NHALO + hi:ft * NHALO + hi + 1]),
                             [uhalob], [ub])
                    P.op("pool", lambda E, u=u, ft=ft: E.tensor_copy(out=ulast[:, ft:ft + 1], in_=u[:, TB:TB + 1]), [ub], [ulastb[ft]])
                    t3, t3b = tF.get()
                    act(ccs[:, 0:TB], u[:, 0:TB], AF.Copy, [ub, cstb], [ccsb], scale=cst[:, C_CW + ft:C_CW + ft + 1])
                    act(ccs[:, 256:512], u[:, 1:TB + 1], AF.Copy, [ub, cstb], [ccsb], scale=cst[:, C_CW + 16 + ft:C_CW + 17 + ft])
                    act(t3[:, 0:TB], u[:, 2:TB + 2], AF.Copy, [ub, cstb], [t3b], scale=cst[:, C_CW + 32 + ft:C_CW + 33 + ft])
                    tt("pool", ccs[:, 0:TB], ccs[:, 0:TB], ccs[:, 256:512], ALU.add, [ccsb], [ccsb])
                    tt("pool", t3[:, 256:512], ccs[:, 0:TB], t3[:, 0:TB], ALU.add, [ccsb, t3b], [t3b])
                    yfin = t3[:, 256:512]
                    ccsb = t3b
                elif CONV_ENG == "aligned":
                    ccs, ccsb = tF.get()
                    act(ccs[:, 0:TB], p1[:, 256:512], AF.Copy, [p1b], [ccsb])
                    act(ccs[:, 256:256 + TB - 1], p1[:, 257:512], AF.Copy, [p1b], [ccsb])
                    u, ub = tF.get()
                    um, umb = tF.get()
                    tt("dve", u[:, 0:TB], ccs[:, 0:TB], p2[:, 0:TB], ALU.mult, [ccsb, p2b], [ub])
                    tt("dve", u[:, 256:256 + TB - 1], ccs[:, 256:256 + TB - 1], p2[:, 1:TB], ALU.mult, [ccsb, p2b], [ub])
                    tt("dve", um[:, 1:TB], ccs[:, 0:TB - 1], p2[:, 0:TB - 1], ALU.mult, [ccsb, p2b], [umb])
                    if bi == 0:
                        P.op("pool", lambda E, um=um: E.memset(um[:, 0:1], 0.0), [], [umb])
                    else:
                        P.op("pool", lambda E, um=um, ft=ft: E.tensor_copy(out=um[:, 0:1], in_=ulast[:, ft:ft + 1]), [ulastb[ft]], [umb])
                    if bi == nblk[s] - 1:
                        P.op("pool", lambda E, u=u: E.memset(u[:, 256 + TB - 1:256 + TB], 0.0), [], [ub])
                    else:
                        hi = halo_idx[(s, bi)]
                        P.op("pool", lambda E, u=u, ft=ft, hi=hi: E.tensor_copy(out=u[:, 256 + TB - 1:256 + TB],
                                                                               in_=uhalo[:, ft * NHALO + hi:ft * NHALO + hi + 1]),
                             [uhalob], [ub])
                    P.op("pool", lambda E, u=u, ft=ft: E.tensor_copy(out=ulast[:, ft:ft + 1], in_=u[:, TB - 1:TB]), [ub], [ulastb[ft]])
                    stt("dve", ccs[:, 0:TB], um[:, 0:TB], cst[:, C_CW + ft:C_CW + ft + 1], cst[:, C_ZERO:C_ZERO + TB], ALU.mult, ALU.add,
                        [umb, ccsb, ub, cstb], [ccsb])
                    stt("dve", ccs[:, 256:512], u[:, 0:TB], cst[:, C_CW + 16 + ft:C_CW + 17 + ft], ccs[:, 0:TB], ALU.mult, ALU.add,
                        [ub, ccsb, cstb], [ccsb])
                    stt("dve", ccs[:, 0:TB], u[:, 256:512], cst[:, C_CW + 32 + ft:C_CW + 33 + ft], ccs[:, 256:512], ALU.mult, ALU.add,
                        [ub, ccsb, cstb], [ccsb])
                    yfin = ccs[:, 0:TB]
                else:
                    ccs, ccsb = tF.get()
                    act(ccs[:, 0:TB], p1[:, 256:512], AF.Copy, [p1b], [ccsb])
                    u, ub = tF.get()
                    tt("dve", u[:, 1:TB + 1], ccs[:, 0:TB], p2[:, 0:256], ALU.mult, [ccsb, p2b], [ub])
                    if bi == 0:
                        P.op("pool", lambda E, u=u: E.memset(u[:, 0:1], 0.0), [], [ub])
                    else:
                        P.op("pool", lambda E, u=u, ft=ft: E.tensor_copy(out=u[:, 0:1], in_=ulast[:, ft:ft + 1]), [ulastb[ft]], [ub])
                    if bi == nblk[s] - 1:
                        P.op("pool", lambda E, u=u: E.memset(u[:, TB + 1:TB + 2], 0.0), [], [ub])
                    else:
                        hi = halo_idx[(s, bi)]
                        P.op("pool", lambda E, u=u, ft=ft, hi=hi: E.tensor_copy(out=u[:, TB + 1:TB + 2],
                                                                               in_=uhalo[:, ft * NHALO + hi:ft * NHALO + hi + 1]),
                             [uhalob], [ub])
                    P.op("pool", lambda E, u=u, ft=ft: E.tensor_copy(out=ulast[:, ft:ft + 1], in_=u[:, TB:TB + 1]), [ub], [ulastb[ft]])
                    stt(CONV_ENG, ccs[:, 256:512], u[:, 0:TB], cst[:, C_CW + ft:C_CW + ft + 1], cst[:, C_ZERO:C_ZERO + TB], ALU.mult, ALU.add,
                        [ub, cstb], [ccsb])
                    stt(CONV_ENG, ccs[:, 0:256], u[:, 1:TB + 1], cst[:, C_CW + 16 + ft:C_CW + 17 + ft], ccs[:, 256:512], ALU.mult, ALU.add,
                        [ub, ccsb, cstb], [ccsb])
                    stt(CONV_ENG, ccs[:, 256:512], u[:, 2:TB + 2], cst[:, C_CW + 32 + ft:C_CW + 33 + ft], ccs[:, 0:256], ALU.mult, ALU.add,
                        [ub, ccsb, cstb], [ccsb])

                    yfin = ccs[:, 256:512]
                tz, tzb = tF.get()
                act(tz[:, 0:TB], p2[:, 256:512], AF.Tanh, [p2b], [tzb], scale=0.5)
                stt("dve", tz[:, 256:512], tz[:, 0:TB], 1.0, p2[:, 256:512], ALU.add, ALU.mult, [tzb, p2b], [tzb])
                tt("dve", tz[:, 0:256], tz[:, 256:512], p1[:, 0:256], ALU.mult, [tzb, p1b], [tzb])
                stt("dve", ycT[:, ft * TB:(ft + 1) * TB], yfin, 0.5, tz[:, 0:256], ALU.mult, ALU.mult, [ccsb, tzb], [ycTb[ft]])
                yield

        def c_proj(s, bi, h, c):
            km_, km_b = kmh.get()
            dma("sp", km_[:, :], km_d[s][:, 4 * h * NMEM:(4 * h + 4) * NMEM], [], [km_b], extra=km_ops[s])
            vm_, vm_b = vmh.get()
            for mt in range(2):
                dma("sp", vm_[:, mt * HDA:(mt + 1) * HDA], vm_d[s][:, mt * D + h * HDA:mt * D + (h + 1) * HDA], [], [vm_b], extra=vm_ops[s])
            qa, qab = qas.get()
            c.update(km_=km_, km_b=km_b, vm_=vm_, vm_b=vm_b, qa=qa, qab=qab)
            for j2 in range(2):
                w, wb_ = load_unit(("qa", h, j2))
                pa, pb = bank()
                for jj in range(2):
                    mm(pa[:, jj * 256:(jj + 1) * 256],
                       [(w[:, kt * 256 + jj * 128:kt * 256 + (jj + 1) * 128], H.t[:, kt * TB:(kt + 1) * TB]) for kt in range(KT)],
                       [wb_, H.b], [pb])
                act(qa[:, j2 * 2 * TB:(j2 + 1) * 2 * TB], pa, AF.Copy, [pb], [qab])
                yield

        def c_attn(s, bi, h, c):
            sc = float(HDA ** -0.5)
            km_, km_b, vm_, vm_b, qa, qab = c["km_"], c["km_b"], c["vm_"], c["vm_b"], c["qa"], c["qab"]
            pT, pTb = pTs.get()
            pns = []
            for ci in range(NCH):
                pS, pSb = bank()
                mm(pS[:, 0:NMEM], [(qa[:, j * TB + ci * 128:j * TB + (ci + 1) * 128], km_[:, j * NMEM:(j + 1) * NMEM]) for j in range(4)],
                   [qab, km_b], [pSb])
                mx, mxb = sm.get()
                P.op("dve", lambda E, mx=mx, pS=pS: E.reduce_max(out=mx[:, 0:1], in_=pS[:, 0:NMEM], axis=mybir.AxisListType.X), [pSb], [mxb])
                nm, nmb = sm.get()
                act(nm[:, 0:1], mx[:, 0:1], AF.Copy, [mxb], [nmb], scale=-sc)
                pe_, peb = tF.get()
                rs_, rsb = sm.get()
                act(pe_[:, 0:NMEM], pS[:, 0:NMEM], AF.Exp, [pSb, nmb], [peb, rsb], scale=sc, bias=nm[:, 0:1], accum=rs_[:, 0:1])
                ri, rib = sm.get()
                P.op("dve", lambda E, ri=ri, rs_=rs_: E.reciprocal(out=ri[:, 0:1], in_=rs_[:, 0:1]), [rsb], [rib])
                pn, pnb = tH.get()
                stt("dve", pn[:, 0:NMEM], pe_[:, 0:NMEM], ri[:, 0:1], cst[:, C_ZERO:C_ZERO + NMEM], ALU.mult, ALU.add, [peb, rib, cstb], [pnb])
                pns.append((pn, pnb))
                yield
            for ci in range(NCH):
                pn, pnb = pns[ci]
                pt, ptb = bankb()
                for mt in range(2):
                    P.op("pe", lambda E, mt=mt, pt=pt, pn=pn: E.transpose(out=pt[:, mt * 128:(mt + 1) * 128],
                                                                         in_=pn[:, mt * 128:(mt + 1) * 128], identity=identb[:, :]),
                         [pnb, identbb], [ptb])
                for mt in range(2):
                    act(pT[:, mt * TB + ci * 128:mt * TB + (ci + 1) * 128], pt[:, mt * 128:(mt + 1) * 128], AF.Copy, [ptb], [pTb])
                yield
            for j2 in range(2):
                w, wb_ = load_unit(("za", h, j2))
                for jj in range(2):
                    j = 2 * j2 + jj
                    poz, pozb = bank()
                    mm(poz[:, 256:512], [(w[:, kt * 256 + jj * 128:kt * 256 + (jj + 1) * 128], H.t[:, kt * TB:(kt + 1) * TB]) for kt in range(KT)],
                       [wb_, H.b], [pozb])
                    mm(poz[:, 0:256], [(vm_[:, mt * HDA + j * 128:mt * HDA + (j + 1) * 128], pT[:, mt * TB:(mt + 1) * TB]) for mt in range(2)],
                       [vm_b, pTb], [pozb])
                    tz, tzb = tF.get()
                    act(tz[:, 0:TB], poz[:, 256:512], AF.Tanh, [pozb], [tzb], scale=0.5)
                    stt("dve", tz[:, 256:512], tz[:, 0:TB], 1.0, poz[:, 256:512], ALU.add, ALU.mult, [tzb, pozb], [tzb])
                    ft = 4 * h + j
                    stt("dve", yaT[:, ft * TB:(ft + 1) * TB], poz[:, 0:256], 0.5, tz[:, 256:512], ALU.mult, ALU.mult, [pozb, tzb], [yaTb[h]])
                    yield

        def phase_c(s, bi):
            cc_ = [dict() for _ in range(NHA)]
            if not CPIPE:
                for h in range(NHA):
                    yield from c_proj(s, bi, h, cc_[h])
                    yield from c_attn(s, bi, h, cc_[h])
                return
            yield from c_proj(s, bi, 0, cc_[0])
            for h in range(NHA):
                fg = c_attn(s, bi, h, cc_[h])
                bg = c_proj(s, bi, h + 1, cc_[h + 1]) if h + 1 < NHA else iter(())
                for _ in fg:
                    step(bg, 1)
                    yield
                for _ in bg:
                    yield

        def phase_d(s, bi):
            tok0 = seq_off[s] + bi * TB
            ysrc = [(ymT, ymTb), (ycT, ycTb), (yaT, yaTb)]
            for c in range(NCH):
                dma("sp", xtok[c][:, :], x_tok[tok0 + c * 128:tok0 + (c + 1) * 128, :], [], [xtokb[c]])
            for pr in range(8):
                accs = [None, None]
                for b in range(3):
                    wm_, wmb = load_unit(("mg", pr, b))
                    wb2, wb2b = load_unit(("wb", pr, b))
                    yt_, ybl = ysrc[b]
                    for jj in range(2):
                        nt = 2 * pr + jj
                        pg, pgb = bank()
                        mm(pg[:, 0:256], [(wm_[:, kt * 256 + jj * 128:kt * 256 + (jj + 1) * 128], H.t[:, kt * TB:(kt + 1) * TB]) for kt in range(KT)],
                           [wmb, H.b], [pgb])
                        mm(pg[:, 256:512], [(wb2[:, kt * 256 + jj * 128:kt * 256 + (jj + 1) * 128], yt_[:, kt * TB:(kt + 1) * TB]) for kt in range(KT)],
                           [wb2b] + ybl, [pgb])
                        tg, tgb = tF.get()
                        act(tg[:, 0:TB], pg[:, 0:256], AF.Tanh, [pgb], [tgb], scale=0.5)
                        stt("dve", tg[:, 256:512], tg[:, 0:TB], 1.0, pg[:, 256:512], ALU.add, ALU.mult, [tgb, pgb], [tgb])
                        if b == 0:
                            accs[jj] = (tg, tgb)
                        elif b == 1:
                            acc, accb = accs[jj]
                            tt("pool", tg[:, 0:256], acc[:, 256:512], tg[:, 256:512], ALU.add, [accb, tgb], [tgb])
                            accs[jj] = (tg, tgb)
                        else:
                            acc, accb = accs[jj]
                            tt("pool", mgT[:, nt * TB:(nt + 1) * TB], acc[:, 0:256], tg[:, 256:512], ALU.add, [accb, tgb], [mgTb[nt]])
            for u in range(8):
                w, wb_ = load_unit(("out", u))
                pa, pb = bank()
                for ci in range(NCH):
                    mm(pa[:, ci * 256:(ci + 1) * 256],
                       [(mgT[:, kt * TB + ci * 128:kt * TB + (ci + 1) * 128], w[:, kt * 256:(kt + 1) * 256]) for kt in range(KT)],
                       [wb_] + mgTb, [pb])
                for ci in range(NCH):
                    xo = xtok[ci][:, u * 256:(u + 1) * 256]
                    stt("dve", xo, pa[:, ci * 256:(ci + 1) * 256], 0.5, xo, ALU.mult, ALU.add, [pb, xtokb[ci]], [xtokb[ci]])
            for ci in range(NCH):
                act(junk[:, :], xtok[ci][:, :], AF.Square, [xtokb[ci]], [junkb, ssqb], accum=ssq[:, 2 + ci:3 + ci])
                act(lnt[:, 2 + ci:3 + ci], ssq[:, 2 + ci:3 + ci], AF.Ln, [ssqb], [lntb], bias=float(D * EPS))
                act(rstd[:, 2 + ci:3 + ci], lnt[:, 2 + ci:3 + ci], AF.Exp, [lntb], [rstdb], scale=-0.5, bias=float(0.5 * np.log(D)))
                if FIN == "split":
                    for hf in range(2):
                        xs = xtok[ci][:, hf * 1024:(hf + 1) * 1024]
                        stt("pool" if hf else "dve", xs, xs, rstd[:, 2 + ci:3 + ci], cst[:, C_FG + hf * 1024:C_FG + (hf + 1) * 1024],
                            ALU.mult, ALU.mult, [xtokb[ci], rstdb, cstb], [xtokb[ci]])
                else:
                    for hf in range(4):
                        xs = xtok[ci][:, hf * 512:(hf + 1) * 512]
                        stt("dve", xs, xs, rstd[:, 2 + ci:3 + ci], cst[:, C_FG + hf * 512:C_FG + (hf + 1) * 512],
                            ALU.mult, ALU.mult, [xtokb[ci], rstdb, cstb], [xtokb[ci]])
                o = dma("pool", y_out[tok0 + ci * 128:tok0 + (ci + 1) * 128, :], xtok[ci][:, :], [xtokb[ci]], [])
                out_ops.append(o)

        def drain(g):
            for _ in g:
                pass

        def step(g, n=1):
            for _ in range(n):
                try:
                    next(g)
                except StopIteration:
                    return False
            return True

        def interleave(fg, bg, k):
            for _ in fg:
                step(bg, k)
            drain(bg)

        def chain(*gs):
            for g in gs:
                yield from g

        def run_pass(d, full):
            items = []
            for s in range(NSEQ):
                blocks = list(range(nblk[s]))
                if d == 1:
                    blocks = blocks[::-1]
                for oi_b, bi in enumerate(blocks):
                    items.append((s, bi, oi_b == 0, oi_b == len(blocks) - 1))
            order = list(range(NCH)) if d == 0 else list(range(NCH))[::-1]

            def pre(i):
                s_, bi_ = items[i][0], items[i][1]
                t_, b_ = hTs[i % 2]
                return norm_pre(seq_off[s_] + bi_ * TB, x_T[blk_base[s_] + bi_], t_, b_)

            drain(pre(0))
            for idx, (s, bi, first_blk, last_blk) in enumerate(items):
                H.t, H.b = hTs[idx % 2]
                ctx = [mk_ctx(s, bi, d, h, order, first_blk, last_blk, full) for h in range(NH)]
                p0 = proj_a(ctx[0])
                gg = chain(*[gates(ci, d, oi + 1) for oi, ci in enumerate(order)])
                step(gg, 2)
                step(p0, 1)
                step(gg, 2)
                step(p0, 1)
                drain(gg)
                drain(p0)
                if full:
                    bc = chain(phase_b(s, bi), phase_c(s, bi))
                npg = pre(idx + 1) if idx + 1 < len(items) else iter(())
                for h in range(NH):
                    bg = proj_a(ctx[h + 1]) if h + 1 < NH else iter(())
                    fg = recur_a(ctx[h])
                    if not INTERLEAVE_BC:
                        for _ in fg:
                            step(bg, 1)
                            step(npg, 1)
                        drain(bg)
                    elif full and INTERLEAVE_BC:
                        for _ in fg:
                            if not step(bg, 1):
                                step(bc, 2)
                            else:
                                step(bc, 1)
                        drain(bg)
                    else:
                        interleave(fg, bg, 1)
                drain(npg)
                P.op("pool", lambda E: E.tensor_copy(out=GT[:, 16:24], in_=GT[:, NCH * 24 + 16:NCH * 24 + 24]),
                     [GTb[NCH]], [GTb[0]])
                if full:
                    chk("p2a")
                    drain(bc)
                    chk("p2bc")
                    phase_d(s, bi)
                    chk("p2d")

        try:
            chk("cast")
            for s in range(NSEQ):
                kv_prologue(s)
            chk("kv")
            halo_prepass()
            chk("halo")
            run_pass(1, False)
            chk("pass1")
            run_pass(0, True)
        except _Stop:
            pass
        if stop is not None:
            out_ops = [o for e in P.ENG for o in P.ops[e] if o.dma]
        P.op("sp", [], [], [], extra=out_ops)
        P.emit(nc, stack)
    return nc


def _tile_T(a):
    T = a.shape[0]
    return np.ascontiguousarray(a.T.reshape(KT, 128, T).transpose(1, 0, 2)).reshape(128, KT * T)


def prepare_shared(norm_g, w_in, b_if, conv_w, mem_norm_g, w_kv_mem, mh_norm_g, w_branch, w_out, final_norm_g):
    units, uidx = unit_catalog()
    srcs = {"in": w_in[0], "kv": w_kv_mem[0], "br0": w_branch[0, 0], "br1": w_branch[0, 1], "br2": w_branch[0, 2], "out": w_out[0]}
    ws32 = np.empty((len(units), 128, KT * 256), np.float32)
    for i, (src, cols) in enumerate(units):
        blk = srcs[src][:, cols]
        ws32[i] = blk.reshape(KT, 128, 256).transpose(1, 0, 2).reshape(128, KT * 256)
    perm = np.concatenate([np.arange(0, 8), np.arange(16, 24), np.arange(8, 16), np.arange(24, 32)])
    wgc = w_in[0][:, 10240 + perm]
    wg32 = np.ascontiguousarray(wgc.reshape(KT, 128, 32).transpose(1, 0, 2)).reshape(128, KT * 32)
    cst = np.zeros((128, C_END), np.float32)
    cst[:, C_ID:C_ID + 128] = np.eye(128, dtype=np.float32)
    cst[:, C_ONE:C_ONE + 128] = 1.0
    ii = np.arange(128)
    cst[:, C_TF:C_TF + 128] = (ii[:, None] <= ii[None, :]).astype(np.float32)
    cst[:, C_TB:C_TB + 128] = (ii[:, None] >= ii[None, :]).astype(np.float32)
    cst[:, C_G:C_G + 16] = norm_g[0].reshape(KT, 128).T
    cst[:, C_GM:C_GM + 16] = mem_norm_g[0].reshape(KT, 128).T
    for j in range(3):
        cst[:, C_CW + 16 * j:C_CW + 16 * (j + 1)] = conv_w[0, j].reshape(KT, 128).T
    cst[:, C_BIF:C_BIF + 32] = b_if[0][perm][None, :]
    cst[:, C_MHG:C_MHG + 16] = mh_norm_g[0].reshape(KT, 128).T
    cst[:, C_FG:C_FG + D] = final_norm_g[None, :]
    return ws32, wg32, cst


def prepare_core(seqs, mems):
    x_tok = np.ascontiguousarray(np.concatenate(seqs, axis=0))
    blocks = []
    halos = []
    for x in seqs:
        nb = x.shape[0] // TB
        for bi in range(nb):
            blocks.append(_tile_T(x[bi * TB:(bi + 1) * TB]))
            if bi < nb - 1:
                halos.append(x[(bi + 1) * TB])
    x_T = np.stack(blocks, axis=0)
    h_tok = np.zeros((32, D), np.float32)
    if halos:
        h_tok[:len(halos)] = np.stack(halos, axis=0)
    h_tok[len(halos):] = 1.0
    h_T = _tile_T(h_tok)
    m_tok = np.ascontiguousarray(np.concatenate(mems, axis=0))
    m_T = np.stack([_tile_T(m) for m in mems], axis=0)
    return {"x_tok": x_tok, "x_T": x_T, "m_tok": m_tok, "m_T": m_T, "h_tok": h_tok, "h_T": h_T}


def kernel(x_prompt, x_sample, mem_prompt, mem_sample, norm_g, w_in, b_if, conv_w, mem_norm_g,
           w_kv_mem, mh_norm_g, w_branch, w_out, final_norm_g):
    f = lambda a: np.asarray(a, dtype=np.float32)
    x_prompt, x_sample, mem_prompt, mem_sample = f(x_prompt), f(x_sample), f(mem_prompt), f(mem_sample)
    ws32, wg32, cst = prepare_shared(f(norm_g), f(w_in), f(b_if), f(conv_w), f(mem_norm_g), f(w_kv_mem), f(mh_norm_g),
                                     f(w_branch), f(w_out), f(final_norm_g))
    n = 8
    SP, SS = x_prompt.shape[1], x_sample.shape[1]
    nc = build_program([SP, SS, SS])
    in_maps = []
    for i in range(n):
        m = prepare_core([x_prompt[i], x_sample[2 * i], x_sample[2 * i + 1]],
                         [mem_prompt[i], mem_sample[2 * i], mem_sample[2 * i + 1]])
        m.update({"ws32": ws32, "wg32": wg32, "cst_in": cst})
        in_maps.append(m)
    res = run_bass_kernel_spmd(nc, in_maps, core_ids=list(range(n)))
    y_prompt = np.empty_like(x_prompt)
    y_sample = np.empty_like(x_sample)
    for i in range(n):
        y = res.results[i]["y"]
        y_prompt[i] = y[0:SP]
        y_sample[2 * i] = y[SP:SP + SS]
        y_sample[2 * i + 1] = y[SP + SS:SP + 2 * SS]
    return (y_prompt, y_sample)
```

```python
import contextlib
import numpy as np
import concourse.bass as bass
import concourse.mybir as mybir
from concourse.bass_utils import run_bass_kernel_spmd

F32 = mybir.dt.float32
BF16 = mybir.dt.bfloat16
AF = mybir.ActivationFunctionType
ALU = mybir.AluOpType

D = 2048
KT = 16
TB = 256
NCH = TB // 128
NH = 8
HD = 256
NHA = 4
HDA = 512
NMEM = 256
EPS = 1e-6
VS = 264
NSLOT = 6
import os
VARIANT = os.environ.get("KVARIANT", "")
INTERLEAVE_BC = os.environ.get("KBC", "0") == "1"
CPIPE = os.environ.get("KCPIPE", "1") == "1"
DIAG_ENG = os.environ.get("KDIAG", "dve")
FIN = os.environ.get("KFIN", "dve4")
CONV_ENG = os.environ.get("KCONV", "actpool")
SQD = float(np.sqrt(D))

C_ID, C_ONE, C_TF, C_TB, C_G, C_GM, C_CW, C_BIF, C_MHG, C_FG = 0, 128, 256, 384, 512, 528, 544, 592, 624, 640
C_ZERO = 2688
C_END = 2944


def unit_catalog():
    units = []
    idx = {}

    def add(key, src, cols):
        idx[key] = len(units)
        units.append((src, np.asarray(cols, dtype=np.int64)))

    r = np.arange
    OQ, OK_, OV, OO, OZ = 0, 2048, 4096, 6144, 8192
    OCB, OCC, OCX, OZC = 10272, 12320, 14368, 16416
    OQA, OZA, OMG = 18464, 20512, 22560
    for h in range(NH):
        add(("q", h), "in", OQ + h * 256 + r(256))
        add(("k", h), "in", OK_ + h * 256 + r(256))
        add(("v", h), "in", OV + h * 256 + r(256))
    for u in range(8):
        add(("kk", u), "kv", u * 256 + r(256))
    for u in range(8):
        add(("kv", u), "kv", 2048 + u * 256 + r(256))
    for ft in range(16):
        add(("b1", ft), "in", np.concatenate([OCB + ft * 128 + r(128), OCC + ft * 128 + r(128)]))
        add(("b2", ft), "in", np.concatenate([OCX + ft * 128 + r(128), OZC + ft * 128 + r(128)]))
    for h in range(NH):
        add(("o", h), "in", OO + h * 256 + r(256))
        add(("z", h), "in", OZ + h * 256 + r(256))
    for h in range(NHA):
        for j in range(2):
            add(("qa", h, j), "in", OQA + h * 512 + j * 256 + r(256))
        for j in range(2):
            add(("za", h, j), "in", OZA + h * 512 + j * 256 + r(256))
    for pr in range(8):
        for b in range(3):
            add(("mg", pr, b), "in", OMG + b * 2048 + pr * 256 + r(256))
        for b in range(3):
            add(("wb", pr, b), "br%d" % b, pr * 256 + r(256))
    for u in range(8):
        add(("out", u), "out", u * 256 + r(256))
    return units, idx


class Buf:
    __slots__ = ("name", "last_w", "readers", "const")

    def __init__(self, name, const=False):
        self.name = name
        self.last_w = None
        self.readers = []
        self.const = const


class Op:
    __slots__ = ("eng", "fns", "deps", "signal", "ev", "dma", "pre")

    def __init__(self, eng, fns, dma):
        self.eng = eng
        self.fns = fns
        self.dma = dma
        self.deps = []
        self.signal = False
        self.ev = None
        self.pre = None


class Prog:
    ENG = ["pe", "act", "dve", "pool", "sp"]

    def __init__(self):
        self.ops = {e: [] for e in self.ENG}
        self.nops = 0

    def op(self, eng, fns, reads=(), writes=(), dma=False, extra=()):
        if not isinstance(fns, (list, tuple)):
            fns = [fns]
        o = Op(eng, list(fns), dma)
        deps = {}
        for b in reads:
            if b.last_w is not None:
                deps[id(b.last_w)] = b.last_w
        for b in writes:
            if b.last_w is not None:
                deps[id(b.last_w)] = b.last_w
            for rd in b.readers:
                deps[id(rd)] = rd
        for x in extra:
            deps[id(x)] = x
        deps.pop(id(o), None)
        for d in deps.values():
            if eng == "pe" and d.eng == "pe" and not d.dma:
                continue
            d.signal = True
            o.deps.append(d)
        for b in reads:
            if not b.const:
                b.readers.append(o)
        for b in writes:
            b.last_w = o
            b.readers = []
        self.ops[eng].append(o)
        self.nops += 1
        return o

    def emit(self, nc, stack):
        esem = {e: stack.enter_context(nc.semaphore("es_" + e)) for e in ["pe", "act", "dve", "pool"]}
        npool = {"sp": int(os.environ.get("KSPQ", "4")), "pool": 4, "act": 4}
        dsem = {e: [stack.enter_context(nc.semaphore("ds_%s%d" % (e, i))) for i in range(n)] for e, n in npool.items()}
        for e, lst in self.ops.items():
            cnt = 0
            rr = 0
            use = [0] * npool.get(e, 0)
            for o in lst:
                if o.dma:
                    k = rr % len(use)
                    rr += 1
                    o.pre = (dsem[e][k], use[k] * 16)
                    use[k] += 1
                    o.ev = (dsem[e][k], use[k] * 16)
                elif o.signal:
                    cnt += 1
                    o.ev = (esem[e], cnt)
        block = stack.enter_context(nc.Block())
        ops = self.ops

        def make(e_name):
            def body(E):
                waited = {}
                for o in ops[e_name]:
                    need = [d.ev for d in o.deps]
                    if o.dma and o.pre[1] > 0:
                        need.append(o.pre)
                    for (s, v) in need:
                        k = id(s)
                        if waited.get(k, 0) < v:
                            E.wait_ge(s, v)
                            waited[k] = v
                    ins = None
                    for fn in o.fns:
                        ins = fn(E)
                    if o.dma:
                        ins.then_inc(o.ev[0], 16)
                    elif o.signal:
                        ins.then_inc(o.ev[0], 1)
            return body

        block.tensor(make("pe"))
        block.scalar(make("act"))
        block.vector(make("dve"))
        block.gpsimd(make("pool"))
        block.sync(make("sp"))


class Rot:
    def __init__(self, items):
        self.items = items
        self.i = 0

    def get(self):
        it = self.items[self.i % len(self.items)]
        self.i += 1
        return it


class _Stop(Exception):
    pass


def build_program(seq_lens, stop=None):
    units, uidx = unit_catalog()
    NU = len(units)
    NSEQ = len(seq_lens)
    NTOK = int(sum(seq_lens))
    seq_off = [int(sum(seq_lens[:i])) for i in range(NSEQ)]
    nblk = [L // TB for L in seq_lens]
    blk_base = [int(sum(nblk[:i])) for i in range(NSEQ)]
    NBLK = int(sum(nblk))
    halo_idx = {}
    for s in range(NSEQ):
        for bi in range(nblk[s] - 1):
            halo_idx[(s, bi)] = len(halo_idx)
    NHALO = 32
    assert len(halo_idx) <= NHALO

    nc = bass.Bass("TRN2", target_bir_lowering=False)
    dt_ = nc.dram_tensor
    x_tok = dt_("x_tok", [NTOK, D], F32, kind="ExternalInput").ap()
    x_T = dt_("x_T", [NBLK, 128, KT * TB], F32, kind="ExternalInput").ap()
    m_tok = dt_("m_tok", [NSEQ * NMEM, D], F32, kind="ExternalInput").ap()
    m_T = dt_("m_T", [NSEQ, 128, KT * NMEM], F32, kind="ExternalInput").ap()
    h_tok = dt_("h_tok", [NHALO, D], F32, kind="ExternalInput").ap()
    h_T = dt_("h_T", [128, KT * NHALO], F32, kind="ExternalInput").ap()
    ws32 = dt_("ws32", [NU, 128, KT * 256], F32, kind="ExternalInput").ap()
    wg32 = dt_("wg32", [128, KT * 32], F32, kind="ExternalInput").ap()
    cst_d = dt_("cst_in", [128, C_END], F32, kind="ExternalInput").ap()
    y_out = dt_("y", [NTOK, D], F32, kind="ExternalOutput").ap()
    ws16 = dt_("ws16", [NU, 128, KT * 256], BF16, kind="Internal").ap()
    hb_d = dt_("hb", [NTOK, D], F32, kind="Internal").ap()
    km_d = dt_("km", [NSEQ, 128, KT * NMEM], BF16, kind="Internal").ap()
    vm_d = dt_("vm", [NSEQ, 128, 2 * D], BF16, kind="Internal").ap()

    P = Prog()
    stack = contextlib.ExitStack()
    with stack:
        def sb(name, shape, dtype):
            return stack.enter_context(nc.sbuf_tensor(name, shape, dtype))

        cst = sb("cst", [128, C_END], F32)
        cstb = Buf("cst", const=True)
        identb = sb("identb", [128, 128], BF16)
        identbb = Buf("identb", const=True)
        wg = sb("wg", [128, KT * 32], BF16)
        wgb = Buf("wg", const=True)
        xtok = [sb("xtok%d" % c, [128, D], F32) for c in range(NCH)]
        xtokb = [Buf("xtok%d" % c) for c in range(NCH)]
        xts = Rot([(sb("xts%d" % i, [128, 4 * TB], F32), Buf("xts%d" % i)) for i in range(2)])
        hT = sb("hT", [128, KT * TB], BF16)
        hTb = Buf("hT")
        junk = sb("junk", [128, D], BF16)
        junkb = Buf("junk")
        ssq = sb("ssq", [128, 4], F32)
        ssqb = Buf("ssq")
        rstd = sb("rstd", [128, 4], F32)
        rstdb = Buf("rstd")
        lnt = sb("lnt", [128, 4], F32)
        lntb = Buf("lnt")
        diag = sb("diag", [128, 256], F32)
        diagb = Buf("diag")
        rbc = sb("rbc", [128, TB], F32)
        rbcb = Buf("rbc")
        ymT = sb("ymT", [128, KT * TB], BF16)
        ymTb = [Buf("ymT%d" % h) for h in range(NH)]
        ycT = sb("ycT", [128, KT * TB], BF16)
        ycTb = [Buf("ycT%d" % i) for i in range(KT)]
        yaT = sb("yaT", [128, KT * TB], BF16)
        yaTb = [Buf("yaT%d" % h) for h in range(NHA)]
        mgT = sb("mgT", [128, KT * TB], BF16)
        mgTb = [Buf("mgT%d" % i) for i in range(KT)]
        U = sb("U", [128, NH * 2 * VS], F32)
        Ub = [Buf("U%d" % h) for h in range(NH)]
        cbf = Rot([(sb("cbf%d" % i, [128, 2 * VS], BF16), Buf("cbf%d" % i)) for i in range(2)])
        GT = sb("GT", [128, (NCH + 1) * 24], F32)
        GTb = [Buf("GT%d" % i) for i in range(NCH + 1)]
        gsb = sb("gsb", [128, 16], F32)
        gsbb = Buf("gsb")
        e1 = sb("e1", [128, 8], F32)
        e1b = Buf("e1")
        nlf = sb("nlf", [128, 8], F32)
        nlfb = Buf("nlf")
        ipn = sb("ipn", [128, 8], F32)
        ipnb = Buf("ipn")
        uhalo = sb("uhalo", [128, KT * NHALO], F32)
        uhalob = Buf("uhalo")
        ulast = sb("ulast", [128, KT], F32)
        ulastb = [Buf("ulast%d" % i) for i in range(KT)]
        sm = Rot([(sb("sm%d" % i, [128, 8], F32), Buf("sm%d" % i)) for i in range(12)])
        qTs = Rot([(sb("qT%d" % i, [128, 2 * TB], BF16), Buf("qT%d" % i)) for i in range(2)])
        kTs = Rot([(sb("kT%d" % i, [128, 2 * TB], BF16), Buf("kT%d" % i)) for i in range(2)])
        kts = Rot([(sb("ktok%d" % i, [128, NCH * 256], BF16), Buf("ktok%d" % i)) for i in range(2)])
        vxs = Rot([(sb("vext%d" % i, [128, NCH * VS], BF16), Buf("vext%d" % i)) for i in range(2)])
        gozs = Rot([(sb("goz%d" % i, [128, NCH * 256], F32), Buf("goz%d" % i)) for i in range(2)])
        qas = Rot([(sb("qaT%d" % i, [128, 4 * TB], BF16), Buf("qaT%d" % i)) for i in range(2)])
        pTs = Rot([(sb("pT%d" % i, [128, 2 * TB], BF16), Buf("pT%d" % i)) for i in range(2)])
        kmh = Rot([(sb("kmh%d" % i, [128, 4 * NMEM], BF16), Buf("kmh%d" % i)) for i in range(2)])
        vmh = Rot([(sb("vmh%d" % i, [128, 2 * HDA], BF16), Buf("vmh%d" % i)) for i in range(2)])
        tF = Rot([(sb("tF%d" % i, [128, 512], F32), Buf("tF%d" % i)) for i in range(7)])
        tH = Rot([(sb("tH%d" % i, [128, 512], BF16), Buf("tH%d" % i)) for i in range(3)])
        tFr = Rot([(sb("tFr%d" % i, [128, 512], F32), Buf("tFr%d" % i)) for i in range(4)])
        tHr = Rot([(sb("tHr%d" % i, [128, 256], BF16), Buf("tHr%d" % i)) for i in range(4)])
        smr = Rot([(sb("smr%d" % i, [128, 8], F32), Buf("smr%d" % i)) for i in range(16)])
        wr = [sb("wr%d" % i, [128, KT * 256], BF16) for i in range(NSLOT)]
        wrb = [Buf("wr%d" % i) for i in range(NSLOT)]
        wring = Rot(list(zip(wr, wrb)))
        NPF = 6
        psf = stack.enter_context(nc.psum_tensor("psf", [128, NPF * 512], F32))
        psfb = [Buf("psf%d" % i) for i in range(NPF)]
        psb = stack.enter_context(nc.psum_tensor("psb", [128, 2 * 1024], BF16))
        psbb = [Buf("psb%d" % i) for i in range(2)]
        ps_i = [0]
        psb_i = [0]

        def bank():
            i = ps_i[0] % NPF
            ps_i[0] += 1
            return psf[:, i * 512:(i + 1) * 512], psfb[i]

        def bankb():
            i = psb_i[0] % 2
            psb_i[0] += 1
            return psb[:, i * 1024:(i + 1) * 1024], psbb[i]

        wsb = [Buf("ws%d" % u, const=True) for u in range(NU)]
        hbb = {}
        km_ops = [[] for _ in range(NSEQ)]
        vm_ops = [[] for _ in range(NSEQ)]
        out_ops = []

        def cc(a, b):
            return cst[:, a:b]

        def dma(eng, out, in_, reads, writes, extra=()):
            return P.op(eng, lambda E: E.dma_start(out=out, in_=in_), reads, writes, dma=True, extra=extra)

        def mm(out, pairs, reads, writes):
            n = len(pairs)
            fns = [(lambda E, l=l, r=r, i=i: E.matmul(out, lhsT=l, rhs=r, start=(i == 0), stop=(i == n - 1)))
                   for i, (l, r) in enumerate(pairs)]
            return P.op("pe", fns, reads, writes)

        def act(out, in_, func, reads, writes, scale=1.0, bias=0.0, accum=None):
            if accum is None:
                return P.op("act", lambda E: E.activation(out=out, in_=in_, func=func, bias=bias, scale=scale), reads, writes)
            return P.op("act", lambda E: E.activation(out=out, in_=in_, func=func, bias=bias, scale=scale, accum_out=accum), reads, writes)

        def tt(eng, out, a, b, op, reads, writes):
            return P.op(eng, lambda E: E.tensor_tensor(out=out, in0=a, in1=b, op=op), reads, writes)

        def ts(eng, out, a, s1, s2, op0, op1, reads, writes):
            if s2 is None:
                return P.op(eng, lambda E: E.tensor_scalar(out=out, in0=a, scalar1=s1, scalar2=None, op0=op0), reads, writes)
            return P.op(eng, lambda E: E.tensor_scalar(out=out, in0=a, scalar1=s1, scalar2=s2, op0=op0, op1=op1), reads, writes)

        def stt(eng, out, a, s, b, op0, op1, reads, writes):
            if eng == "pool":
                P.op(eng, lambda E: E.tensor_scalar(out=out, in0=a, scalar1=s, scalar2=None, op0=op0), reads, writes)
                return P.op(eng, lambda E: E.tensor_tensor(out=out, in0=out, in1=b, op=op1), list(reads) + list(writes), writes)
            return P.op(eng, lambda E: E.scalar_tensor_tensor(out=out, in0=a, scalar=s, in1=b, op0=op0, op1=op1), reads, writes)

        def load_unit(key):
            u = uidx[key]
            w, b = wring.get()
            dma("sp", w[:, :], ws16[u], [wsb[u]], [b])
            return w, b

        def chk(tag):
            if stop == tag:
                raise _Stop()

        dma("sp", cst[:, :], cst_d[:, :], [], [cstb])
        act(identb[:, :], cst[:, C_ID:C_ID + 128], AF.Copy, [cstb], [identbb])
        st0, st0b = xts.get()
        dma("sp", st0[:, 0:KT * 32], wg32[:, :], [], [st0b])
        act(wg[:, :], st0[:, 0:KT * 32], AF.Copy, [st0b], [wgb])
        P.op("pool", lambda E: E.memset(U[:, :], 0.0), [], Ub)
        NCAST0 = uidx[("o", 0)]
        cast_next = [NCAST0]
        for u in range(NCAST0):
            dma("pool", ws16[u], ws32[u], [], [wsb[u]])

        def cast_more(n):
            for _ in range(n):
                if cast_next[0] < NU:
                    u = cast_next[0]
                    dma("pool", ws16[u], ws32[u], [], [wsb[u]])
                    cast_next[0] += 1

        def norm_block(tok_aps, xT_ap, nT, gcol):
            pa, pb = bank()
            for c, (tap, rows) in enumerate(tok_aps):
                dma("sp", xtok[c][0:rows, :], tap, [], [xtokb[c]])
                act(junk[0:rows, :], xtok[c][0:rows, :], AF.Square, [xtokb[c]], [junkb, ssqb], accum=ssq[0:rows, c:c + 1])
                act(lnt[0:rows, c:c + 1], ssq[0:rows, c:c + 1], AF.Ln, [ssqb], [lntb], bias=float(D * EPS))
                act(rstd[0:rows, c:c + 1], lnt[0:rows, c:c + 1], AF.Exp, [lntb], [rstdb], scale=-0.5)
                stt("dve", diag[0:rows, c * 128:c * 128 + rows], cst[0:rows, C_ID:C_ID + rows], rstd[0:rows, c:c + 1],
                    cst[0:rows, C_ZERO:C_ZERO + rows], ALU.mult, ALU.add, [rstdb, cstb], [diagb])
            for c, (tap, rows) in enumerate(tok_aps):
                mm(pa[:, c * 128:c * 128 + rows], [(cst[0:rows, C_ONE:C_ONE + 128], diag[0:rows, c * 128:c * 128 + rows])], [diagb, cstb], [pb])
            act(rbc[:, 0:nT], pa[:, 0:nT], AF.Copy, [pb], [rbcb], scale=SQD)
            chk("norm1")
            for q in range(4):
                st, stb = xts.get()
                if nT == TB:
                    dma("sp", st[:, :], xT_ap[:, q * 4 * nT:(q + 1) * 4 * nT], [], [stb])
                else:
                    dma("sp", st[:, 0:4 * nT], xT_ap[:, q * 4 * nT:(q + 1) * 4 * nT], [], [stb])
                for j in range(4):
                    kt = q * 4 + j
                    stt("dve", hT[:, kt * TB:kt * TB + nT], st[:, j * nT:(j + 1) * nT], cst[:, gcol + kt:gcol + kt + 1],
                        rbc[:, 0:nT], ALU.mult, ALU.mult, [stb, rbcb, cstb], [hTb])

        def kv_prologue(s):
          if True:
            norm_block([(m_tok[s * NMEM + c * 128:s * NMEM + (c + 1) * 128, :], 128) for c in range(2)], m_T[s], NMEM, C_GM)
            chk("norm")
            for u in range(8):
                w, wb_ = load_unit(("kk", u))
                pa, pb = bank()
                for j in range(2):
                    mm(pa[:, j * 256:(j + 1) * 256],
                       [(w[:, kt * 256 + j * 128:kt * 256 + (j + 1) * 128], hT[:, kt * TB:(kt + 1) * TB]) for kt in range(KT)],
                       [wb_, hTb], [pb])
                t, tb_ = tH.get()
                act(t[:, :], pa, AF.Copy, [pb], [tb_])
                km_ops[s].append(dma("pool", km_d[s][:, 2 * u * NMEM:(2 * u + 2) * NMEM], t[:, :], [tb_], []))
            chk("km")
            for u in range(8):
                w, wb_ = load_unit(("kv", u))
                pa, pb = bank()
                for mt in range(2):
                    mm(pa[:, mt * 256:(mt + 1) * 256],
                       [(hT[:, kt * TB + mt * 128:kt * TB + (mt + 1) * 128], w[:, kt * 256:(kt + 1) * 256]) for kt in range(KT)],
                       [wb_, hTb], [pb])
                t, tb_ = tH.get()
                act(t[:, :], pa, AF.Copy, [pb], [tb_])
                for mt in range(2):
                    vm_ops[s].append(dma("pool", vm_d[s][:, mt * D + u * 256:mt * D + (u + 1) * 256], t[:, mt * 256:(mt + 1) * 256], [tb_], []))

        def halo_prepass():
          norm_block([(h_tok[:, :], NHALO)], h_T, NHALO, C_G)
          for ft in range(16):
            w1, w1b = load_unit(("b1", ft))
            w2, w2b = load_unit(("b2", ft))
            pa, pb = bank()
            mm(pa[:, 0:NHALO], [(w1[:, kt * 256 + 128:kt * 256 + 256], hT[:, kt * TB:kt * TB + NHALO]) for kt in range(KT)],
               [w1b, hTb], [pb])
            mm(pa[:, 256:256 + NHALO], [(w2[:, kt * 256:kt * 256 + 128], hT[:, kt * TB:kt * TB + NHALO]) for kt in range(KT)],
               [w2b, hTb], [pb])
            t, tb_ = tF.get()
            act(t[:, 0:NHALO], pa[:, 0:NHALO], AF.Copy, [pb], [tb_])
            tt("dve", uhalo[:, ft * NHALO:(ft + 1) * NHALO], t[:, 0:NHALO], pa[:, 256:256 + NHALO], ALU.mult, [tb_, pb], [uhalob])

        def gates(ci, d, slot):
            pa, pb = bank()
            mm(pa[:, 0:16], [(hT[:, kt * TB + ci * 128:kt * TB + (ci + 1) * 128], wg[:, kt * 32 + d * 16:kt * 32 + (d + 1) * 16])
                             for kt in range(KT)], [hTb, wgb], [pb])
            yield
            tt("dve", gsb[:, :], pa[:, 0:16], cst[:, C_BIF + d * 16:C_BIF + (d + 1) * 16], ALU.add, [pb, cstb], [gsbb])
            act(e1[:, :], gsb[:, 8:16], AF.Exp, [gsbb], [e1b], scale=-1.0)
            act(nlf[:, :], e1[:, :], AF.Ln, [e1b], [nlfb], bias=1.0)
            tri = C_TF if d == 0 else C_TB
            pc, pcb = bank()
            mm(pc[:, 0:8], [(cst[:, tri:tri + 128], nlf[:, :])], [nlfb, cstb], [pcb])
            mm(pc[:, 8:16], [(cst[:, C_ONE:C_ONE + 128], nlf[:, :])], [nlfb, cstb], [pcb])
            yield
            tt("dve", ipn[:, :], gsb[:, 0:8], pc[:, 0:8], ALU.add, [gsbb, pcb], [ipnb])
            g0 = slot * 24
            act(GT[:, g0:g0 + 8], ipn[:, :], AF.Exp, [ipnb], [GTb[slot]])
            act(GT[:, g0 + 8:g0 + 16], pc[:, 0:8], AF.Exp, [pcb], [GTb[slot]])
            act(GT[:, g0 + 16:g0 + 24], pc[:, 8:16], AF.Exp, [pcb], [GTb[slot]], scale=-1.0)

        def mk_ctx(s, bi, d, h, order, first_blk, last_blk, full):
            return dict(s=s, bi=bi, d=d, h=h, order=order, first_blk=first_blk, last_blk=last_blk, full=full)

        def proj_a(c):
            h, order, full = c["h"], c["order"], c["full"]
            wq, wqb = load_unit(("q", h))
            wk, wkb = load_unit(("k", h))
            qT, qTb = qTs.get()
            kT, kTb = kTs.get()
            ktk, ktkb = kts.get()
            vx, vxb = vxs.get()
            c.update(qT=qT, qTb=qTb, kT=kT, kTb=kTb, ktk=ktk, ktkb=ktkb, vx=vx, vxb=vxb)
            pa, pb = bank()
            for j in range(2):
                mm(pa[:, j * 256:(j + 1) * 256],
                   [(wq[:, kt * 256 + j * 128:kt * 256 + (j + 1) * 128], hT[:, kt * TB:(kt + 1) * TB]) for kt in range(KT)],
                   [wqb, hTb], [pb])
            act(qT[:, :], pa, AF.Copy, [pb], [qTb])
            yield
            pa, pb = bank()
            for j in range(2):
                mm(pa[:, j * 256:(j + 1) * 256],
                   [(wk[:, kt * 256 + j * 128:kt * 256 + (j + 1) * 128], hT[:, kt * TB:(kt + 1) * TB]) for kt in range(KT)],
                   [wkb, hTb], [pb])
            act(kT[:, :], pa, AF.Copy, [pb], [kTb], scale=float(HD ** -0.5))
            yield
            wv, wvb = load_unit(("v", h))
            pv, pvb = bank()
            for ci in order:
                mm(pv[:, ci * 256:(ci + 1) * 256],
                   [(hT[:, kt * TB + ci * 128:kt * TB + (ci + 1) * 128], wv[:, kt * 256:(kt + 1) * 256]) for kt in range(KT)],
                   [wvb, hTb], [pvb])
            pt, ptb = bankb()
            for ci in order:
                for j in range(2):
                    P.op("pe", lambda E, j=j, ci=ci, pt=pt: E.transpose(out=pt[:, ci * 256 + j * 128:ci * 256 + (j + 1) * 128],
                                                                       in_=kT[:, j * TB + ci * 128:j * TB + (ci + 1) * 128],
                                                                       identity=identb[:, :]),
                         [kTb, identbb], [ptb])
            act(ktk[:, :], pt[:, 0:NCH * 256], AF.Copy, [ptb], [ktkb])
            for oi, ci in enumerate(order):
                g0 = (oi + 1) * 24
                ts("dve", vx[:, ci * VS:ci * VS + 256], pv[:, ci * 256:(ci + 1) * 256], GT[:, g0 + h:g0 + h + 1], None, ALU.mult, None,
                   [pvb, GTb[oi + 1]], [vxb])
                P.op("pool", lambda E, ci=ci, g0=g0: E.tensor_copy(out=vx[:, ci * VS + 256:ci * VS + 257], in_=GT[:, g0 + h:g0 + h + 1]),
                     [GTb[oi + 1]], [vxb])
            yield
            if full:
                wo, wob = load_unit(("o", h))
                wz, wzb = load_unit(("z", h))
                goz, gozb = gozs.get()
                c.update(goz=goz, gozb=gozb)
                for ci in order:
                    poz, pozb = bank()
                    mm(poz[:, 0:256], [(hT[:, kt * TB + ci * 128:kt * TB + (ci + 1) * 128], wo[:, kt * 256:(kt + 1) * 256]) for kt in range(KT)],
                       [wob, hTb], [pozb])
                    mm(poz[:, 256:512], [(hT[:, kt * TB + ci * 128:kt * TB + (ci + 1) * 128], wz[:, kt * 256:(kt + 1) * 256]) for kt in range(KT)],
                       [wzb, hTb], [pozb])
                    toz, tozb = tF.get()
                    act(toz[:, :], poz, AF.Tanh, [pozb], [tozb], scale=0.5)
                    t1, t1b = tF.get()
                    stt("dve", t1[:, 0:256], toz[:, 256:512], 1.0, poz[:, 256:512], ALU.add, ALU.mult, [tozb, pozb], [t1b])
                    stt("dve", goz[:, ci * 256:(ci + 1) * 256], toz[:, 0:256], 1.0, t1[:, 0:256], ALU.add, ALU.mult, [tozb, t1b], [gozb])
                    yield

        def recur_a(c):
            s, bi, d, h, order, full = c["s"], c["bi"], c["d"], c["h"], c["order"], c["full"]
            qT, qTb, kT, kTb, ktk, ktkb, vx, vxb = c["qT"], c["qTb"], c["kT"], c["kTb"], c["ktk"], c["ktkb"], c["vx"], c["vxb"]
            tok0 = seq_off[s] + bi * TB
            tri = C_TF if d == 0 else C_TB
            u0 = h * 2 * VS
            for oi, ci in enumerate(order):
                slot = oi + 1
                g0 = slot * 24
                gp = (slot - 1) * 24
                first = c["first_blk"] and oi == 0
                last = c["last_blk"] and oi == len(order) - 1
                rows = slice(tok0 + ci * 128, tok0 + (ci + 1) * 128)
                key = (s, bi, ci, h)
                if not first:
                    cb_, cbb = cbf.get()
                    act(cb_[:, :], U[:, u0:u0 + 2 * VS], AF.Copy, [Ub[h], GTb[slot - 1]], [cbb], scale=GT[:, gp + 16 + h:gp + 17 + h])
                hd, hdb = tFr.get()
                if full and VARIANT != "v2":
                    dma("sp", hd[:, 256:512], hb_d[rows, h * 256:(h + 1) * 256], [hbb[key]], [hdb])
                pS, pSb = bank()
                mm(pS[:, 0:128], [(kT[:, j * TB + ci * 128:j * TB + (ci + 1) * 128], qT[:, j * TB + ci * 128:j * TB + (ci + 1) * 128])
                                  for j in range(2)], [kTb, qTb], [pSb])
                Pm, Pmb = tHr.get()
                tt("dve", Pm[:, 0:128], pS[:, 0:128], cst[:, tri:tri + 128], ALU.mult, [pSb, cstb], [Pmb])
                if not last:
                    for j in range(2):
                        pC, pCb = bank()
                        mm(pC[:, 0:257], [(ktk[:, ci * 256 + j * 128:ci * 256 + (j + 1) * 128], vx[:, ci * VS:ci * VS + 257])],
                           [ktkb, vxb], [pCb])
                        uo = U[:, u0 + j * VS:u0 + j * VS + 257]
                        if first:
                            P.op("dve", lambda E, uo=uo, pC=pC: E.tensor_copy(out=uo, in_=pC[:, 0:257]), [pCb], [Ub[h]])
                        else:
                            stt("dve", uo, uo, GT[:, gp + 16 + h:gp + 17 + h], pC[:, 0:257], ALU.mult, ALU.add,
                                [pCb, Ub[h], GTb[slot - 1]], [Ub[h]])
                yield
                pN, pNb = bank()
                pairs = [(Pm[:, 0:128], vx[:, ci * VS:ci * VS + 257])]
                rd = [Pmb, vxb]
                if not first:
                    for j in range(2):
                        pairs.append((qT[:, j * TB + ci * 128:j * TB + (ci + 1) * 128], cb_[:, j * VS:j * VS + 257]))
                    rd += [qTb, cbb]
                mm(pN[:, 0:257], pairs, rd, [pNb])
                r0, r0b = smr.get()
                tt("dve", r0[:, 0:1], pN[:, 256:257], GT[:, g0 + 8 + h:g0 + 9 + h], ALU.max, [pNb, GTb[slot]], [r0b])
                r1, r1b = smr.get()
                stt("dve", r1[:, 0:1], pN[:, 256:257], -1.0, r0[:, 0:1], ALU.mult, ALU.max, [pNb, r0b], [r1b])
                r2, r2b = smr.get()
                P.op("dve", lambda E, r1=r1, r2=r2: E.reciprocal(out=r2[:, 0:1], in_=r1[:, 0:1]), [r1b], [r2b])
                act(hd[:, 0:256], pN[:, 0:256], AF.Copy, [pNb, r2b], [hdb], scale=r2[:, 0:1])
                if not full:
                    hbb[key] = Buf("hb")
                    dma("pool", hb_d[rows, h * 256:(h + 1) * 256], hd[:, 0:256], [hdb], [hbb[key]])
                    yield
                    continue
                goz, gozb = c["goz"], c["gozb"]
                if VARIANT == "v2":
                    dma("sp", hd[:, 256:512], hb_d[rows, h * 256:(h + 1) * 256], [hbb[key]], [hdb])
                hs, hsb = tFr.get()
                tt("dve", hs[:, 0:256], hd[:, 0:256], hd[:, 256:512], ALU.add, [hdb], [hsb])
                st6, st6b = smr.get()
                P.op("dve", lambda E, st6=st6, hs=hs: E.bn_stats(out=st6[:, 0:6], in_=hs[:, 0:256]), [hsb], [st6b])
                mv, mvb = smr.get()
                P.op("dve", lambda E, st6=st6, mv=mv: E.bn_aggr(out=mv[:, 0:2], in_=st6[:, 0:6]), [st6b], [mvb])
                l1, l1b = smr.get()
                act(l1[:, 0:1], mv[:, 1:2], AF.Ln, [mvb], [l1b], bias=float(EPS))
                rs_, rsb = smr.get()
                act(rs_[:, 0:1], l1[:, 0:1], AF.Exp, [l1b], [rsb], scale=-0.5, bias=float(np.log(0.25)))
                stt("dve", hs[:, 256:512], hs[:, 0:256], mv[:, 0:1], goz[:, ci * 256:(ci + 1) * 256], ALU.subtract, ALU.mult,
                    [hsb, mvb, gozb], [hsb])
                ym, ymb = tHr.get()
                stt("dve", ym[:, 0:256], hs[:, 256:512], rs_[:, 0:1], cst[:, C_ZERO:C_ZERO + 256], ALU.mult, ALU.add,
                    [hsb, rsb, cstb], [ymb])
                yield
                pt, ptb = bankb()
                for j in range(2):
                    P.op("pe", lambda E, j=j, pt=pt, ym=ym: E.transpose(out=pt[:, j * 128:(j + 1) * 128],
                                                                       in_=ym[:, j * 128:(j + 1) * 128], identity=identb[:, :]),
                         [ymb, identbb], [ptb])
                for j in range(2):
                    act(ymT[:, (2 * h + j) * TB + ci * 128:(2 * h + j) * TB + (ci + 1) * 128], pt[:, j * 128:(j + 1) * 128], AF.Copy,
                        [ptb, cstb], [ymTb[h]], scale=(1.0 if VARIANT == "v1" else cst[:, C_MHG + 2 * h + j:C_MHG + 2 * h + j + 1]))
                yield

        def phase_b(s, bi):
            for ft in range(16):
                w1, w1b = load_unit(("b1", ft))
                w2, w2b = load_unit(("b2", ft))
                p1, p1b = bank()
                for half in range(2):
                    mm(p1[:, half * 256:(half + 1) * 256],
                       [(w1[:, kt * 256 + half * 128:kt * 256 + (half + 1) * 128], hT[:, kt * TB:(kt + 1) * TB]) for kt in range(KT)],
                       [w1b, hTb], [p1b])
                p2, p2b = bank()
                for half in range(2):
                    mm(p2[:, half * 256:(half + 1) * 256],
                       [(w2[:, kt * 256 + half * 128:kt * 256 + (half + 1) * 128], hT[:, kt * TB:(kt + 1) * TB]) for kt in range(KT)],
                       [w2b, hTb], [p2b])
                if CONV_ENG == "actpool":
                    ccs, ccsb = tF.get()
                    act(ccs[:, 0:TB], p1[:, 256:512], AF.Copy, [p1b], [ccsb])
                    u, ub = tF.get()
                    tt("dve", u[:, 1:TB + 1], ccs[:, 0:TB], p2[:, 0:256], ALU.mult, [ccsb, p2b], [ub])
                    if bi == 0:
                        P.op("pool", lambda E, u=u: E.memset(u[:, 0:1], 0.0), [], [ub])
                    else:
                        P.op("pool", lambda E, u=u, ft=ft: E.tensor_copy(out=u[:, 0:1], in_=ulast[:, ft:ft + 1]), [ulastb[ft]], [ub])
                    if bi == nblk[s] - 1:
                        P.op("pool", lambda E, u=u: E.memset(u[:, TB + 1:TB + 2], 0.0), [], [ub])
                    else:
                        hi = halo_idx[(s, bi)]
                        P.op("pool", lambda E, u=u, ft=ft, hi=hi: E.tensor_copy(out=u[:, TB + 1:TB + 2],
                                                                               in_=uhalo[:, ft * NHALO + hi:ft * NHALO + hi + 1]),
                             [uhalob], [ub])
                    P.op("pool", lambda E, u=u, ft=ft: E.tensor_copy(out=ulast[:, ft:ft + 1], in_=u[:, TB:TB + 1]), [ub], [ulastb[ft]])
                    t3, t3b = tF.get()
                    act(ccs[:, 0:TB], u[:, 0:TB], AF.Copy, [ub, cstb], [ccsb], scale=cst[:, C_CW + ft:C_CW + ft + 1])
                    act(ccs[:, 256:512], u[:, 1:TB + 1], AF.Copy, [ub, cstb], [ccsb], scale=cst[:, C_CW + 16 + ft:C_CW + 17 + ft])
                    act(t3[:, 0:TB], u[:, 2:TB + 2], AF.Copy, [ub, cstb], [t3b], scale=cst[:, C_CW + 32 + ft:C_CW + 33 + ft])
                    tt("pool", ccs[:, 0:TB], ccs[:, 0:TB], ccs[:, 256:512], ALU.add, [ccsb], [ccsb])
                    tt("pool", t3[:, 256:512], ccs[:, 0:TB], t3[:, 0:TB], ALU.add, [ccsb, t3b], [t3b])
                    yfin = t3[:, 256:512]
                    ccsb = t3b
                elif CONV_ENG == "aligned":
                    ccs, ccsb = tF.get()
                    act(ccs[:, 0:TB], p1[:, 256:512], AF.Copy, [p1b], [ccsb])
                    act(ccs[:, 256:256 + TB - 1], p1[:, 257:512], AF.Copy, [p1b], [ccsb])
                    u, ub = tF.get()
                    um, umb = tF.get()
                    tt("dve", u[:, 0:TB], ccs[:, 0:TB], p2[:, 0:TB], ALU.mult, [ccsb, p2b], [ub])
                    tt("dve", u[:, 256:256 + TB - 1], ccs[:, 256:256 + TB - 1], p2[:, 1:TB], ALU.mult, [ccsb, p2b], [ub])
                    tt("dve", um[:, 1:TB], ccs[:, 0:TB - 1], p2[:, 0:TB - 1], ALU.mult, [ccsb, p2b], [umb])
                    if bi == 0:
                        P.op("pool", lambda E, um=um: E.memset(um[:, 0:1], 0.0), [], [umb])
                    else:
                        P.op("pool", lambda E, um=um, ft=ft: E.tensor_copy(out=um[:, 0:1], in_=ulast[:, ft:ft + 1]), [ulastb[ft]], [umb])
                    if bi == nblk[s] - 1:
                        P.op("pool", lambda E, u=u: E.memset(u[:, 256 + TB - 1:256 + TB], 0.0), [], [ub])
                    else:
                        hi = halo_idx[(s, bi)]
                        P.op("pool", lambda E, u=u, ft=ft, hi=hi: E.tensor_copy(out=u[:, 256 + TB - 1:256 + TB],
                                                                               in_=uhalo[:, ft * NHALO + hi:ft * NHALO + hi + 1]),
                             [uhalob], [ub])
                    P.op("pool", lambda E, u=u, ft=ft: E.tensor_copy(out=ulast[:, ft:ft + 1], in_=u[:, TB - 1:TB]), [ub], [ulastb[ft]])
                    stt("dve", ccs[:, 0:TB], um[:, 0:TB], cst[:, C_CW + ft:C_CW + ft + 1], cst[:, C_ZERO:C_ZERO + TB], ALU.mult, ALU.add,
                        [umb, ccsb, ub, cstb], [ccsb])
                    stt("dve", ccs[:, 256:512], u[:, 0:TB], cst[:, C_CW + 16 + ft:C_CW + 17 + ft], ccs[:, 0:TB], ALU.mult, ALU.add,
                        [ub, ccsb, cstb], [ccsb])
                    stt("dve", ccs[:, 0:TB], u[:, 256:512], cst[:, C_CW + 32 + ft:C_CW + 33 + ft], ccs[:, 256:512], ALU.mult, ALU.add,
                        [ub, ccsb, cstb], [ccsb])
                    yfin = ccs[:, 0:TB]
                else:
                    ccs, ccsb = tF.get()
                    act(ccs[:, 0:TB], p1[:, 256:512], AF.Copy, [p1b], [ccsb])
                    u, ub = tF.get()
                    tt("dve", u[:, 1:TB + 1], ccs[:, 0:TB], p2[:, 0:256], ALU.mult, [ccsb, p2b], [ub])
                    if bi == 0:
                        P.op("pool", lambda E, u=u: E.memset(u[:, 0:1], 0.0), [], [ub])
                    else:
                        P.op("pool", lambda E, u=u, ft=ft: E.tensor_copy(out=u[:, 0:1], in_=ulast[:, ft:ft + 1]), [ulastb[ft]], [ub])
                    if bi == nblk[s] - 1:
                        P.op("pool", lambda E, u=u: E.memset(u[:, TB + 1:TB + 2], 0.0), [], [ub])
                    else:
                        hi = halo_idx[(s, bi)]
                        P.op("pool", lambda E, u=u, ft=ft, hi=hi: E.tensor_copy(out=u[:, TB + 1:TB + 2],
                                                                               in_=uhalo[:, ft * NHALO + hi:ft * NHALO + hi + 1]),
                             [uhalob], [ub])
                    P.op("pool", lambda E, u=u, ft=ft: E.tensor_copy(out=ulast[:, ft:ft + 1], in_=u[:, TB:TB + 1]), [ub], [ulastb[ft]])
                    stt(CONV_ENG, ccs[:, 256:512], u[:, 0:TB], cst[:, C_CW + ft:C_CW + ft + 1], cst[:, C_ZERO:C_ZERO + TB], ALU.mult, ALU.add,
                        [ub, cstb], [ccsb])
                    stt(CONV_ENG, ccs[:, 0:256], u[:, 1:TB + 1], cst[:, C_CW + 16 + ft:C_CW + 17 + ft], ccs[:, 256:512], ALU.mult, ALU.add,
                        [ub, ccsb, cstb], [ccsb])
                    stt(CONV_ENG, ccs[:, 256:512], u[:, 2:TB + 2], cst[:, C_CW + 32 + ft:C_CW + 33 + ft], ccs[:, 0:256], ALU.mult, ALU.add,
                        [ub, ccsb, cstb], [ccsb])

                    yfin = ccs[:, 256:512]
                tz, tzb = tF.get()
                act(tz[:, 0:TB], p2[:, 256:512], AF.Tanh, [p2b], [tzb], scale=0.5)
                stt("dve", tz[:, 256:512], tz[:, 0:TB], 1.0, p2[:, 256:512], ALU.add, ALU.mult, [tzb, p2b], [tzb])
                tt("dve", tz[:, 0:256], tz[:, 256:512], p1[:, 0:256], ALU.mult, [tzb, p1b], [tzb])
                stt("dve", ycT[:, ft * TB:(ft + 1) * TB], yfin, 0.5, tz[:, 0:256], ALU.mult, ALU.mult, [ccsb, tzb], [ycTb[ft]])
                yield

        def c_proj(s, bi, h, c):
            km_, km_b = kmh.get()
            dma("sp", km_[:, :], km_d[s][:, 4 * h * NMEM:(4 * h + 4) * NMEM], [], [km_b], extra=km_ops[s])
            vm_, vm_b = vmh.get()
            for mt in range(2):
                dma("sp", vm_[:, mt * HDA:(mt + 1) * HDA], vm_d[s][:, mt * D + h * HDA:mt * D + (h + 1) * HDA], [], [vm_b], extra=vm_ops[s])
            qa, qab = qas.get()
            c.update(km_=km_, km_b=km_b, vm_=vm_, vm_b=vm_b, qa=qa, qab=qab)
            for j2 in range(2):
                w, wb_ = load_unit(("qa", h, j2))
                pa, pb = bank()
                for jj in range(2):
                    mm(pa[:, jj * 256:(jj + 1) * 256],
                       [(w[:, kt * 256 + jj * 128:kt * 256 + (jj + 1) * 128], hT[:, kt * TB:(kt + 1) * TB]) for kt in range(KT)],
                       [wb_, hTb], [pb])
                act(qa[:, j2 * 2 * TB:(j2 + 1) * 2 * TB], pa, AF.Copy, [pb], [qab])
                yield

        def c_attn(s, bi, h, c):
            sc = float(HDA ** -0.5)
            km_, km_b, vm_, vm_b, qa, qab = c["km_"], c["km_b"], c["vm_"], c["vm_b"], c["qa"], c["qab"]
            pT, pTb = pTs.get()
            pns = []
            for ci in range(NCH):
                pS, pSb = bank()
                mm(pS[:, 0:NMEM], [(qa[:, j * TB + ci * 128:j * TB + (ci + 1) * 128], km_[:, j * NMEM:(j + 1) * NMEM]) for j in range(4)],
                   [qab, km_b], [pSb])
                mx, mxb = sm.get()
                P.op("dve", lambda E, mx=mx, pS=pS: E.reduce_max(out=mx[:, 0:1], in_=pS[:, 0:NMEM], axis=mybir.AxisListType.X), [pSb], [mxb])
                nm, nmb = sm.get()
                act(nm[:, 0:1], mx[:, 0:1], AF.Copy, [mxb], [nmb], scale=-sc)
                pe_, peb = tF.get()
                rs_, rsb = sm.get()
                act(pe_[:, 0:NMEM], pS[:, 0:NMEM], AF.Exp, [pSb, nmb], [peb, rsb], scale=sc, bias=nm[:, 0:1], accum=rs_[:, 0:1])
                ri, rib = sm.get()
                P.op("dve", lambda E, ri=ri, rs_=rs_: E.reciprocal(out=ri[:, 0:1], in_=rs_[:, 0:1]), [rsb], [rib])
                pn, pnb = tH.get()
                stt("dve", pn[:, 0:NMEM], pe_[:, 0:NMEM], ri[:, 0:1], cst[:, C_ZERO:C_ZERO + NMEM], ALU.mult, ALU.add, [peb, rib, cstb], [pnb])
                pns.append((pn, pnb))
                yield
            for ci in range(NCH):
                pn, pnb = pns[ci]
                pt, ptb = bankb()
                for mt in range(2):
                    P.op("pe", lambda E, mt=mt, pt=pt, pn=pn: E.transpose(out=pt[:, mt * 128:(mt + 1) * 128],
                                                                         in_=pn[:, mt * 128:(mt + 1) * 128], identity=identb[:, :]),
                         [pnb, identbb], [ptb])
                for mt in range(2):
                    act(pT[:, mt * TB + ci * 128:mt * TB + (ci + 1) * 128], pt[:, mt * 128:(mt + 1) * 128], AF.Copy, [ptb], [pTb])
                yield
            for j2 in range(2):
                w, wb_ = load_unit(("za", h, j2))
                for jj in range(2):
                    j = 2 * j2 + jj
                    poz, pozb = bank()
                    mm(poz[:, 256:512], [(w[:, kt * 256 + jj * 128:kt * 256 + (jj + 1) * 128], hT[:, kt * TB:(kt + 1) * TB]) for kt in range(KT)],
                       [wb_, hTb], [pozb])
                    mm(poz[:, 0:256], [(vm_[:, mt * HDA + j * 128:mt * HDA + (j + 1) * 128], pT[:, mt * TB:(mt + 1) * TB]) for mt in range(2)],
                       [vm_b, pTb], [pozb])
                    tz, tzb = tF.get()
                    act(tz[:, 0:TB], poz[:, 256:512], AF.Tanh, [pozb], [tzb], scale=0.5)
                    stt("dve", tz[:, 256:512], tz[:, 0:TB], 1.0, poz[:, 256:512], ALU.add, ALU.mult, [tzb, pozb], [tzb])
                    ft = 4 * h + j
                    stt("dve", yaT[:, ft * TB:(ft + 1) * TB], poz[:, 0:256], 0.5, tz[:, 256:512], ALU.mult, ALU.mult, [pozb, tzb], [yaTb[h]])
                    yield

        def phase_c(s, bi):
            cc_ = [dict() for _ in range(NHA)]
            if not CPIPE:
                for h in range(NHA):
                    yield from c_proj(s, bi, h, cc_[h])
                    yield from c_attn(s, bi, h, cc_[h])
                return
            yield from c_proj(s, bi, 0, cc_[0])
            for h in range(NHA):
                fg = c_attn(s, bi, h, cc_[h])
                bg = c_proj(s, bi, h + 1, cc_[h + 1]) if h + 1 < NHA else iter(())
                for _ in fg:
                    step(bg, 1)
                    yield
                for _ in bg:
                    yield

        def phase_d(s, bi):
            tok0 = seq_off[s] + bi * TB
            ysrc = [(ymT, ymTb), (ycT, ycTb), (yaT, yaTb)]
            for pr in range(8):
                accs = [None, None]
                for b in range(3):
                    wm_, wmb = load_unit(("mg", pr, b))
                    wb2, wb2b = load_unit(("wb", pr, b))
                    yt_, ybl = ysrc[b]
                    for jj in range(2):
                        nt = 2 * pr + jj
                        pg, pgb = bank()
                        mm(pg[:, 0:256], [(wm_[:, kt * 256 + jj * 128:kt * 256 + (jj + 1) * 128], hT[:, kt * TB:(kt + 1) * TB]) for kt in range(KT)],
                           [wmb, hTb], [pgb])
                        mm(pg[:, 256:512], [(wb2[:, kt * 256 + jj * 128:kt * 256 + (jj + 1) * 128], yt_[:, kt * TB:(kt + 1) * TB]) for kt in range(KT)],
                           [wb2b] + ybl, [pgb])
                        tg, tgb = tF.get()
                        act(tg[:, 0:TB], pg[:, 0:256], AF.Tanh, [pgb], [tgb], scale=0.5)
                        stt("dve", tg[:, 256:512], tg[:, 0:TB], 1.0, pg[:, 256:512], ALU.add, ALU.mult, [tgb, pgb], [tgb])
                        if b == 0:
                            accs[jj] = (tg, tgb)
                        elif b == 1:
                            acc, accb = accs[jj]
                            tt("pool", tg[:, 0:256], acc[:, 256:512], tg[:, 256:512], ALU.add, [accb, tgb], [tgb])
                            accs[jj] = (tg, tgb)
                        else:
                            acc, accb = accs[jj]
                            tt("pool", mgT[:, nt * TB:(nt + 1) * TB], acc[:, 0:256], tg[:, 256:512], ALU.add, [accb, tgb], [mgTb[nt]])
            for u in range(8):
                w, wb_ = load_unit(("out", u))
                pa, pb = bank()
                for ci in range(NCH):
                    mm(pa[:, ci * 256:(ci + 1) * 256],
                       [(mgT[:, kt * TB + ci * 128:kt * TB + (ci + 1) * 128], w[:, kt * 256:(kt + 1) * 256]) for kt in range(KT)],
                       [wb_] + mgTb, [pb])
                for ci in range(NCH):
                    xo = xtok[ci][:, u * 256:(u + 1) * 256]
                    stt("dve", xo, pa[:, ci * 256:(ci + 1) * 256], 0.5, xo, ALU.mult, ALU.add, [pb, xtokb[ci]], [xtokb[ci]])
            for ci in range(NCH):
                act(junk[:, :], xtok[ci][:, :], AF.Square, [xtokb[ci]], [junkb, ssqb], accum=ssq[:, 2 + ci:3 + ci])
                act(lnt[:, 2 + ci:3 + ci], ssq[:, 2 + ci:3 + ci], AF.Ln, [ssqb], [lntb], bias=float(D * EPS))
                act(rstd[:, 2 + ci:3 + ci], lnt[:, 2 + ci:3 + ci], AF.Exp, [lntb], [rstdb], scale=-0.5, bias=float(0.5 * np.log(D)))
                if FIN == "split":
                    for hf in range(2):
                        xs = xtok[ci][:, hf * 1024:(hf + 1) * 1024]
                        stt("pool" if hf else "dve", xs, xs, rstd[:, 2 + ci:3 + ci], cst[:, C_FG + hf * 1024:C_FG + (hf + 1) * 1024],
                            ALU.mult, ALU.mult, [xtokb[ci], rstdb, cstb], [xtokb[ci]])
                else:
                    for hf in range(4):
                        xs = xtok[ci][:, hf * 512:(hf + 1) * 512]
                        stt("dve", xs, xs, rstd[:, 2 + ci:3 + ci], cst[:, C_FG + hf * 512:C_FG + (hf + 1) * 512],
                            ALU.mult, ALU.mult, [xtokb[ci], rstdb, cstb], [xtokb[ci]])
                o = dma("pool", y_out[tok0 + ci * 128:tok0 + (ci + 1) * 128, :], xtok[ci][:, :], [xtokb[ci]], [])
                out_ops.append(o)

        def drain(g):
            for _ in g:
                pass

        def step(g, n=1):
            for _ in range(n):
                try:
                    next(g)
                except StopIteration:
                    return False
            return True

        def interleave(fg, bg, k):
            for _ in fg:
                step(bg, k)
            drain(bg)

        def chain(*gs):
            for g in gs:
                yield from g

        def run_pass(d, full):
            for s in range(NSEQ):
                blocks = list(range(nblk[s]))
                if d == 1:
                    blocks = blocks[::-1]
                order = list(range(NCH)) if d == 0 else list(range(NCH))[::-1]
                for oi_b, bi in enumerate(blocks):
                    tok0 = seq_off[s] + bi * TB
                    norm_block([(x_tok[tok0 + c * 128:tok0 + (c + 1) * 128, :], 128) for c in range(NCH)],
                               x_T[blk_base[s] + bi], TB, C_G)
                    ctx = [mk_ctx(s, bi, d, h, order, oi_b == 0, oi_b == len(blocks) - 1, full) for h in range(NH)]
                    p0 = proj_a(ctx[0])
                    gg = chain(*[gates(ci, d, oi + 1) for oi, ci in enumerate(order)])
                    step(gg, 2)
                    step(p0, 1)
                    step(gg, 2)
                    step(p0, 1)
                    drain(gg)
                    drain(p0)
                    if full:
                        bc = chain(phase_b(s, bi), phase_c(s, bi))
                    for h in range(NH):
                        bgs = []
                        if h + 1 < NH:
                            bgs.append(proj_a(ctx[h + 1]))
                        bg = chain(*bgs)
                        fg = recur_a(ctx[h])
                        if full and not INTERLEAVE_BC:
                            interleave(fg, bg, 1)
                        elif full:
                            for _ in fg:
                                if not step(bg, 1):
                                    step(bc, 2)
                                else:
                                    step(bc, 1)
                            drain(bg)
                        else:
                            interleave(fg, bg, 1)
                    if not full:
                        cast_more(6)
                    P.op("pool", lambda E: E.tensor_copy(out=GT[:, 16:24], in_=GT[:, NCH * 24 + 16:NCH * 24 + 24]),
                         [GTb[NCH]], [GTb[0]])
                    if full:
                        chk("p2a")
                        drain(bc)
                        chk("p2bc")
                        phase_d(s, bi)
                        chk("p2d")

        try:
            chk("cast")
            for s in range(NSEQ):
                kv_prologue(s)
            chk("kv")
            halo_prepass()
            chk("halo")
            run_pass(1, False)
            cast_more(NU)
            chk("pass1")
            run_pass(0, True)
        except _Stop:
            pass
        if stop is not None:
            out_ops = [o for e in P.ENG for o in P.ops[e] if o.dma]
        P.op("sp", [], [], [], extra=out_ops)
        P.emit(nc, stack)
    return nc


def _tile_T(a):
    T = a.shape[0]
    return np.ascontiguousarray(a.T.reshape(KT, 128, T).transpose(1, 0, 2)).reshape(128, KT * T)


def prepare_shared(norm_g, w_in, b_if, conv_w, mem_norm_g, w_kv_mem, mh_norm_g, w_branch, w_out, final_norm_g):
    units, uidx = unit_catalog()
    srcs = {"in": w_in[0], "kv": w_kv_mem[0], "br0": w_branch[0, 0], "br1": w_branch[0, 1], "br2": w_branch[0, 2], "out": w_out[0]}
    ws32 = np.empty((len(units), 128, KT * 256), np.float32)
    for i, (src, cols) in enumerate(units):
        blk = srcs[src][:, cols]
        ws32[i] = blk.reshape(KT, 128, 256).transpose(1, 0, 2).reshape(128, KT * 256)
    perm = np.concatenate([np.arange(0, 8), np.arange(16, 24), np.arange(8, 16), np.arange(24, 32)])
    wgc = w_in[0][:, 10240 + perm]
    wg32 = np.ascontiguousarray(wgc.reshape(KT, 128, 32).transpose(1, 0, 2)).reshape(128, KT * 32)
    cst = np.zeros((128, C_END), np.float32)
    cst[:, C_ID:C_ID + 128] = np.eye(128, dtype=np.float32)
    cst[:, C_ONE:C_ONE + 128] = 1.0
    ii = np.arange(128)
    cst[:, C_TF:C_TF + 128] = (ii[:, None] <= ii[None, :]).astype(np.float32)
    cst[:, C_TB:C_TB + 128] = (ii[:, None] >= ii[None, :]).astype(np.float32)
    cst[:, C_G:C_G + 16] = norm_g[0].reshape(KT, 128).T
    cst[:, C_GM:C_GM + 16] = mem_norm_g[0].reshape(KT, 128).T
    for j in range(3):
        cst[:, C_CW + 16 * j:C_CW + 16 * (j + 1)] = conv_w[0, j].reshape(KT, 128).T
    cst[:, C_BIF:C_BIF + 32] = b_if[0][perm][None, :]
    cst[:, C_MHG:C_MHG + 16] = mh_norm_g[0].reshape(KT, 128).T
    cst[:, C_FG:C_FG + D] = final_norm_g[None, :]
    return ws32, wg32, cst


def prepare_core(seqs, mems):
    x_tok = np.ascontiguousarray(np.concatenate(seqs, axis=0))
    blocks = []
    halos = []
    for x in seqs:
        nb = x.shape[0] // TB
        for bi in range(nb):
            blocks.append(_tile_T(x[bi * TB:(bi + 1) * TB]))
            if bi < nb - 1:
                halos.append(x[(bi + 1) * TB])
    x_T = np.stack(blocks, axis=0)
    h_tok = np.zeros((32, D), np.float32)
    if halos:
        h_tok[:len(halos)] = np.stack(halos, axis=0)
    h_tok[len(halos):] = 1.0
    h_T = _tile_T(h_tok)
    m_tok = np.ascontiguousarray(np.concatenate(mems, axis=0))
    m_T = np.stack([_tile_T(m) for m in mems], axis=0)
    return {"x_tok": x_tok, "x_T": x_T, "m_tok": m_tok, "m_T": m_T, "h_tok": h_tok, "h_T": h_T}


def kernel(x_prompt, x_sample, mem_prompt, mem_sample, norm_g, w_in, b_if, conv_w, mem_norm_g,
           w_kv_mem, mh_norm_g, w_branch, w_out, final_norm_g):
    f = lambda a: np.asarray(a, dtype=np.float32)
    x_prompt, x_sample, mem_prompt, mem_sample = f(x_prompt), f(x_sample), f(mem_prompt), f(mem_sample)
    ws32, wg32, cst = prepare_shared(f(norm_g), f(w_in), f(b_if), f(conv_w), f(mem_norm_g), f(w_kv_mem), f(mh_norm_g),
                                     f(w_branch), f(w_out), f(final_norm_g))
    n = 8
    SP, SS = x_prompt.shape[1], x_sample.shape[1]
    nc = build_program([SP, SS, SS])
    in_maps = []
    for i in range(n):
        m = prepare_core([x_prompt[i], x_sample[2 * i], x_sample[2 * i + 1]],
                         [mem_prompt[i], mem_sample[2 * i], mem_sample[2 * i + 1]])
        m.update({"ws32": ws32, "wg32": wg32, "cst_in": cst})
        in_maps.append(m)
    res = run_bass_kernel_spmd(nc, in_maps, core_ids=list(range(n)))
    y_prompt = np.empty_like(x_prompt)
    y_sample = np.empty_like(x_sample)
    for i in range(n):
        y = res.results[i]["y"]
        y_prompt[i] = y[0:SP]
        y_sample[2 * i] = y[SP:SP + SS]
        y_sample[2 * i + 1] = y[SP + SS:SP + 2 * SS]
    return (y_prompt, y_sample)
```

```python
import contextlib
import numpy as np
import concourse.bass as bass
import concourse.mybir as mybir
from concourse.bass_utils import run_bass_kernel_spmd

F32 = mybir.dt.float32
BF16 = mybir.dt.bfloat16
AF = mybir.ActivationFunctionType
ALU = mybir.AluOpType

D = 2048
KT = 16
TB = 256
NCH = TB // 128
NH = 8
HD = 256
NHA = 4
HDA = 512
NMEM = 256
EPS = 1e-6
VS = 264
NSLOT = 5
import os
VARIANT = os.environ.get("KVARIANT", "")
INTERLEAVE_BC = os.environ.get("KBC", "0") == "1"
CPIPE = os.environ.get("KCPIPE", "1") == "1"
DIAG_ENG = os.environ.get("KDIAG", "dve")
FIN = os.environ.get("KFIN", "dve4")
CONV_ENG = os.environ.get("KCONV", "actpool")
SQD = float(np.sqrt(D))

C_ID, C_ONE, C_TF, C_TB, C_G, C_GM, C_CW, C_BIF, C_MHG, C_FG = 0, 128, 256, 384, 512, 528, 544, 592, 624, 640
C_ZERO = 2688
C_END = 2944


def unit_catalog():
    units = []
    idx = {}

    def add(key, src, cols):
        idx[key] = len(units)
        units.append((src, np.asarray(cols, dtype=np.int64)))

    r = np.arange
    OQ, OK_, OV, OO, OZ = 0, 2048, 4096, 6144, 8192
    OCB, OCC, OCX, OZC = 10272, 12320, 14368, 16416
    OQA, OZA, OMG = 18464, 20512, 22560
    for h in range(NH):
        add(("q", h), "in", OQ + h * 256 + r(256))
        add(("k", h), "in", OK_ + h * 256 + r(256))
        add(("v", h), "in", OV + h * 256 + r(256))
    for u in range(8):
        add(("kk", u), "kv", u * 256 + r(256))
    for u in range(8):
        add(("kv", u), "kv", 2048 + u * 256 + r(256))
    for ft in range(16):
        add(("b1", ft), "in", np.concatenate([OCB + ft * 128 + r(128), OCC + ft * 128 + r(128)]))
        add(("b2", ft), "in", np.concatenate([OCX + ft * 128 + r(128), OZC + ft * 128 + r(128)]))
    for h in range(NH):
        add(("o", h), "in", OO + h * 256 + r(256))
        add(("z", h), "in", OZ + h * 256 + r(256))
    for h in range(NHA):
        for j in range(2):
            add(("qa", h, j), "in", OQA + h * 512 + j * 256 + r(256))
        for j in range(2):
            add(("za", h, j), "in", OZA + h * 512 + j * 256 + r(256))
    for pr in range(8):
        for b in range(3):
            add(("mg", pr, b), "in", OMG + b * 2048 + pr * 256 + r(256))
        for b in range(3):
            add(("wb", pr, b), "br%d" % b, pr * 256 + r(256))
    for u in range(8):
        add(("out", u), "out", u * 256 + r(256))
    return units, idx


class Buf:
    __slots__ = ("name", "last_w", "readers", "const")

    def __init__(self, name, const=False):
        self.name = name
        self.last_w = None
        self.readers = []
        self.const = const


class Op:
    __slots__ = ("eng", "fns", "deps", "signal", "ev", "dma", "pre")

    def __init__(self, eng, fns, dma):
        self.eng = eng
        self.fns = fns
        self.dma = dma
        self.deps = []
        self.signal = False
        self.ev = None
        self.pre = None


class Prog:
    ENG = ["pe", "act", "dve", "pool", "sp"]

    def __init__(self):
        self.ops = {e: [] for e in self.ENG}
        self.nops = 0

    def op(self, eng, fns, reads=(), writes=(), dma=False, extra=()):
        if not isinstance(fns, (list, tuple)):
            fns = [fns]
        o = Op(eng, list(fns), dma)
        deps = {}
        for b in reads:
            if b.last_w is not None:
                deps[id(b.last_w)] = b.last_w
        for b in writes:
            if b.last_w is not None:
                deps[id(b.last_w)] = b.last_w
            for rd in b.readers:
                deps[id(rd)] = rd
        for x in extra:
            deps[id(x)] = x
        deps.pop(id(o), None)
        for d in deps.values():
            if eng == "pe" and d.eng == "pe" and not d.dma:
                continue
            d.signal = True
            o.deps.append(d)
        for b in reads:
            if not b.const:
                b.readers.append(o)
        for b in writes:
            b.last_w = o
            b.readers = []
        self.ops[eng].append(o)
        self.nops += 1
        return o

    def emit(self, nc, stack):
        esem = {e: stack.enter_context(nc.semaphore("es_" + e)) for e in ["pe", "act", "dve", "pool"]}
        npool = {"sp": int(os.environ.get("KSPQ", "4")), "pool": 4, "act": 4}
        dsem = {e: [stack.enter_context(nc.semaphore("ds_%s%d" % (e, i))) for i in range(n)] for e, n in npool.items()}
        for e, lst in self.ops.items():
            cnt = 0
            rr = 0
            use = [0] * npool.get(e, 0)
            for o in lst:
                if o.dma:
                    k = rr % len(use)
                    rr += 1
                    o.pre = (dsem[e][k], use[k] * 16)
                    use[k] += 1
                    o.ev = (dsem[e][k], use[k] * 16)
                elif o.signal:
                    cnt += 1
                    o.ev = (esem[e], cnt)
        block = stack.enter_context(nc.Block())
        ops = self.ops

        def make(e_name):
            def body(E):
                waited = {}
                for o in ops[e_name]:
                    need = [d.ev for d in o.deps]
                    if o.dma and o.pre[1] > 0:
                        need.append(o.pre)
                    for (s, v) in need:
                        k = id(s)
                        if waited.get(k, 0) < v:
                            E.wait_ge(s, v)
                            waited[k] = v
                    ins = None
                    for fn in o.fns:
                        ins = fn(E)
                    if o.dma:
                        ins.then_inc(o.ev[0], 16)
                    elif o.signal:
                        ins.then_inc(o.ev[0], 1)
            return body

        block.tensor(make("pe"))
        block.scalar(make("act"))
        block.vector(make("dve"))
        block.gpsimd(make("pool"))
        block.sync(make("sp"))


class Rot:
    def __init__(self, items):
        self.items = items
        self.i = 0

    def get(self):
        it = self.items[self.i % len(self.items)]
        self.i += 1
        return it


class _Stop(Exception):
    pass


def build_program(seq_lens, stop=None):
    units, uidx = unit_catalog()
    NU = len(units)
    NSEQ = len(seq_lens)
    NTOK = int(sum(seq_lens))
    seq_off = [int(sum(seq_lens[:i])) for i in range(NSEQ)]
    nblk = [L // TB for L in seq_lens]
    blk_base = [int(sum(nblk[:i])) for i in range(NSEQ)]
    NBLK = int(sum(nblk))
    halo_idx = {}
    for s in range(NSEQ):
        for bi in range(nblk[s] - 1):
            halo_idx[(s, bi)] = len(halo_idx)
    NHALO = 32
    assert len(halo_idx) <= NHALO

    nc = bass.Bass("TRN2", target_bir_lowering=False)
    dt_ = nc.dram_tensor
    x_tok = dt_("x_tok", [NTOK, D], F32, kind="ExternalInput").ap()
    x_T = dt_("x_T", [NBLK, 128, KT * TB], F32, kind="ExternalInput").ap()
    m_tok = dt_("m_tok", [NSEQ * NMEM, D], F32, kind="ExternalInput").ap()
    m_T = dt_("m_T", [NSEQ, 128, KT * NMEM], F32, kind="ExternalInput").ap()
    h_tok = dt_("h_tok", [NHALO, D], F32, kind="ExternalInput").ap()
    h_T = dt_("h_T", [128, KT * NHALO], F32, kind="ExternalInput").ap()
    ws32 = dt_("ws32", [NU, 128, KT * 256], F32, kind="ExternalInput").ap()
    wg32 = dt_("wg32", [128, KT * 32], F32, kind="ExternalInput").ap()
    cst_d = dt_("cst_in", [128, C_END], F32, kind="ExternalInput").ap()
    y_out = dt_("y", [NTOK, D], F32, kind="ExternalOutput").ap()
    ws16 = dt_("ws16", [NU, 128, KT * 256], BF16, kind="Internal").ap()
    hb_d = dt_("hb", [NTOK, D], F32, kind="Internal").ap()
    km_d = dt_("km", [NSEQ, 128, KT * NMEM], BF16, kind="Internal").ap()
    vm_d = dt_("vm", [NSEQ, 128, 2 * D], BF16, kind="Internal").ap()

    P = Prog()
    stack = contextlib.ExitStack()
    with stack:
        def sb(name, shape, dtype):
            return stack.enter_context(nc.sbuf_tensor(name, shape, dtype))

        cst = sb("cst", [128, C_END], F32)
        cstb = Buf("cst", const=True)
        identb = sb("identb", [128, 128], BF16)
        identbb = Buf("identb", const=True)
        wg = sb("wg", [128, KT * 32], BF16)
        wgb = Buf("wg", const=True)
        xtok = [sb("xtok%d" % c, [128, D], F32) for c in range(NCH)]
        xtokb = [Buf("xtok%d" % c) for c in range(NCH)]
        xts = Rot([(sb("xts%d" % i, [128, 4 * TB], F32), Buf("xts%d" % i)) for i in range(2)])
        hT = sb("hT", [128, KT * TB], BF16)
        hTb = Buf("hT")
        junk = sb("junk", [128, D], BF16)
        junkb = Buf("junk")
        ssq = sb("ssq", [128, 4], F32)
        ssqb = Buf("ssq")
        rstd = sb("rstd", [128, 4], F32)
        rstdb = Buf("rstd")
        lnt = sb("lnt", [128, 4], F32)
        lntb = Buf("lnt")
        diag = sb("diag", [128, 256], F32)
        diagb = Buf("diag")
        rbc = sb("rbc", [128, TB], F32)
        rbcb = Buf("rbc")
        ymT = sb("ymT", [128, KT * TB], BF16)
        ymTb = [Buf("ymT%d" % h) for h in range(NH)]
        ycT = sb("ycT", [128, KT * TB], BF16)
        ycTb = [Buf("ycT%d" % i) for i in range(KT)]
        yaT = sb("yaT", [128, KT * TB], BF16)
        yaTb = [Buf("yaT%d" % h) for h in range(NHA)]
        mgT = sb("mgT", [128, KT * TB], BF16)
        mgTb = [Buf("mgT%d" % i) for i in range(KT)]
        U = sb("U", [128, NH * 2 * VS], F32)
        Ub = [Buf("U%d" % h) for h in range(NH)]
        cbf = Rot([(sb("cbf%d" % i, [128, 2 * VS], BF16), Buf("cbf%d" % i)) for i in range(2)])
        GT = sb("GT", [128, (NCH + 1) * 24], F32)
        GTb = [Buf("GT%d" % i) for i in range(NCH + 1)]
        gsb = sb("gsb", [128, 16], F32)
        gsbb = Buf("gsb")
        e1 = sb("e1", [128, 8], F32)
        e1b = Buf("e1")
        nlf = sb("nlf", [128, 8], F32)
        nlfb = Buf("nlf")
        ipn = sb("ipn", [128, 8], F32)
        ipnb = Buf("ipn")
        uhalo = sb("uhalo", [128, KT * NHALO], F32)
        uhalob = Buf("uhalo")
        ulast = sb("ulast", [128, KT], F32)
        ulastb = [Buf("ulast%d" % i) for i in range(KT)]
        sm = Rot([(sb("sm%d" % i, [128, 8], F32), Buf("sm%d" % i)) for i in range(12)])
        qTs = Rot([(sb("qT%d" % i, [128, 2 * TB], BF16), Buf("qT%d" % i)) for i in range(2)])
        kTs = Rot([(sb("kT%d" % i, [128, 2 * TB], BF16), Buf("kT%d" % i)) for i in range(2)])
        kts = Rot([(sb("ktok%d" % i, [128, NCH * 256], BF16), Buf("ktok%d" % i)) for i in range(2)])
        vxs = Rot([(sb("vext%d" % i, [128, NCH * VS], BF16), Buf("vext%d" % i)) for i in range(2)])
        gozs = Rot([(sb("goz%d" % i, [128, NCH * 256], F32), Buf("goz%d" % i)) for i in range(2)])
        qas = Rot([(sb("qaT%d" % i, [128, 4 * TB], BF16), Buf("qaT%d" % i)) for i in range(2)])
        pTs = Rot([(sb("pT%d" % i, [128, 2 * TB], BF16), Buf("pT%d" % i)) for i in range(2)])
        kmh = Rot([(sb("kmh%d" % i, [128, 4 * NMEM], BF16), Buf("kmh%d" % i)) for i in range(2)])
        vmh = Rot([(sb("vmh%d" % i, [128, 2 * HDA], BF16), Buf("vmh%d" % i)) for i in range(2)])
        tF = Rot([(sb("tF%d" % i, [128, 512], F32), Buf("tF%d" % i)) for i in range(7)])
        tH = Rot([(sb("tH%d" % i, [128, 512], BF16), Buf("tH%d" % i)) for i in range(3)])
        tFr = Rot([(sb("tFr%d" % i, [128, 512], F32), Buf("tFr%d" % i)) for i in range(4)])
        tHr = Rot([(sb("tHr%d" % i, [128, 256], BF16), Buf("tHr%d" % i)) for i in range(4)])
        smr = Rot([(sb("smr%d" % i, [128, 8], F32), Buf("smr%d" % i)) for i in range(16)])
        wr = [sb("wr%d" % i, [128, KT * 256], BF16) for i in range(NSLOT)]
        wrb = [Buf("wr%d" % i) for i in range(NSLOT)]
        wring = Rot(list(zip(wr, wrb)))
        NPF = 6
        psf = stack.enter_context(nc.psum_tensor("psf", [128, NPF * 512], F32))
        psfb = [Buf("psf%d" % i) for i in range(NPF)]
        psb = stack.enter_context(nc.psum_tensor("psb", [128, 2 * 1024], BF16))
        psbb = [Buf("psb%d" % i) for i in range(2)]
        ps_i = [0]
        psb_i = [0]

        def bank():
            i = ps_i[0] % NPF
            ps_i[0] += 1
            return psf[:, i * 512:(i + 1) * 512], psfb[i]

        def bankb():
            i = psb_i[0] % 2
            psb_i[0] += 1
            return psb[:, i * 1024:(i + 1) * 1024], psbb[i]

        wsb = [Buf("ws%d" % u, const=True) for u in range(NU)]
        hbb = {}
        km_ops = [[] for _ in range(NSEQ)]
        vm_ops = [[] for _ in range(NSEQ)]
        out_ops = []

        def cc(a, b):
            return cst[:, a:b]

        def dma(eng, out, in_, reads, writes, extra=()):
            return P.op(eng, lambda E: E.dma_start(out=out, in_=in_), reads, writes, dma=True, extra=extra)

        def mm(out, pairs, reads, writes):
            n = len(pairs)
            fns = [(lambda E, l=l, r=r, i=i: E.matmul(out, lhsT=l, rhs=r, start=(i == 0), stop=(i == n - 1)))
                   for i, (l, r) in enumerate(pairs)]
            return P.op("pe", fns, reads, writes)

        def act(out, in_, func, reads, writes, scale=1.0, bias=0.0, accum=None):
            if accum is None:
                return P.op("act", lambda E: E.activation(out=out, in_=in_, func=func, bias=bias, scale=scale), reads, writes)
            return P.op("act", lambda E: E.activation(out=out, in_=in_, func=func, bias=bias, scale=scale, accum_out=accum), reads, writes)

        def tt(eng, out, a, b, op, reads, writes):
            return P.op(eng, lambda E: E.tensor_tensor(out=out, in0=a, in1=b, op=op), reads, writes)

        def ts(eng, out, a, s1, s2, op0, op1, reads, writes):
            if s2 is None:
                return P.op(eng, lambda E: E.tensor_scalar(out=out, in0=a, scalar1=s1, scalar2=None, op0=op0), reads, writes)
            return P.op(eng, lambda E: E.tensor_scalar(out=out, in0=a, scalar1=s1, scalar2=s2, op0=op0, op1=op1), reads, writes)

        def stt(eng, out, a, s, b, op0, op1, reads, writes):
            if eng == "pool":
                P.op(eng, lambda E: E.tensor_scalar(out=out, in0=a, scalar1=s, scalar2=None, op0=op0), reads, writes)
                return P.op(eng, lambda E: E.tensor_tensor(out=out, in0=out, in1=b, op=op1), list(reads) + list(writes), writes)
            return P.op(eng, lambda E: E.scalar_tensor_tensor(out=out, in0=a, scalar=s, in1=b, op0=op0, op1=op1), reads, writes)

        def load_unit(key):
            u = uidx[key]
            w, b = wring.get()
            dma("sp", w[:, :], ws16[u], [wsb[u]], [b])
            return w, b

        def chk(tag):
            if stop == tag:
                raise _Stop()

        dma("sp", cst[:, :], cst_d[:, :], [], [cstb])
        act(identb[:, :], cst[:, C_ID:C_ID + 128], AF.Copy, [cstb], [identbb])
        st0, st0b = xts.get()
        dma("sp", st0[:, 0:KT * 32], wg32[:, :], [], [st0b])
        act(wg[:, :], st0[:, 0:KT * 32], AF.Copy, [st0b], [wgb])
        P.op("pool", lambda E: E.memset(U[:, :], 0.0), [], Ub)
        for u in range(NU):
            dma("pool", ws16[u], ws32[u], [], [wsb[u]])

        def norm_block(tok_aps, xT_ap, nT, gcol):
            pa, pb = bank()
            for c, (tap, rows) in enumerate(tok_aps):
                dma("sp", xtok[c][0:rows, :], tap, [], [xtokb[c]])
                act(junk[0:rows, :], xtok[c][0:rows, :], AF.Square, [xtokb[c]], [junkb, ssqb], accum=ssq[0:rows, c:c + 1])
                act(lnt[0:rows, c:c + 1], ssq[0:rows, c:c + 1], AF.Ln, [ssqb], [lntb], bias=float(D * EPS))
                act(rstd[0:rows, c:c + 1], lnt[0:rows, c:c + 1], AF.Exp, [lntb], [rstdb], scale=-0.5)
                stt("dve", diag[0:rows, c * 128:c * 128 + rows], cst[0:rows, C_ID:C_ID + rows], rstd[0:rows, c:c + 1],
                    cst[0:rows, C_ZERO:C_ZERO + rows], ALU.mult, ALU.add, [rstdb, cstb], [diagb])
            for c, (tap, rows) in enumerate(tok_aps):
                mm(pa[:, c * 128:c * 128 + rows], [(cst[0:rows, C_ONE:C_ONE + 128], diag[0:rows, c * 128:c * 128 + rows])], [diagb, cstb], [pb])
            act(rbc[:, 0:nT], pa[:, 0:nT], AF.Copy, [pb], [rbcb], scale=SQD)
            chk("norm1")
            for q in range(4):
                st, stb = xts.get()
                if nT == TB:
                    dma("sp", st[:, :], xT_ap[:, q * 4 * nT:(q + 1) * 4 * nT], [], [stb])
                else:
                    dma("sp", st[:, 0:4 * nT], xT_ap[:, q * 4 * nT:(q + 1) * 4 * nT], [], [stb])
                for j in range(4):
                    kt = q * 4 + j
                    stt("dve", hT[:, kt * TB:kt * TB + nT], st[:, j * nT:(j + 1) * nT], cst[:, gcol + kt:gcol + kt + 1],
                        rbc[:, 0:nT], ALU.mult, ALU.mult, [stb, rbcb, cstb], [hTb])

        def kv_prologue(s):
          if True:
            norm_block([(m_tok[s * NMEM + c * 128:s * NMEM + (c + 1) * 128, :], 128) for c in range(2)], m_T[s], NMEM, C_GM)
            chk("norm")
            for u in range(8):
                w, wb_ = load_unit(("kk", u))
                pa, pb = bank()
                for j in range(2):
                    mm(pa[:, j * 256:(j + 1) * 256],
                       [(w[:, kt * 256 + j * 128:kt * 256 + (j + 1) * 128], hT[:, kt * TB:(kt + 1) * TB]) for kt in range(KT)],
                       [wb_, hTb], [pb])
                t, tb_ = tH.get()
                act(t[:, :], pa, AF.Copy, [pb], [tb_])
                km_ops[s].append(dma("pool", km_d[s][:, 2 * u * NMEM:(2 * u + 2) * NMEM], t[:, :], [tb_], []))
            chk("km")
            for u in range(8):
                w, wb_ = load_unit(("kv", u))
                pa, pb = bank()
                for mt in range(2):
                    mm(pa[:, mt * 256:(mt + 1) * 256],
                       [(hT[:, kt * TB + mt * 128:kt * TB + (mt + 1) * 128], w[:, kt * 256:(kt + 1) * 256]) for kt in range(KT)],
                       [wb_, hTb], [pb])
                t, tb_ = tH.get()
                act(t[:, :], pa, AF.Copy, [pb], [tb_])
                for mt in range(2):
                    vm_ops[s].append(dma("pool", vm_d[s][:, mt * D + u * 256:mt * D + (u + 1) * 256], t[:, mt * 256:(mt + 1) * 256], [tb_], []))

        def halo_prepass():
          norm_block([(h_tok[:, :], NHALO)], h_T, NHALO, C_G)
          for ft in range(16):
            w1, w1b = load_unit(("b1", ft))
            w2, w2b = load_unit(("b2", ft))
            pa, pb = bank()
            mm(pa[:, 0:NHALO], [(w1[:, kt * 256 + 128:kt * 256 + 256], hT[:, kt * TB:kt * TB + NHALO]) for kt in range(KT)],
               [w1b, hTb], [pb])
            mm(pa[:, 256:256 + NHALO], [(w2[:, kt * 256:kt * 256 + 128], hT[:, kt * TB:kt * TB + NHALO]) for kt in range(KT)],
               [w2b, hTb], [pb])
            t, tb_ = tF.get()
            act(t[:, 0:NHALO], pa[:, 0:NHALO], AF.Copy, [pb], [tb_])
            tt("dve", uhalo[:, ft * NHALO:(ft + 1) * NHALO], t[:, 0:NHALO], pa[:, 256:256 + NHALO], ALU.mult, [tb_, pb], [uhalob])

        def gates(ci, d, slot):
            pa, pb = bank()
            mm(pa[:, 0:16], [(hT[:, kt * TB + ci * 128:kt * TB + (ci + 1) * 128], wg[:, kt * 32 + d * 16:kt * 32 + (d + 1) * 16])
                             for kt in range(KT)], [hTb, wgb], [pb])
            yield
            tt("dve", gsb[:, :], pa[:, 0:16], cst[:, C_BIF + d * 16:C_BIF + (d + 1) * 16], ALU.add, [pb, cstb], [gsbb])
            act(e1[:, :], gsb[:, 8:16], AF.Exp, [gsbb], [e1b], scale=-1.0)
            act(nlf[:, :], e1[:, :], AF.Ln, [e1b], [nlfb], bias=1.0)
            tri = C_TF if d == 0 else C_TB
            pc, pcb = bank()
            mm(pc[:, 0:8], [(cst[:, tri:tri + 128], nlf[:, :])], [nlfb, cstb], [pcb])
            mm(pc[:, 8:16], [(cst[:, C_ONE:C_ONE + 128], nlf[:, :])], [nlfb, cstb], [pcb])
            yield
            tt("dve", ipn[:, :], gsb[:, 0:8], pc[:, 0:8], ALU.add, [gsbb, pcb], [ipnb])
            g0 = slot * 24
            act(GT[:, g0:g0 + 8], ipn[:, :], AF.Exp, [ipnb], [GTb[slot]])
            act(GT[:, g0 + 8:g0 + 16], pc[:, 0:8], AF.Exp, [pcb], [GTb[slot]])
            act(GT[:, g0 + 16:g0 + 24], pc[:, 8:16], AF.Exp, [pcb], [GTb[slot]], scale=-1.0)

        def mk_ctx(s, bi, d, h, order, first_blk, last_blk, full):
            return dict(s=s, bi=bi, d=d, h=h, order=order, first_blk=first_blk, last_blk=last_blk, full=full)

        def proj_a(c):
            h, order, full = c["h"], c["order"], c["full"]
            wq, wqb = load_unit(("q", h))
            wk, wkb = load_unit(("k", h))
            qT, qTb = qTs.get()
            kT, kTb = kTs.get()
            ktk, ktkb = kts.get()
            vx, vxb = vxs.get()
            c.update(qT=qT, qTb=qTb, kT=kT, kTb=kTb, ktk=ktk, ktkb=ktkb, vx=vx, vxb=vxb)
            pa, pb = bank()
            for j in range(2):
                mm(pa[:, j * 256:(j + 1) * 256],
                   [(wq[:, kt * 256 + j * 128:kt * 256 + (j + 1) * 128], hT[:, kt * TB:(kt + 1) * TB]) for kt in range(KT)],
                   [wqb, hTb], [pb])
            act(qT[:, :], pa, AF.Copy, [pb], [qTb])
            yield
            pa, pb = bank()
            for j in range(2):
                mm(pa[:, j * 256:(j + 1) * 256],
                   [(wk[:, kt * 256 + j * 128:kt * 256 + (j + 1) * 128], hT[:, kt * TB:(kt + 1) * TB]) for kt in range(KT)],
                   [wkb, hTb], [pb])
            act(kT[:, :], pa, AF.Copy, [pb], [kTb], scale=float(HD ** -0.5))
            yield
            wv, wvb = load_unit(("v", h))
            pv, pvb = bank()
            for ci in order:
                mm(pv[:, ci * 256:(ci + 1) * 256],
                   [(hT[:, kt * TB + ci * 128:kt * TB + (ci + 1) * 128], wv[:, kt * 256:(kt + 1) * 256]) for kt in range(KT)],
                   [wvb, hTb], [pvb])
            pt, ptb = bankb()
            for ci in order:
                for j in range(2):
                    P.op("pe", lambda E, j=j, ci=ci, pt=pt: E.transpose(out=pt[:, ci * 256 + j * 128:ci * 256 + (j + 1) * 128],
                                                                       in_=kT[:, j * TB + ci * 128:j * TB + (ci + 1) * 128],
                                                                       identity=identb[:, :]),
                         [kTb, identbb], [ptb])
            act(ktk[:, :], pt[:, 0:NCH * 256], AF.Copy, [ptb], [ktkb])
            for oi, ci in enumerate(order):
                g0 = (oi + 1) * 24
                ts("dve", vx[:, ci * VS:ci * VS + 256], pv[:, ci * 256:(ci + 1) * 256], GT[:, g0 + h:g0 + h + 1], None, ALU.mult, None,
                   [pvb, GTb[oi + 1]], [vxb])
                P.op("pool", lambda E, ci=ci, g0=g0: E.tensor_copy(out=vx[:, ci * VS + 256:ci * VS + 257], in_=GT[:, g0 + h:g0 + h + 1]),
                     [GTb[oi + 1]], [vxb])
            yield
            if full:
                wo, wob = load_unit(("o", h))
                wz, wzb = load_unit(("z", h))
                goz, gozb = gozs.get()
                c.update(goz=goz, gozb=gozb)
                for ci in order:
                    poz, pozb = bank()
                    mm(poz[:, 0:256], [(hT[:, kt * TB + ci * 128:kt * TB + (ci + 1) * 128], wo[:, kt * 256:(kt + 1) * 256]) for kt in range(KT)],
                       [wob, hTb], [pozb])
                    mm(poz[:, 256:512], [(hT[:, kt * TB + ci * 128:kt * TB + (ci + 1) * 128], wz[:, kt * 256:(kt + 1) * 256]) for kt in range(KT)],
                       [wzb, hTb], [pozb])
                    toz, tozb = tF.get()
                    act(toz[:, :], poz, AF.Tanh, [pozb], [tozb], scale=0.5)
                    t1, t1b = tF.get()
                    stt("dve", t1[:, 0:256], toz[:, 256:512], 1.0, poz[:, 256:512], ALU.add, ALU.mult, [tozb, pozb], [t1b])
                    stt("dve", goz[:, ci * 256:(ci + 1) * 256], toz[:, 0:256], 1.0, t1[:, 0:256], ALU.add, ALU.mult, [tozb, t1b], [gozb])
                    yield

        def recur_a(c):
            s, bi, d, h, order, full = c["s"], c["bi"], c["d"], c["h"], c["order"], c["full"]
            qT, qTb, kT, kTb, ktk, ktkb, vx, vxb = c["qT"], c["qTb"], c["kT"], c["kTb"], c["ktk"], c["ktkb"], c["vx"], c["vxb"]
            tok0 = seq_off[s] + bi * TB
            tri = C_TF if d == 0 else C_TB
            u0 = h * 2 * VS
            def chunk(oi, ci):
                slot = oi + 1
                g0 = slot * 24
                gp = (slot - 1) * 24
                first = c["first_blk"] and oi == 0
                last = c["last_blk"] and oi == len(order) - 1
                rows = slice(tok0 + ci * 128, tok0 + (ci + 1) * 128)
                key = (s, bi, ci, h)
                if not first:
                    cb_, cbb = cbf.get()
                    act(cb_[:, :], U[:, u0:u0 + 2 * VS], AF.Copy, [Ub[h], GTb[slot - 1]], [cbb], scale=GT[:, gp + 16 + h:gp + 17 + h])
                hd, hdb = tFr.get()
                if full and VARIANT != "v2":
                    dma("sp", hd[:, 256:512], hb_d[rows, h * 256:(h + 1) * 256], [hbb[key]], [hdb])
                pS, pSb = bank()
                mm(pS[:, 0:128], [(kT[:, j * TB + ci * 128:j * TB + (ci + 1) * 128], qT[:, j * TB + ci * 128:j * TB + (ci + 1) * 128])
                                  for j in range(2)], [kTb, qTb], [pSb])
                Pm, Pmb = tHr.get()
                tt("dve", Pm[:, 0:128], pS[:, 0:128], cst[:, tri:tri + 128], ALU.mult, [pSb, cstb], [Pmb])
                if not last:
                    for j in range(2):
                        pC, pCb = bank()
                        mm(pC[:, 0:257], [(ktk[:, ci * 256 + j * 128:ci * 256 + (j + 1) * 128], vx[:, ci * VS:ci * VS + 257])],
                           [ktkb, vxb], [pCb])
                        uo = U[:, u0 + j * VS:u0 + j * VS + 257]
                        if first:
                            P.op("dve", lambda E, uo=uo, pC=pC: E.tensor_copy(out=uo, in_=pC[:, 0:257]), [pCb], [Ub[h]])
                        else:
                            stt("dve", uo, uo, GT[:, gp + 16 + h:gp + 17 + h], pC[:, 0:257], ALU.mult, ALU.add,
                                [pCb, Ub[h], GTb[slot - 1]], [Ub[h]])
                yield
                pN, pNb = bank()
                pairs = [(Pm[:, 0:128], vx[:, ci * VS:ci * VS + 257])]
                rd = [Pmb, vxb]
                if not first:
                    for j in range(2):
                        pairs.append((qT[:, j * TB + ci * 128:j * TB + (ci + 1) * 128], cb_[:, j * VS:j * VS + 257]))
                    rd += [qTb, cbb]
                mm(pN[:, 0:257], pairs, rd, [pNb])
                r0, r0b = smr.get()
                tt("dve", r0[:, 0:1], pN[:, 256:257], GT[:, g0 + 8 + h:g0 + 9 + h], ALU.max, [pNb, GTb[slot]], [r0b])
                r1, r1b = smr.get()
                stt("dve", r1[:, 0:1], pN[:, 256:257], -1.0, r0[:, 0:1], ALU.mult, ALU.max, [pNb, r0b], [r1b])
                r2, r2b = smr.get()
                P.op("dve", lambda E, r1=r1, r2=r2: E.reciprocal(out=r2[:, 0:1], in_=r1[:, 0:1]), [r1b], [r2b])
                act(hd[:, 0:256], pN[:, 0:256], AF.Copy, [pNb, r2b], [hdb], scale=r2[:, 0:1])
                if not full:
                    hbb[key] = Buf("hb")
                    dma("pool", hb_d[rows, h * 256:(h + 1) * 256], hd[:, 0:256], [hdb], [hbb[key]])
                    yield
                    return
                goz, gozb = c["goz"], c["gozb"]
                if VARIANT == "v2":
                    dma("sp", hd[:, 256:512], hb_d[rows, h * 256:(h + 1) * 256], [hbb[key]], [hdb])
                hs, hsb = tFr.get()
                tt("dve", hs[:, 0:256], hd[:, 0:256], hd[:, 256:512], ALU.add, [hdb], [hsb])
                st6, st6b = smr.get()
                P.op("dve", lambda E, st6=st6, hs=hs: E.bn_stats(out=st6[:, 0:6], in_=hs[:, 0:256]), [hsb], [st6b])
                mv, mvb = smr.get()
                P.op("dve", lambda E, st6=st6, mv=mv: E.bn_aggr(out=mv[:, 0:2], in_=st6[:, 0:6]), [st6b], [mvb])
                l1, l1b = smr.get()
                act(l1[:, 0:1], mv[:, 1:2], AF.Ln, [mvb], [l1b], bias=float(EPS))
                rs_, rsb = smr.get()
                act(rs_[:, 0:1], l1[:, 0:1], AF.Exp, [l1b], [rsb], scale=-0.5, bias=float(np.log(0.25)))
                stt("dve", hs[:, 256:512], hs[:, 0:256], mv[:, 0:1], goz[:, ci * 256:(ci + 1) * 256], ALU.subtract, ALU.mult,
                    [hsb, mvb, gozb], [hsb])
                ym, ymb = tHr.get()
                stt("dve", ym[:, 0:256], hs[:, 256:512], rs_[:, 0:1], cst[:, C_ZERO:C_ZERO + 256], ALU.mult, ALU.add,
                    [hsb, rsb, cstb], [ymb])
                yield
                pt, ptb = bankb()
                for j in range(2):
                    P.op("pe", lambda E, j=j, pt=pt, ym=ym: E.transpose(out=pt[:, j * 128:(j + 1) * 128],
                                                                       in_=ym[:, j * 128:(j + 1) * 128], identity=identb[:, :]),
                         [ymb, identbb], [ptb])
                for j in range(2):
                    act(ymT[:, (2 * h + j) * TB + ci * 128:(2 * h + j) * TB + (ci + 1) * 128], pt[:, j * 128:(j + 1) * 128], AF.Copy,
                        [ptb, cstb], [ymTb[h]], scale=(1.0 if VARIANT == "v1" else cst[:, C_MHG + 2 * h + j:C_MHG + 2 * h + j + 1]))
                yield

            gens = [chunk(oi, ci) for oi, ci in enumerate(order)]
            if full and len(gens) == 2:
                for gi in (0, 0, 1, 0, 1, 1):
                    next(gens[gi], None)
                    yield
            else:
                for g in gens:
                    for _ in g:
                        yield

        def phase_b(s, bi):
            for ft in range(16):
                w1, w1b = load_unit(("b1", ft))
                w2, w2b = load_unit(("b2", ft))
                p1, p1b = bank()
                for half in range(2):
                    mm(p1[:, half * 256:(half + 1) * 256],
                       [(w1[:, kt * 256 + half * 128:kt * 256 + (half + 1) * 128], hT[:, kt * TB:(kt + 1) * TB]) for kt in range(KT)],
                       [w1b, hTb], [p1b])
                p2, p2b = bank()
                for half in range(2):
                    mm(p2[:, half * 256:(half + 1) * 256],
                       [(w2[:, kt * 256 + half * 128:kt * 256 + (half + 1) * 128], hT[:, kt * TB:(kt + 1) * TB]) for kt in range(KT)],
                       [w2b, hTb], [p2b])
                if CONV_ENG == "actpool":
                    ccs, ccsb = tF.get()
                    act(ccs[:, 0:TB], p1[:, 256:512], AF.Copy, [p1b], [ccsb])
                    u, ub = tF.get()
                    tt("dve", u[:, 1:TB + 1], ccs[:, 0:TB], p2[:, 0:256], ALU.mult, [ccsb, p2b], [ub])
                    if bi == 0:
                        P.op("pool", lambda E, u=u: E.memset(u[:, 0:1], 0.0), [], [ub])
                    else:
                        P.op("pool", lambda E, u=u, ft=ft: E.tensor_copy(out=u[:, 0:1], in_=ulast[:, ft:ft + 1]), [ulastb[ft]], [ub])
                    if bi == nblk[s] - 1:
                        P.op("pool", lambda E, u=u: E.memset(u[:, TB + 1:TB + 2], 0.0), [], [ub])
                    else:
                        hi = halo_idx[(s, bi)]
                        P.op("pool", lambda E, u=u, ft=ft, hi=hi: E.tensor_copy(out=u[:, TB + 1:TB + 2],
                                                                               in_=uhalo[:, ft * NHALO + hi:ft * NHALO + hi + 1]),
                             [uhalob], [ub])
                    P.op("pool", lambda E, u=u, ft=ft: E.tensor_copy(out=ulast[:, ft:ft + 1], in_=u[:, TB:TB + 1]), [ub], [ulastb[ft]])
                    t3, t3b = tF.get()
                    act(ccs[:, 0:TB], u[:, 0:TB], AF.Copy, [ub, cstb], [ccsb], scale=cst[:, C_CW + ft:C_CW + ft + 1])
                    act(ccs[:, 256:512], u[:, 1:TB + 1], AF.Copy, [ub, cstb], [ccsb], scale=cst[:, C_CW + 16 + ft:C_CW + 17 + ft])
                    act(t3[:, 0:TB], u[:, 2:TB + 2], AF.Copy, [ub, cstb], [t3b], scale=cst[:, C_CW + 32 + ft:C_CW + 33 + ft])
                    tt("pool", ccs[:, 0:TB], ccs[:, 0:TB], ccs[:, 256:512], ALU.add, [ccsb], [ccsb])
                    tt("pool", t3[:, 256:512], ccs[:, 0:TB], t3[:, 0:TB], ALU.add, [ccsb, t3b], [t3b])
                    yfin = t3[:, 256:512]
                    ccsb = t3b
                elif CONV_ENG == "aligned":
                    ccs, ccsb = tF.get()
                    act(ccs[:, 0:TB], p1[:, 256:512], AF.Copy, [p1b], [ccsb])
                    act(ccs[:, 256:256 + TB - 1], p1[:, 257:512], AF.Copy, [p1b], [ccsb])
                    u, ub = tF.get()
                    um, umb = tF.get()
                    tt("dve", u[:, 0:TB], ccs[:, 0:TB], p2[:, 0:TB], ALU.mult, [ccsb, p2b], [ub])
                    tt("dve", u[:, 256:256 + TB - 1], ccs[:, 256:256 + TB - 1], p2[:, 1:TB], ALU.mult, [ccsb, p2b], [ub])
                    tt("dve", um[:, 1:TB], ccs[:, 0:TB - 1], p2[:, 0:TB - 1], ALU.mult, [ccsb, p2b], [umb])
                    if bi == 0:
                        P.op("pool", lambda E, um=um: E.memset(um[:, 0:1], 0.0), [], [umb])
                    else:
                        P.op("pool", lambda E, um=um, ft=ft: E.tensor_copy(out=um[:, 0:1], in_=ulast[:, ft:ft + 1]), [ulastb[ft]], [umb])
                    if bi == nblk[s] - 1:
                        P.op("pool", lambda E, u=u: E.memset(u[:, 256 + TB - 1:256 + TB], 0.0), [], [ub])
                    else:
                        hi = halo_idx[(s, bi)]
                        P.op("pool", lambda E, u=u, ft=ft, hi=hi: E.tensor_copy(out=u[:, 256 + TB - 1:256 + TB],
                                                                               in_=uhalo[:, ft * NHALO + hi:ft * NHALO + hi + 1]),
                             [uhalob], [ub])
                    P.op("pool", lambda E, u=u, ft=ft: E.tensor_copy(out=ulast[:, ft:ft + 1], in_=u[:, TB - 1:TB]), [ub], [ulastb[ft]])
                    stt("dve", ccs[:, 0:TB], um[:, 0:TB], cst[:, C_CW + ft:C_CW + ft + 1], cst[:, C_ZERO:C_ZERO + TB], ALU.mult, ALU.add,
                        [umb, ccsb, ub, cstb], [ccsb])
                    stt("dve", ccs[:, 256:512], u[:, 0:TB], cst[:, C_CW + 16 + ft:C_CW + 17 + ft], ccs[:, 0:TB], ALU.mult, ALU.add,
                        [ub, ccsb, cstb], [ccsb])
                    stt("dve", ccs[:, 0:TB], u[:, 256:512], cst[:, C_CW + 32 + ft:C_CW + 33 + ft], ccs[:, 256:512], ALU.mult, ALU.add,
                        [ub, ccsb, cstb], [ccsb])
                    yfin = ccs[:, 0:TB]
                else:
                    ccs, ccsb = tF.get()
                    act(ccs[:, 0:TB], p1[:, 256:512], AF.Copy, [p1b], [ccsb])
                    u, ub = tF.get()
                    tt("dve", u[:, 1:TB + 1], ccs[:, 0:TB], p2[:, 0:256], ALU.mult, [ccsb, p2b], [ub])
                    if bi == 0:
                        P.op("pool", lambda E, u=u: E.memset(u[:, 0:1], 0.0), [], [ub])
                    else:
                        P.op("pool", lambda E, u=u, ft=ft: E.tensor_copy(out=u[:, 0:1], in_=ulast[:, ft:ft + 1]), [ulastb[ft]], [ub])
                    if bi == nblk[s] - 1:
                        P.op("pool", lambda E, u=u: E.memset(u[:, TB + 1:TB + 2], 0.0), [], [ub])
                    else:
                        hi = halo_idx[(s, bi)]
                        P.op("pool", lambda E, u=u, ft=ft, hi=hi: E.tensor_copy(out=u[:, TB + 1:TB + 2],
                                                                               in_=uhalo[:, ft * NHALO + hi:ft * NHALO + hi + 1]),
                             [uhalob], [ub])
                    P.op("pool", lambda E, u=u, ft=ft: E.tensor_copy(out=ulast[:, ft:ft + 1], in_=u[:, TB:TB + 1]), [ub], [ulastb[ft]])
                    stt(CONV_ENG, ccs[:, 256:512], u[:, 0:TB], cst[:, C_CW + ft:C_CW + ft + 1], cst[:, C_ZERO:C_ZERO + TB], ALU.mult, ALU.add,
                        [ub, cstb], [ccsb])
                    stt(CONV_ENG, ccs[:, 0:256], u[:, 1:TB + 1], cst[:, C_CW + 16 + ft:C_CW + 17 + ft], ccs[:, 256:512], ALU.mult, ALU.add,
                        [ub, ccsb, cstb], [ccsb])
                    stt(CONV_ENG, ccs[:, 256:512], u[:, 2:TB + 2], cst[:, C_CW + 32 + ft:C_CW + 33 + ft], ccs[:, 0:256], ALU.mult, ALU.add,
                        [ub, ccsb, cstb], [ccsb])

                    yfin = ccs[:, 256:512]
                tz, tzb = tF.get()
                act(tz[:, 0:TB], p2[:, 256:512], AF.Tanh, [p2b], [tzb], scale=0.5)
                stt("dve", tz[:, 256:512], tz[:, 0:TB], 1.0, p2[:, 256:512], ALU.add, ALU.mult, [tzb, p2b], [tzb])
                tt("dve", tz[:, 0:256], tz[:, 256:512], p1[:, 0:256], ALU.mult, [tzb, p1b], [tzb])
                stt("dve", ycT[:, ft * TB:(ft + 1) * TB], yfin, 0.5, tz[:, 0:256], ALU.mult, ALU.mult, [ccsb, tzb], [ycTb[ft]])
                yield

        def c_proj(s, bi, h, c):
            km_, km_b = kmh.get()
            dma("sp", km_[:, :], km_d[s][:, 4 * h * NMEM:(4 * h + 4) * NMEM], [], [km_b], extra=km_ops[s])
            vm_, vm_b = vmh.get()
            for mt in range(2):
                dma("sp", vm_[:, mt * HDA:(mt + 1) * HDA], vm_d[s][:, mt * D + h * HDA:mt * D + (h + 1) * HDA], [], [vm_b], extra=vm_ops[s])
            qa, qab = qas.get()
            c.update(km_=km_, km_b=km_b, vm_=vm_, vm_b=vm_b, qa=qa, qab=qab)
            for j2 in range(2):
                w, wb_ = load_unit(("qa", h, j2))
                pa, pb = bank()
                for jj in range(2):
                    mm(pa[:, jj * 256:(jj + 1) * 256],
                       [(w[:, kt * 256 + jj * 128:kt * 256 + (jj + 1) * 128], hT[:, kt * TB:(kt + 1) * TB]) for kt in range(KT)],
                       [wb_, hTb], [pb])
                act(qa[:, j2 * 2 * TB:(j2 + 1) * 2 * TB], pa, AF.Copy, [pb], [qab])
                yield

        def c_attn(s, bi, h, c):
            sc = float(HDA ** -0.5)
            km_, km_b, vm_, vm_b, qa, qab = c["km_"], c["km_b"], c["vm_"], c["vm_b"], c["qa"], c["qab"]
            pT, pTb = pTs.get()
            pns = []
            for ci in range(NCH):
                pS, pSb = bank()
                mm(pS[:, 0:NMEM], [(qa[:, j * TB + ci * 128:j * TB + (ci + 1) * 128], km_[:, j * NMEM:(j + 1) * NMEM]) for j in range(4)],
                   [qab, km_b], [pSb])
                mx, mxb = sm.get()
                P.op("dve", lambda E, mx=mx, pS=pS: E.reduce_max(out=mx[:, 0:1], in_=pS[:, 0:NMEM], axis=mybir.AxisListType.X), [pSb], [mxb])
                nm, nmb = sm.get()
                act(nm[:, 0:1], mx[:, 0:1], AF.Copy, [mxb], [nmb], scale=-sc)
                pe_, peb = tF.get()
                rs_, rsb = sm.get()
                act(pe_[:, 0:NMEM], pS[:, 0:NMEM], AF.Exp, [pSb, nmb], [peb, rsb], scale=sc, bias=nm[:, 0:1], accum=rs_[:, 0:1])
                ri, rib = sm.get()
                P.op("dve", lambda E, ri=ri, rs_=rs_: E.reciprocal(out=ri[:, 0:1], in_=rs_[:, 0:1]), [rsb], [rib])
                pn, pnb = tH.get()
                stt("dve", pn[:, 0:NMEM], pe_[:, 0:NMEM], ri[:, 0:1], cst[:, C_ZERO:C_ZERO + NMEM], ALU.mult, ALU.add, [peb, rib, cstb], [pnb])
                pns.append((pn, pnb))
                yield
            for ci in range(NCH):
                pn, pnb = pns[ci]
                pt, ptb = bankb()
                for mt in range(2):
                    P.op("pe", lambda E, mt=mt, pt=pt, pn=pn: E.transpose(out=pt[:, mt * 128:(mt + 1) * 128],
                                                                         in_=pn[:, mt * 128:(mt + 1) * 128], identity=identb[:, :]),
                         [pnb, identbb], [ptb])
                for mt in range(2):
                    act(pT[:, mt * TB + ci * 128:mt * TB + (ci + 1) * 128], pt[:, mt * 128:(mt + 1) * 128], AF.Copy, [ptb], [pTb])
                yield
            for j2 in range(2):
                w, wb_ = load_unit(("za", h, j2))
                for jj in range(2):
                    j = 2 * j2 + jj
                    poz, pozb = bank()
                    mm(poz[:, 256:512], [(w[:, kt * 256 + jj * 128:kt * 256 + (jj + 1) * 128], hT[:, kt * TB:(kt + 1) * TB]) for kt in range(KT)],
                       [wb_, hTb], [pozb])
                    mm(poz[:, 0:256], [(vm_[:, mt * HDA + j * 128:mt * HDA + (j + 1) * 128], pT[:, mt * TB:(mt + 1) * TB]) for mt in range(2)],
                       [vm_b, pTb], [pozb])
                    tz, tzb = tF.get()
                    act(tz[:, 0:TB], poz[:, 256:512], AF.Tanh, [pozb], [tzb], scale=0.5)
                    stt("dve", tz[:, 256:512], tz[:, 0:TB], 1.0, poz[:, 256:512], ALU.add, ALU.mult, [tzb, pozb], [tzb])
                    ft = 4 * h + j
                    stt("dve", yaT[:, ft * TB:(ft + 1) * TB], poz[:, 0:256], 0.5, tz[:, 256:512], ALU.mult, ALU.mult, [pozb, tzb], [yaTb[h]])
                    yield

        def phase_c(s, bi):
            cc_ = [dict() for _ in range(NHA)]
            if not CPIPE:
                for h in range(NHA):
                    yield from c_proj(s, bi, h, cc_[h])
                    yield from c_attn(s, bi, h, cc_[h])
                return
            yield from c_proj(s, bi, 0, cc_[0])
            for h in range(NHA):
                fg = c_attn(s, bi, h, cc_[h])
                bg = c_proj(s, bi, h + 1, cc_[h + 1]) if h + 1 < NHA else iter(())
                for _ in fg:
                    step(bg, 1)
                    yield
                for _ in bg:
                    yield

        def phase_d(s, bi):
            tok0 = seq_off[s] + bi * TB
            ysrc = [(ymT, ymTb), (ycT, ycTb), (yaT, yaTb)]
            for pr in range(8):
                accs = [None, None]
                for b in range(3):
                    wm_, wmb = load_unit(("mg", pr, b))
                    wb2, wb2b = load_unit(("wb", pr, b))
                    yt_, ybl = ysrc[b]
                    for jj in range(2):
                        nt = 2 * pr + jj
                        pg, pgb = bank()
                        mm(pg[:, 0:256], [(wm_[:, kt * 256 + jj * 128:kt * 256 + (jj + 1) * 128], hT[:, kt * TB:(kt + 1) * TB]) for kt in range(KT)],
                           [wmb, hTb], [pgb])
                        mm(pg[:, 256:512], [(wb2[:, kt * 256 + jj * 128:kt * 256 + (jj + 1) * 128], yt_[:, kt * TB:(kt + 1) * TB]) for kt in range(KT)],
                           [wb2b] + ybl, [pgb])
                        tg, tgb = tF.get()
                        act(tg[:, 0:TB], pg[:, 0:256], AF.Tanh, [pgb], [tgb], scale=0.5)
                        stt("dve", tg[:, 256:512], tg[:, 0:TB], 1.0, pg[:, 256:512], ALU.add, ALU.mult, [tgb, pgb], [tgb])
                        if b == 0:
                            accs[jj] = (tg, tgb)
                        elif b == 1:
                            acc, accb = accs[jj]
                            tt("pool", tg[:, 0:256], acc[:, 256:512], tg[:, 256:512], ALU.add, [accb, tgb], [tgb])
                            accs[jj] = (tg, tgb)
                        else:
                            acc, accb = accs[jj]
                            tt("pool", mgT[:, nt * TB:(nt + 1) * TB], acc[:, 0:256], tg[:, 256:512], ALU.add, [accb, tgb], [mgTb[nt]])
            for u in range(8):
                w, wb_ = load_unit(("out", u))
                pa, pb = bank()
                for ci in range(NCH):
                    mm(pa[:, ci * 256:(ci + 1) * 256],
                       [(mgT[:, kt * TB + ci * 128:kt * TB + (ci + 1) * 128], w[:, kt * 256:(kt + 1) * 256]) for kt in range(KT)],
                       [wb_] + mgTb, [pb])
                for ci in range(NCH):
                    xo = xtok[ci][:, u * 256:(u + 1) * 256]
                    stt("dve", xo, pa[:, ci * 256:(ci + 1) * 256], 0.5, xo, ALU.mult, ALU.add, [pb, xtokb[ci]], [xtokb[ci]])
            for ci in range(NCH):
                act(junk[:, :], xtok[ci][:, :], AF.Square, [xtokb[ci]], [junkb, ssqb], accum=ssq[:, 2 + ci:3 + ci])
                act(lnt[:, 2 + ci:3 + ci], ssq[:, 2 + ci:3 + ci], AF.Ln, [ssqb], [lntb], bias=float(D * EPS))
                act(rstd[:, 2 + ci:3 + ci], lnt[:, 2 + ci:3 + ci], AF.Exp, [lntb], [rstdb], scale=-0.5, bias=float(0.5 * np.log(D)))
                if FIN == "split":
                    for hf in range(2):
                        xs = xtok[ci][:, hf * 1024:(hf + 1) * 1024]
                        stt("pool" if hf else "dve", xs, xs, rstd[:, 2 + ci:3 + ci], cst[:, C_FG + hf * 1024:C_FG + (hf + 1) * 1024],
                            ALU.mult, ALU.mult, [xtokb[ci], rstdb, cstb], [xtokb[ci]])
                else:
                    for hf in range(4):
                        xs = xtok[ci][:, hf * 512:(hf + 1) * 512]
                        stt("dve", xs, xs, rstd[:, 2 + ci:3 + ci], cst[:, C_FG + hf * 512:C_FG + (hf + 1) * 512],
                            ALU.mult, ALU.mult, [xtokb[ci], rstdb, cstb], [xtokb[ci]])
                o = dma("pool", y_out[tok0 + ci * 128:tok0 + (ci + 1) * 128, :], xtok[ci][:, :], [xtokb[ci]], [])
                out_ops.append(o)

        def drain(g):
            for _ in g:
                pass

        def step(g, n=1):
            for _ in range(n):
                try:
                    next(g)
                except StopIteration:
                    return False
            return True

        def interleave(fg, bg, k):
            for _ in fg:
                step(bg, k)
            drain(bg)

        def chain(*gs):
            for g in gs:
                yield from g

        def run_pass(d, full):
            for s in range(NSEQ):
                blocks = list(range(nblk[s]))
                if d == 1:
                    blocks = blocks[::-1]
                order = list(range(NCH)) if d == 0 else list(range(NCH))[::-1]
                for oi_b, bi in enumerate(blocks):
                    tok0 = seq_off[s] + bi * TB
                    norm_block([(x_tok[tok0 + c * 128:tok0 + (c + 1) * 128, :], 128) for c in range(NCH)],
                               x_T[blk_base[s] + bi], TB, C_G)
                    ctx = [mk_ctx(s, bi, d, h, order, oi_b == 0, oi_b == len(blocks) - 1, full) for h in range(NH)]
                    p0 = proj_a(ctx[0])
                    gg = chain(*[gates(ci, d, oi + 1) for oi, ci in enumerate(order)])
                    step(gg, 2)
                    step(p0, 1)
                    step(gg, 2)
                    step(p0, 1)
                    drain(gg)
                    drain(p0)
                    if full:
                        bc = chain(phase_b(s, bi), phase_c(s, bi))
                    for h in range(NH):
                        bgs = []
                        if h + 1 < NH:
                            bgs.append(proj_a(ctx[h + 1]))
                        bg = chain(*bgs)
                        fg = recur_a(ctx[h])
                        if full and not INTERLEAVE_BC:
                            interleave(fg, bg, 1)
                        elif full:
                            for _ in fg:
                                if not step(bg, 1):
                                    step(bc, 2)
                                else:
                                    step(bc, 1)
                            drain(bg)
                        else:
                            interleave(fg, bg, 1)
                    P.op("pool", lambda E: E.tensor_copy(out=GT[:, 16:24], in_=GT[:, NCH * 24 + 16:NCH * 24 + 24]),
                         [GTb[NCH]], [GTb[0]])
                    if full:
                        chk("p2a")
                        drain(bc)
                        chk("p2bc")
                        phase_d(s, bi)
                        chk("p2d")

        try:
            chk("cast")
            for s in range(NSEQ):
                kv_prologue(s)
            chk("kv")
            halo_prepass()
            chk("halo")
            run_pass(1, False)
            chk("pass1")
            run_pass(0, True)
        except _Stop:
            pass
        if stop is not None:
            out_ops = [o for e in P.ENG for o in P.ops[e] if o.dma]
        P.op("sp", [], [], [], extra=out_ops)
        P.emit(nc, stack)
    return nc


def _tile_T(a):
    T = a.shape[0]
    return np.ascontiguousarray(a.T.reshape(KT, 128, T).transpose(1, 0, 2)).reshape(128, KT * T)


def prepare_shared(norm_g, w_in, b_if, conv_w, mem_norm_g, w_kv_mem, mh_norm_g, w_branch, w_out, final_norm_g):
    units, uidx = unit_catalog()
    srcs = {"in": w_in[0], "kv": w_kv_mem[0], "br0": w_branch[0, 0], "br1": w_branch[0, 1], "br2": w_branch[0, 2], "out": w_out[0]}
    ws32 = np.empty((len(units), 128, KT * 256), np.float32)
    for i, (src, cols) in enumerate(units):
        blk = srcs[src][:, cols]
        ws32[i] = blk.reshape(KT, 128, 256).transpose(1, 0, 2).reshape(128, KT * 256)
    perm = np.concatenate([np.arange(0, 8), np.arange(16, 24), np.arange(8, 16), np.arange(24, 32)])
    wgc = w_in[0][:, 10240 + perm]
    wg32 = np.ascontiguousarray(wgc.reshape(KT, 128, 32).transpose(1, 0, 2)).reshape(128, KT * 32)
    cst = np.zeros((128, C_END), np.float32)
    cst[:, C_ID:C_ID + 128] = np.eye(128, dtype=np.float32)
    cst[:, C_ONE:C_ONE + 128] = 1.0
    ii = np.arange(128)
    cst[:, C_TF:C_TF + 128] = (ii[:, None] <= ii[None, :]).astype(np.float32)
    cst[:, C_TB:C_TB + 128] = (ii[:, None] >= ii[None, :]).astype(np.float32)
    cst[:, C_G:C_G + 16] = norm_g[0].reshape(KT, 128).T
    cst[:, C_GM:C_GM + 16] = mem_norm_g[0].reshape(KT, 128).T
    for j in range(3):
        cst[:, C_CW + 16 * j:C_CW + 16 * (j + 1)] = conv_w[0, j].reshape(KT, 128).T
    cst[:, C_BIF:C_BIF + 32] = b_if[0][perm][None, :]
    cst[:, C_MHG:C_MHG + 16] = mh_norm_g[0].reshape(KT, 128).T
    cst[:, C_FG:C_FG + D] = final_norm_g[None, :]
    return ws32, wg32, cst


def prepare_core(seqs, mems):
    x_tok = np.ascontiguousarray(np.concatenate(seqs, axis=0))
    blocks = []
    halos = []
    for x in seqs:
        nb = x.shape[0] // TB
        for bi in range(nb):
            blocks.append(_tile_T(x[bi * TB:(bi + 1) * TB]))
            if bi < nb - 1:
                halos.append(x[(bi + 1) * TB])
    x_T = np.stack(blocks, axis=0)
    h_tok = np.zeros((32, D), np.float32)
    if halos:
        h_tok[:len(halos)] = np.stack(halos, axis=0)
    h_tok[len(halos):] = 1.0
    h_T = _tile_T(h_tok)
    m_tok = np.ascontiguousarray(np.concatenate(mems, axis=0))
    m_T = np.stack([_tile_T(m) for m in mems], axis=0)
    return {"x_tok": x_tok, "x_T": x_T, "m_tok": m_tok, "m_T": m_T, "h_tok": h_tok, "h_T": h_T}


def kernel(x_prompt, x_sample, mem_prompt, mem_sample, norm_g, w_in, b_if, conv_w, mem_norm_g,
           w_kv_mem, mh_norm_g, w_branch, w_out, final_norm_g):
    f = lambda a: np.asarray(a, dtype=np.float32)
    x_prompt, x_sample, mem_prompt, mem_sample = f(x_prompt), f(x_sample), f(mem_prompt), f(mem_sample)
    ws32, wg32, cst = prepare_shared(f(norm_g), f(w_in), f(b_if), f(conv_w), f(mem_norm_g), f(w_kv_mem), f(mh_norm_g),
                                     f(w_branch), f(w_out), f(final_norm_g))
    n = 8
    SP, SS = x_prompt.shape[1], x_sample.shape[1]
    nc = build_program([SP, SS, SS])
    in_maps = []
    for i in range(n):
        m = prepare_core([x_prompt[i], x_sample[2 * i], x_sample[2 * i + 1]],
                         [mem_prompt[i], mem_sample[2 * i], mem_sample[2 * i + 1]])
        m.update({"ws32": ws32, "wg32": wg32, "cst_in": cst})
        in_maps.append(m)
    res = run_bass_kernel_spmd(nc, in_maps, core_ids=list(range(n)))
    y_prompt = np.empty_like(x_prompt)
    y_sample = np.empty_like(x_sample)
    for i in range(n):
        y = res.results[i]["y"]
        y_prompt[i] = y[0:SP]
        y_sample[2 * i] = y[SP:SP + SS]
        y_sample[2 * i + 1] = y[SP + SS:SP + 2 * SS]
    return (y_prompt, y_sample)
```

```python
import contextlib
import numpy as np
import concourse.bass as bass
import concourse.mybir as mybir
from concourse.bass_utils import run_bass_kernel_spmd

F32 = mybir.dt.float32
BF16 = mybir.dt.bfloat16
AF = mybir.ActivationFunctionType
ALU = mybir.AluOpType

D = 2048
KT = 16
TB = 256
NCH = TB // 128
NH = 8
HD = 256
NHA = 4
HDA = 512
NMEM = 256
EPS = 1e-6
VS = 264
NSLOT = 5
import os
VARIANT = os.environ.get("KVARIANT", "")
INTERLEAVE_BC = os.environ.get("KBC", "0") == "1"
CPIPE = os.environ.get("KCPIPE", "1") == "1"
DIAG_ENG = os.environ.get("KDIAG", "dve")
FIN = os.environ.get("KFIN", "dve4")
CONV_ENG = os.environ.get("KCONV", "actpool")
SQD = float(np.sqrt(D))

C_ID, C_ONE, C_TF, C_TB, C_G, C_GM, C_CW, C_BIF, C_MHG, C_FG = 0, 128, 256, 384, 512, 528, 544, 592, 624, 640
C_ZERO = 2688
C_END = 2944


def unit_catalog():
    units = []
    idx = {}

    def add(key, src, cols):
        idx[key] = len(units)
        units.append((src, np.asarray(cols, dtype=np.int64)))

    r = np.arange
    OQ, OK_, OV, OO, OZ = 0, 2048, 4096, 6144, 8192
    OCB, OCC, OCX, OZC = 10272, 12320, 14368, 16416
    OQA, OZA, OMG = 18464, 20512, 22560
    for h in range(NH):
        add(("q", h), "in", OQ + h * 256 + r(256))
        add(("k", h), "in", OK_ + h * 256 + r(256))
        add(("v", h), "in", OV + h * 256 + r(256))
    for u in range(8):
        add(("kk", u), "kv", u * 256 + r(256))
    for u in range(8):
        add(("kv", u), "kv", 2048 + u * 256 + r(256))
    for ft in range(16):
        add(("b1", ft), "in", np.concatenate([OCB + ft * 128 + r(128), OCC + ft * 128 + r(128)]))
        add(("b2", ft), "in", np.concatenate([OCX + ft * 128 + r(128), OZC + ft * 128 + r(128)]))
    for h in range(NH):
        add(("o", h), "in", OO + h * 256 + r(256))
        add(("z", h), "in", OZ + h * 256 + r(256))
    for h in range(NHA):
        for j in range(2):
            add(("qa", h, j), "in", OQA + h * 512 + j * 256 + r(256))
        for j in range(2):
            add(("za", h, j), "in", OZA + h * 512 + j * 256 + r(256))
    for pr in range(8):
        for b in range(3):
            add(("mg", pr, b), "in", OMG + b * 2048 + pr * 256 + r(256))
        for b in range(3):
            add(("wb", pr, b), "br%d" % b, pr * 256 + r(256))
    for u in range(8):
        add(("out", u), "out", u * 256 + r(256))
    return units, idx


class Buf:
    __slots__ = ("name", "last_w", "readers", "const")

    def __init__(self, name, const=False):
        self.name = name
        self.last_w = None
        self.readers = []
        self.const = const


class Op:
    __slots__ = ("eng", "fns", "deps", "signal", "ev", "dma", "pre")

    def __init__(self, eng, fns, dma):
        self.eng = eng
        self.fns = fns
        self.dma = dma
        self.deps = []
        self.signal = False
        self.ev = None
        self.pre = None


class Prog:
    ENG = ["pe", "act", "dve", "pool", "sp"]

    def __init__(self):
        self.ops = {e: [] for e in self.ENG}
        self.nops = 0

    def op(self, eng, fns, reads=(), writes=(), dma=False, extra=()):
        if not isinstance(fns, (list, tuple)):
            fns = [fns]
        o = Op(eng, list(fns), dma)
        deps = {}
        for b in reads:
            if b.last_w is not None:
                deps[id(b.last_w)] = b.last_w
        for b in writes:
            if b.last_w is not None:
                deps[id(b.last_w)] = b.last_w
            for rd in b.readers:
                deps[id(rd)] = rd
        for x in extra:
            deps[id(x)] = x
        deps.pop(id(o), None)
        for d in deps.values():
            if eng == "pe" and d.eng == "pe" and not d.dma:
                continue
            d.signal = True
            o.deps.append(d)
        for b in reads:
            if not b.const:
                b.readers.append(o)
        for b in writes:
            b.last_w = o
            b.readers = []
        self.ops[eng].append(o)
        self.nops += 1
        return o

    def emit(self, nc, stack):
        esem = {e: stack.enter_context(nc.semaphore("es_" + e)) for e in ["pe", "act", "dve", "pool"]}
        npool = {"sp": int(os.environ.get("KSPQ", "4")), "pool": 4, "act": 4}
        dsem = {e: [stack.enter_context(nc.semaphore("ds_%s%d" % (e, i))) for i in range(n)] for e, n in npool.items()}
        for e, lst in self.ops.items():
            cnt = 0
            rr = 0
            use = [0] * npool.get(e, 0)
            for o in lst:
                if o.dma:
                    k = rr % len(use)
                    rr += 1
                    o.pre = (dsem[e][k], use[k] * 16)
                    use[k] += 1
                    o.ev = (dsem[e][k], use[k] * 16)
                elif o.signal:
                    cnt += 1
                    o.ev = (esem[e], cnt)
        block = stack.enter_context(nc.Block())
        ops = self.ops

        def make(e_name):
            def body(E):
                waited = {}
                for o in ops[e_name]:
                    need = [d.ev for d in o.deps]
                    if o.dma and o.pre[1] > 0:
                        need.append(o.pre)
                    for (s, v) in need:
                        k = id(s)
                        if waited.get(k, 0) < v:
                            E.wait_ge(s, v)
                            waited[k] = v
                    ins = None
                    for fn in o.fns:
                        ins = fn(E)
                    if o.dma:
                        ins.then_inc(o.ev[0], 16)
                    elif o.signal:
                        ins.then_inc(o.ev[0], 1)
            return body

        block.tensor(make("pe"))
        block.scalar(make("act"))
        block.vector(make("dve"))
        block.gpsimd(make("pool"))
        block.sync(make("sp"))


class Rot:
    def __init__(self, items):
        self.items = items
        self.i = 0

    def get(self):
        it = self.items[self.i % len(self.items)]
        self.i += 1
        return it


class _Stop(Exception):
    pass


def build_program(seq_lens, stop=None):
    units, uidx = unit_catalog()
    NU = len(units)
    NSEQ = len(seq_lens)
    NTOK = int(sum(seq_lens))
    seq_off = [int(sum(seq_lens[:i])) for i in range(NSEQ)]
    nblk = [L // TB for L in seq_lens]
    blk_base = [int(sum(nblk[:i])) for i in range(NSEQ)]
    NBLK = int(sum(nblk))
    halo_idx = {}
    for s in range(NSEQ):
        for bi in range(nblk[s] - 1):
            halo_idx[(s, bi)] = len(halo_idx)
    NHALO = 32
    assert len(halo_idx) <= NHALO

    nc = bass.Bass("TRN2", target_bir_lowering=False)
    dt_ = nc.dram_tensor
    x_tok = dt_("x_tok", [NTOK, D], F32, kind="ExternalInput").ap()
    x_T = dt_("x_T", [NBLK, 128, KT * TB], F32, kind="ExternalInput").ap()
    m_tok = dt_("m_tok", [NSEQ * NMEM, D], F32, kind="ExternalInput").ap()
    m_T = dt_("m_T", [NSEQ, 128, KT * NMEM], F32, kind="ExternalInput").ap()
    h_tok = dt_("h_tok", [NHALO, D], F32, kind="ExternalInput").ap()
    h_T = dt_("h_T", [128, KT * NHALO], F32, kind="ExternalInput").ap()
    ws32 = dt_("ws32", [NU, 128, KT * 256], F32, kind="ExternalInput").ap()
    wg32 = dt_("wg32", [128, KT * 32], F32, kind="ExternalInput").ap()
    cst_d = dt_("cst_in", [128, C_END], F32, kind="ExternalInput").ap()
    y_out = dt_("y", [NTOK, D], F32, kind="ExternalOutput").ap()
    ws16 = dt_("ws16", [NU, 128, KT * 256], BF16, kind="Internal").ap()
    hb_d = dt_("hb", [NTOK, D], F32, kind="Internal").ap()
    km_d = dt_("km", [NSEQ, 128, KT * NMEM], BF16, kind="Internal").ap()
    vm_d = dt_("vm", [NSEQ, 128, 2 * D], BF16, kind="Internal").ap()

    P = Prog()
    stack = contextlib.ExitStack()
    with stack:
        def sb(name, shape, dtype):
            return stack.enter_context(nc.sbuf_tensor(name, shape, dtype))

        cst = sb("cst", [128, C_END], F32)
        cstb = Buf("cst", const=True)
        identb = sb("identb", [128, 128], BF16)
        identbb = Buf("identb", const=True)
        wg = sb("wg", [128, KT * 32], BF16)
        wgb = Buf("wg", const=True)
        xtok = [sb("xtok%d" % c, [128, D], F32) for c in range(NCH)]
        xtokb = [Buf("xtok%d" % c) for c in range(NCH)]
        xts = Rot([(sb("xts%d" % i, [128, 4 * TB], F32), Buf("xts%d" % i)) for i in range(2)])
        hT = sb("hT", [128, KT * TB], BF16)
        hTb = Buf("hT")
        junk = sb("junk", [128, D], BF16)
        junkb = Buf("junk")
        ssq = sb("ssq", [128, 4], F32)
        ssqb = Buf("ssq")
        rstd = sb("rstd", [128, 4], F32)
        rstdb = Buf("rstd")
        lnt = sb("lnt", [128, 4], F32)
        lntb = Buf("lnt")
        diag = sb("diag", [128, 256], F32)
        diagb = Buf("diag")
        rbc = sb("rbc", [128, TB], F32)
        rbcb = Buf("rbc")
        ymT = sb("ymT", [128, KT * TB], BF16)
        ymTb = [Buf("ymT%d" % h) for h in range(NH)]
        ycT = sb("ycT", [128, KT * TB], BF16)
        ycTb = [Buf("ycT%d" % i) for i in range(KT)]
        yaT = sb("yaT", [128, KT * TB], BF16)
        yaTb = [Buf("yaT%d" % h) for h in range(NHA)]
        mgT = sb("mgT", [128, KT * TB], BF16)
        mgTb = [Buf("mgT%d" % i) for i in range(KT)]
        U = sb("U", [128, NH * 2 * VS], F32)
        Ub = [Buf("U%d" % h) for h in range(NH)]
        cbf = Rot([(sb("cbf%d" % i, [128, 2 * VS], BF16), Buf("cbf%d" % i)) for i in range(2)])
        GT = sb("GT", [128, (NCH + 1) * 24], F32)
        GTb = [Buf("GT%d" % i) for i in range(NCH + 1)]
        gsb = sb("gsb", [128, 16], F32)
        gsbb = Buf("gsb")
        e1 = sb("e1", [128, 8], F32)
        e1b = Buf("e1")
        nlf = sb("nlf", [128, 8], F32)
        nlfb = Buf("nlf")
        ipn = sb("ipn", [128, 8], F32)
        ipnb = Buf("ipn")
        uhalo = sb("uhalo", [128, KT * NHALO], F32)
        uhalob = Buf("uhalo")
        ulast = sb("ulast", [128, KT], F32)
        ulastb = [Buf("ulast%d" % i) for i in range(KT)]
        sm = Rot([(sb("sm%d" % i, [128, 8], F32), Buf("sm%d" % i)) for i in range(12)])
        qTs = Rot([(sb("qT%d" % i, [128, 2 * TB], BF16), Buf("qT%d" % i)) for i in range(2)])
        kTs = Rot([(sb("kT%d" % i, [128, 2 * TB], BF16), Buf("kT%d" % i)) for i in range(2)])
        kts = Rot([(sb("ktok%d" % i, [128, NCH * 256], BF16), Buf("ktok%d" % i)) for i in range(2)])
        vxs = Rot([(sb("vext%d" % i, [128, NCH * VS], BF16), Buf("vext%d" % i)) for i in range(2)])
        gozs = Rot([(sb("goz%d" % i, [128, NCH * 256], F32), Buf("goz%d" % i)) for i in range(2)])
        qas = Rot([(sb("qaT%d" % i, [128, 4 * TB], BF16), Buf("qaT%d" % i)) for i in range(2)])
        pTs = Rot([(sb("pT%d" % i, [128, 2 * TB], BF16), Buf("pT%d" % i)) for i in range(2)])
        kmh = Rot([(sb("kmh%d" % i, [128, 4 * NMEM], BF16), Buf("kmh%d" % i)) for i in range(2)])
        vmh = Rot([(sb("vmh%d" % i, [128, 2 * HDA], BF16), Buf("vmh%d" % i)) for i in range(2)])
        tF = Rot([(sb("tF%d" % i, [128, 512], F32), Buf("tF%d" % i)) for i in range(7)])
        tH = Rot([(sb("tH%d" % i, [128, 512], BF16), Buf("tH%d" % i)) for i in range(3)])
        tFr = Rot([(sb("tFr%d" % i, [128, 512], F32), Buf("tFr%d" % i)) for i in range(4)])
        tHr = Rot([(sb("tHr%d" % i, [128, 256], BF16), Buf("tHr%d" % i)) for i in range(4)])
        smr = Rot([(sb("smr%d" % i, [128, 8], F32), Buf("smr%d" % i)) for i in range(16)])
        wr = [sb("wr%d" % i, [128, KT * 256], BF16) for i in range(NSLOT)]
        wrb = [Buf("wr%d" % i) for i in range(NSLOT)]
        wring = Rot(list(zip(wr, wrb)))
        NPF = 6
        psf = stack.enter_context(nc.psum_tensor("psf", [128, NPF * 512], F32))
        psfb = [Buf("psf%d" % i) for i in range(NPF)]
        psb = stack.enter_context(nc.psum_tensor("psb", [128, 2 * 1024], BF16))
        psbb = [Buf("psb%d" % i) for i in range(2)]
        ps_i = [0]
        psb_i = [0]

        def bank():
            i = ps_i[0] % NPF
            ps_i[0] += 1
            return psf[:, i * 512:(i + 1) * 512], psfb[i]

        def bankb():
            i = psb_i[0] % 2
            psb_i[0] += 1
            return psb[:, i * 1024:(i + 1) * 1024], psbb[i]

        wsb = [Buf("ws%d" % u, const=True) for u in range(NU)]
        hbb = {}
        km_ops = [[] for _ in range(NSEQ)]
        vm_ops = [[] for _ in range(NSEQ)]
        out_ops = []

        def cc(a, b):
            return cst[:, a:b]

        def dma(eng, out, in_, reads, writes, extra=()):
            return P.op(eng, lambda E: E.dma_start(out=out, in_=in_), reads, writes, dma=True, extra=extra)

        def mm(out, pairs, reads, writes):
            n = len(pairs)
            fns = [(lambda E, l=l, r=r, i=i: E.matmul(out, lhsT=l, rhs=r, start=(i == 0), stop=(i == n - 1)))
                   for i, (l, r) in enumerate(pairs)]
            return P.op("pe", fns, reads, writes)

        def act(out, in_, func, reads, writes, scale=1.0, bias=0.0, accum=None):
            if accum is None:
                return P.op("act", lambda E: E.activation(out=out, in_=in_, func=func, bias=bias, scale=scale), reads, writes)
            return P.op("act", lambda E: E.activation(out=out, in_=in_, func=func, bias=bias, scale=scale, accum_out=accum), reads, writes)

        def tt(eng, out, a, b, op, reads, writes):
            return P.op(eng, lambda E: E.tensor_tensor(out=out, in0=a, in1=b, op=op), reads, writes)

        def ts(eng, out, a, s1, s2, op0, op1, reads, writes):
            if s2 is None:
                return P.op(eng, lambda E: E.tensor_scalar(out=out, in0=a, scalar1=s1, scalar2=None, op0=op0), reads, writes)
            return P.op(eng, lambda E: E.tensor_scalar(out=out, in0=a, scalar1=s1, scalar2=s2, op0=op0, op1=op1), reads, writes)

        def stt(eng, out, a, s, b, op0, op1, reads, writes):
            if eng == "pool":
                P.op(eng, lambda E: E.tensor_scalar(out=out, in0=a, scalar1=s, scalar2=None, op0=op0), reads, writes)
                return P.op(eng, lambda E: E.tensor_tensor(out=out, in0=out, in1=b, op=op1), list(reads) + list(writes), writes)
            return P.op(eng, lambda E: E.scalar_tensor_tensor(out=out, in0=a, scalar=s, in1=b, op0=op0, op1=op1), reads, writes)

        def load_unit(key):
            u = uidx[key]
            w, b = wring.get()
            dma("sp", w[:, :], ws16[u], [wsb[u]], [b])
            return w, b

        def chk(tag):
            if stop == tag:
                raise _Stop()

        dma("sp", cst[:, :], cst_d[:, :], [], [cstb])
        act(identb[:, :], cst[:, C_ID:C_ID + 128], AF.Copy, [cstb], [identbb])
        st0, st0b = xts.get()
        dma("sp", st0[:, 0:KT * 32], wg32[:, :], [], [st0b])
        act(wg[:, :], st0[:, 0:KT * 32], AF.Copy, [st0b], [wgb])
        P.op("pool", lambda E: E.memset(U[:, :], 0.0), [], Ub)
        for u in range(NU):
            dma("pool", ws16[u], ws32[u], [], [wsb[u]])

        def norm_block(tok_aps, xT_ap, nT, gcol):
            pa, pb = bank()
            for c, (tap, rows) in enumerate(tok_aps):
                dma("sp", xtok[c][0:rows, :], tap, [], [xtokb[c]])
                act(junk[0:rows, :], xtok[c][0:rows, :], AF.Square, [xtokb[c]], [junkb, ssqb], accum=ssq[0:rows, c:c + 1])
                act(lnt[0:rows, c:c + 1], ssq[0:rows, c:c + 1], AF.Ln, [ssqb], [lntb], bias=float(D * EPS))
                act(rstd[0:rows, c:c + 1], lnt[0:rows, c:c + 1], AF.Exp, [lntb], [rstdb], scale=-0.5)
                stt("dve", diag[0:rows, c * 128:c * 128 + rows], cst[0:rows, C_ID:C_ID + rows], rstd[0:rows, c:c + 1],
                    cst[0:rows, C_ZERO:C_ZERO + rows], ALU.mult, ALU.add, [rstdb, cstb], [diagb])
            for c, (tap, rows) in enumerate(tok_aps):
                mm(pa[:, c * 128:c * 128 + rows], [(cst[0:rows, C_ONE:C_ONE + 128], diag[0:rows, c * 128:c * 128 + rows])], [diagb, cstb], [pb])
            act(rbc[:, 0:nT], pa[:, 0:nT], AF.Copy, [pb], [rbcb], scale=SQD)
            chk("norm1")
            for q in range(4):
                st, stb = xts.get()
                if nT == TB:
                    dma("sp", st[:, :], xT_ap[:, q * 4 * nT:(q + 1) * 4 * nT], [], [stb])
                else:
                    dma("sp", st[:, 0:4 * nT], xT_ap[:, q * 4 * nT:(q + 1) * 4 * nT], [], [stb])
                for j in range(4):
                    kt = q * 4 + j
                    stt("dve", hT[:, kt * TB:kt * TB + nT], st[:, j * nT:(j + 1) * nT], cst[:, gcol + kt:gcol + kt + 1],
                        rbc[:, 0:nT], ALU.mult, ALU.mult, [stb, rbcb, cstb], [hTb])

        def kv_prologue(s):
          if True:
            norm_block([(m_tok[s * NMEM + c * 128:s * NMEM + (c + 1) * 128, :], 128) for c in range(2)], m_T[s], NMEM, C_GM)
            chk("norm")
            for u in range(8):
                w, wb_ = load_unit(("kk", u))
                pa, pb = bank()
                for j in range(2):
                    mm(pa[:, j * 256:(j + 1) * 256],
                       [(w[:, kt * 256 + j * 128:kt * 256 + (j + 1) * 128], hT[:, kt * TB:(kt + 1) * TB]) for kt in range(KT)],
                       [wb_, hTb], [pb])
                t, tb_ = tH.get()
                act(t[:, :], pa, AF.Copy, [pb], [tb_])
                km_ops[s].append(dma("pool", km_d[s][:, 2 * u * NMEM:(2 * u + 2) * NMEM], t[:, :], [tb_], []))
            chk("km")
            for u in range(8):
                w, wb_ = load_unit(("kv", u))
                pa, pb = bank()
                for mt in range(2):
                    mm(pa[:, mt * 256:(mt + 1) * 256],
                       [(hT[:, kt * TB + mt * 128:kt * TB + (mt + 1) * 128], w[:, kt * 256:(kt + 1) * 256]) for kt in range(KT)],
                       [wb_, hTb], [pb])
                t, tb_ = tH.get()
                act(t[:, :], pa, AF.Copy, [pb], [tb_])
                for mt in range(2):
                    vm_ops[s].append(dma("pool", vm_d[s][:, mt * D + u * 256:mt * D + (u + 1) * 256], t[:, mt * 256:(mt + 1) * 256], [tb_], []))

        def halo_prepass():
          norm_block([(h_tok[:, :], NHALO)], h_T, NHALO, C_G)
          for ft in range(16):
            w1, w1b = load_unit(("b1", ft))
            w2, w2b = load_unit(("b2", ft))
            pa, pb = bank()
            mm(pa[:, 0:NHALO], [(w1[:, kt * 256 + 128:kt * 256 + 256], hT[:, kt * TB:kt * TB + NHALO]) for kt in range(KT)],
               [w1b, hTb], [pb])
            mm(pa[:, 256:256 + NHALO], [(w2[:, kt * 256:kt * 256 + 128], hT[:, kt * TB:kt * TB + NHALO]) for kt in range(KT)],
               [w2b, hTb], [pb])
            t, tb_ = tF.get()
            act(t[:, 0:NHALO], pa[:, 0:NHALO], AF.Copy, [pb], [tb_])
            tt("dve", uhalo[:, ft * NHALO:(ft + 1) * NHALO], t[:, 0:NHALO], pa[:, 256:256 + NHALO], ALU.mult, [tb_, pb], [uhalob])

        def gates(ci, d, slot):
            pa, pb = bank()
            mm(pa[:, 0:16], [(hT[:, kt * TB + ci * 128:kt * TB + (ci + 1) * 128], wg[:, kt * 32 + d * 16:kt * 32 + (d + 1) * 16])
                             for kt in range(KT)], [hTb, wgb], [pb])
            yield
            tt("dve", gsb[:, :], pa[:, 0:16], cst[:, C_BIF + d * 16:C_BIF + (d + 1) * 16], ALU.add, [pb, cstb], [gsbb])
            act(e1[:, :], gsb[:, 8:16], AF.Exp, [gsbb], [e1b], scale=-1.0)
            act(nlf[:, :], e1[:, :], AF.Ln, [e1b], [nlfb], bias=1.0)
            tri = C_TF if d == 0 else C_TB
            pc, pcb = bank()
            mm(pc[:, 0:8], [(cst[:, tri:tri + 128], nlf[:, :])], [nlfb, cstb], [pcb])
            mm(pc[:, 8:16], [(cst[:, C_ONE:C_ONE + 128], nlf[:, :])], [nlfb, cstb], [pcb])
            yield
            tt("dve", ipn[:, :], gsb[:, 0:8], pc[:, 0:8], ALU.add, [gsbb, pcb], [ipnb])
            g0 = slot * 24
            act(GT[:, g0:g0 + 8], ipn[:, :], AF.Exp, [ipnb], [GTb[slot]])
            act(GT[:, g0 + 8:g0 + 16], pc[:, 0:8], AF.Exp, [pcb], [GTb[slot]])
            act(GT[:, g0 + 16:g0 + 24], pc[:, 8:16], AF.Exp, [pcb], [GTb[slot]], scale=-1.0)

        def mk_ctx(s, bi, d, h, order, first_blk, last_blk, full):
            return dict(s=s, bi=bi, d=d, h=h, order=order, first_blk=first_blk, last_blk=last_blk, full=full)

        def proj_a(c):
            h, order, full = c["h"], c["order"], c["full"]
            wq, wqb = load_unit(("q", h))
            wk, wkb = load_unit(("k", h))
            qT, qTb = qTs.get()
            kT, kTb = kTs.get()
            ktk, ktkb = kts.get()
            vx, vxb = vxs.get()
            c.update(qT=qT, qTb=qTb, kT=kT, kTb=kTb, ktk=ktk, ktkb=ktkb, vx=vx, vxb=vxb)
            pa, pb = bank()
            for j in range(2):
                mm(pa[:, j * 256:(j + 1) * 256],
                   [(wq[:, kt * 256 + j * 128:kt * 256 + (j + 1) * 128], hT[:, kt * TB:(kt + 1) * TB]) for kt in range(KT)],
                   [wqb, hTb], [pb])
            act(qT[:, :], pa, AF.Copy, [pb], [qTb])
            yield
            pa, pb = bank()
            for j in range(2):
                mm(pa[:, j * 256:(j + 1) * 256],
                   [(wk[:, kt * 256 + j * 128:kt * 256 + (j + 1) * 128], hT[:, kt * TB:(kt + 1) * TB]) for kt in range(KT)],
                   [wkb, hTb], [pb])
            act(kT[:, :], pa, AF.Copy, [pb], [kTb], scale=float(HD ** -0.5))
            yield
            wv, wvb = load_unit(("v", h))
            pv, pvb = bank()
            for ci in order:
                mm(pv[:, ci * 256:(ci + 1) * 256],
                   [(hT[:, kt * TB + ci * 128:kt * TB + (ci + 1) * 128], wv[:, kt * 256:(kt + 1) * 256]) for kt in range(KT)],
                   [wvb, hTb], [pvb])
            pt, ptb = bankb()
            for ci in order:
                for j in range(2):
                    P.op("pe", lambda E, j=j, ci=ci, pt=pt: E.transpose(out=pt[:, ci * 256 + j * 128:ci * 256 + (j + 1) * 128],
                                                                       in_=kT[:, j * TB + ci * 128:j * TB + (ci + 1) * 128],
                                                                       identity=identb[:, :]),
                         [kTb, identbb], [ptb])
            act(ktk[:, :], pt[:, 0:NCH * 256], AF.Copy, [ptb], [ktkb])
            for oi, ci in enumerate(order):
                g0 = (oi + 1) * 24
                ts("dve", vx[:, ci * VS:ci * VS + 256], pv[:, ci * 256:(ci + 1) * 256], GT[:, g0 + h:g0 + h + 1], None, ALU.mult, None,
                   [pvb, GTb[oi + 1]], [vxb])
                P.op("pool", lambda E, ci=ci, g0=g0: E.tensor_copy(out=vx[:, ci * VS + 256:ci * VS + 257], in_=GT[:, g0 + h:g0 + h + 1]),
                     [GTb[oi + 1]], [vxb])
            yield
            if full:
                wo, wob = load_unit(("o", h))
                wz, wzb = load_unit(("z", h))
                goz, gozb = gozs.get()
                c.update(goz=goz, gozb=gozb)
                for ci in order:
                    poz, pozb = bank()
                    mm(poz[:, 0:256], [(hT[:, kt * TB + ci * 128:kt * TB + (ci + 1) * 128], wo[:, kt * 256:(kt + 1) * 256]) for kt in range(KT)],
                       [wob, hTb], [pozb])
                    mm(poz[:, 256:512], [(hT[:, kt * TB + ci * 128:kt * TB + (ci + 1) * 128], wz[:, kt * 256:(kt + 1) * 256]) for kt in range(KT)],
                       [wzb, hTb], [pozb])
                    toz, tozb = tF.get()
                    act(toz[:, :], poz, AF.Tanh, [pozb], [tozb], scale=0.5)
                    t1, t1b = tF.get()
                    stt("dve", t1[:, 0:256], toz[:, 256:512], 1.0, poz[:, 256:512], ALU.add, ALU.mult, [tozb, pozb], [t1b])
                    stt("dve", goz[:, ci * 256:(ci + 1) * 256], toz[:, 0:256], 1.0, t1[:, 0:256], ALU.add, ALU.mult, [tozb, t1b], [gozb])
                    yield

        def recur_a(c):
            s, bi, d, h, order, full = c["s"], c["bi"], c["d"], c["h"], c["order"], c["full"]
            qT, qTb, kT, kTb, ktk, ktkb, vx, vxb = c["qT"], c["qTb"], c["kT"], c["kTb"], c["ktk"], c["ktkb"], c["vx"], c["vxb"]
            tok0 = seq_off[s] + bi * TB
            tri = C_TF if d == 0 else C_TB
            u0 = h * 2 * VS
            def chunk(oi, ci):
                slot = oi + 1
                g0 = slot * 24
                gp = (slot - 1) * 24
                first = c["first_blk"] and oi == 0
                last = c["last_blk"] and oi == len(order) - 1
                rows = slice(tok0 + ci * 128, tok0 + (ci + 1) * 128)
                key = (s, bi, ci, h)
                if not first:
                    cb_, cbb = cbf.get()
                    act(cb_[:, :], U[:, u0:u0 + 2 * VS], AF.Copy, [Ub[h], GTb[slot - 1]], [cbb], scale=GT[:, gp + 16 + h:gp + 17 + h])
                hd, hdb = tFr.get()
                if full and VARIANT != "v2":
                    dma("sp", hd[:, 256:512], hb_d[rows, h * 256:(h + 1) * 256], [hbb[key]], [hdb])
                pS, pSb = bank()
                mm(pS[:, 0:128], [(kT[:, j * TB + ci * 128:j * TB + (ci + 1) * 128], qT[:, j * TB + ci * 128:j * TB + (ci + 1) * 128])
                                  for j in range(2)], [kTb, qTb], [pSb])
                Pm, Pmb = tHr.get()
                tt("dve", Pm[:, 0:128], pS[:, 0:128], cst[:, tri:tri + 128], ALU.mult, [pSb, cstb], [Pmb])
                if not last:
                    for j in range(2):
                        pC, pCb = bank()
                        mm(pC[:, 0:257], [(ktk[:, ci * 256 + j * 128:ci * 256 + (j + 1) * 128], vx[:, ci * VS:ci * VS + 257])],
                           [ktkb, vxb], [pCb])
                        uo = U[:, u0 + j * VS:u0 + j * VS + 257]
                        if first:
                            P.op("dve", lambda E, uo=uo, pC=pC: E.tensor_copy(out=uo, in_=pC[:, 0:257]), [pCb], [Ub[h]])
                        else:
                            stt("dve", uo, uo, GT[:, gp + 16 + h:gp + 17 + h], pC[:, 0:257], ALU.mult, ALU.add,
                                [pCb, Ub[h], GTb[slot - 1]], [Ub[h]])
                yield
                pN, pNb = bank()
                pairs = [(Pm[:, 0:128], vx[:, ci * VS:ci * VS + 257])]
                rd = [Pmb, vxb]
                if not first:
                    for j in range(2):
                        pairs.append((qT[:, j * TB + ci * 128:j * TB + (ci + 1) * 128], cb_[:, j * VS:j * VS + 257]))
                    rd += [qTb, cbb]
                mm(pN[:, 0:257], pairs, rd, [pNb])
                r0, r0b = smr.get()
                tt("dve", r0[:, 0:1], pN[:, 256:257], GT[:, g0 + 8 + h:g0 + 9 + h], ALU.max, [pNb, GTb[slot]], [r0b])
                r1, r1b = smr.get()
                stt("dve", r1[:, 0:1], pN[:, 256:257], -1.0, r0[:, 0:1], ALU.mult, ALU.max, [pNb, r0b], [r1b])
                r2, r2b = smr.get()
                P.op("dve", lambda E, r1=r1, r2=r2: E.reciprocal(out=r2[:, 0:1], in_=r1[:, 0:1]), [r1b], [r2b])
                act(hd[:, 0:256], pN[:, 0:256], AF.Copy, [pNb, r2b], [hdb], scale=r2[:, 0:1])
                if not full:
                    hbb[key] = Buf("hb")
                    dma("pool", hb_d[rows, h * 256:(h + 1) * 256], hd[:, 0:256], [hdb], [hbb[key]])
                    yield
                    return
                goz, gozb = c["goz"], c["gozb"]
                if VARIANT == "v2":
                    dma("sp", hd[:, 256:512], hb_d[rows, h * 256:(h + 1) * 256], [hbb[key]], [hdb])
                hs, hsb = tFr.get()
                tt("dve", hs[:, 0:256], hd[:, 0:256], hd[:, 256:512], ALU.add, [hdb], [hsb])
                st6, st6b = smr.get()
                P.op("dve", lambda E, st6=st6, hs=hs: E.bn_stats(out=st6[:, 0:6], in_=hs[:, 0:256]), [hsb], [st6b])
                mv, mvb = smr.get()
                P.op("dve", lambda E, st6=st6, mv=mv: E.bn_aggr(out=mv[:, 0:2], in_=st6[:, 0:6]), [st6b], [mvb])
                l1, l1b = smr.get()
                act(l1[:, 0:1], mv[:, 1:2], AF.Ln, [mvb], [l1b], bias=float(EPS))
                rs_, rsb = smr.get()
                act(rs_[:, 0:1], l1[:, 0:1], AF.Exp, [l1b], [rsb], scale=-0.5, bias=float(np.log(0.25)))
                stt("dve", hs[:, 256:512], hs[:, 0:256], mv[:, 0:1], goz[:, ci * 256:(ci + 1) * 256], ALU.subtract, ALU.mult,
                    [hsb, mvb, gozb], [hsb])
                ym, ymb = tHr.get()
                stt("dve", ym[:, 0:256], hs[:, 256:512], rs_[:, 0:1], cst[:, C_ZERO:C_ZERO + 256], ALU.mult, ALU.add,
                    [hsb, rsb, cstb], [ymb])
                yield
                pt, ptb = bankb()
                for j in range(2):
                    P.op("pe", lambda E, j=j, pt=pt, ym=ym: E.transpose(out=pt[:, j * 128:(j + 1) * 128],
                                                                       in_=ym[:, j * 128:(j + 1) * 128], identity=identb[:, :]),
                         [ymb, identbb], [ptb])
                for j in range(2):
                    act(ymT[:, (2 * h + j) * TB + ci * 128:(2 * h + j) * TB + (ci + 1) * 128], pt[:, j * 128:(j + 1) * 128], AF.Copy,
                        [ptb, cstb], [ymTb[h]], scale=(1.0 if VARIANT == "v1" else cst[:, C_MHG + 2 * h + j:C_MHG + 2 * h + j + 1]))
                yield

            gens = [chunk(oi, ci) for oi, ci in enumerate(order)]
            if len(gens) == 2:
                for gi in ((0, 1, 0, 1, 0, 1) if full else (0, 1, 0, 1)):
                    next(gens[gi], None)
                    yield
            else:
                for g in gens:
                    for _ in g:
                        yield

        def phase_b(s, bi):
            for ft in range(16):
                w1, w1b = load_unit(("b1", ft))
                w2, w2b = load_unit(("b2", ft))
                p1, p1b = bank()
                for half in range(2):
                    mm(p1[:, half * 256:(half + 1) * 256],
                       [(w1[:, kt * 256 + half * 128:kt * 256 + (half + 1) * 128], hT[:, kt * TB:(kt + 1) * TB]) for kt in range(KT)],
                       [w1b, hTb], [p1b])
                p2, p2b = bank()
                for half in range(2):
                    mm(p2[:, half * 256:(half + 1) * 256],
                       [(w2[:, kt * 256 + half * 128:kt * 256 + (half + 1) * 128], hT[:, kt * TB:(kt + 1) * TB]) for kt in range(KT)],
                       [w2b, hTb], [p2b])
                if CONV_ENG == "actpool":
                    ccs, ccsb = tF.get()
                    act(ccs[:, 0:TB], p1[:, 256:512], AF.Copy, [p1b], [ccsb])
                    u, ub = tF.get()
                    tt("dve", u[:, 1:TB + 1], ccs[:, 0:TB], p2[:, 0:256], ALU.mult, [ccsb, p2b], [ub])
                    if bi == 0:
                        P.op("pool", lambda E, u=u: E.memset(u[:, 0:1], 0.0), [], [ub])
                    else:
                        P.op("pool", lambda E, u=u, ft=ft: E.tensor_copy(out=u[:, 0:1], in_=ulast[:, ft:ft + 1]), [ulastb[ft]], [ub])
                    if bi == nblk[s] - 1:
                        P.op("pool", lambda E, u=u: E.memset(u[:, TB + 1:TB + 2], 0.0), [], [ub])
                    else:
                        hi = halo_idx[(s, bi)]
                        P.op("pool", lambda E, u=u, ft=ft, hi=hi: E.tensor_copy(out=u[:, TB + 1:TB + 2],
                                                                               in_=uhalo[:, ft * NHALO + hi:ft * NHALO + hi + 1]),
                             [uhalob], [ub])
                    P.op("pool", lambda E, u=u, ft=ft: E.tensor_copy(out=ulast[:, ft:ft + 1], in_=u[:, TB:TB + 1]), [ub], [ulastb[ft]])
                    t3, t3b = tF.get()
                    act(ccs[:, 0:TB], u[:, 0:TB], AF.Copy, [ub, cstb], [ccsb], scale=cst[:, C_CW + ft:C_CW + ft + 1])
                    act(ccs[:, 256:512], u[:, 1:TB + 1], AF.Copy, [ub, cstb], [ccsb], scale=cst[:, C_CW + 16 + ft:C_CW + 17 + ft])
                    act(t3[:, 0:TB], u[:, 2:TB + 2], AF.Copy, [ub, cstb], [t3b], scale=cst[:, C_CW + 32 + ft:C_CW + 33 + ft])
                    tt("pool", ccs[:, 0:TB], ccs[:, 0:TB], ccs[:, 256:512], ALU.add, [ccsb], [ccsb])
                    tt("pool", t3[:, 256:512], ccs[:, 0:TB], t3[:, 0:TB], ALU.add, [ccsb, t3b], [t3b])
                    yfin = t3[:, 256:512]
                    ccsb = t3b
                elif CONV_ENG == "aligned":
                    ccs, ccsb = tF.get()
                    act(ccs[:, 0:TB], p1[:, 256:512], AF.Copy, [p1b], [ccsb])
                    act(ccs[:, 256:256 + TB - 1], p1[:, 257:512], AF.Copy, [p1b], [ccsb])
                    u, ub = tF.get()
                    um, umb = tF.get()
                    tt("dve", u[:, 0:TB], ccs[:, 0:TB], p2[:, 0:TB], ALU.mult, [ccsb, p2b], [ub])
                    tt("dve", u[:, 256:256 + TB - 1], ccs[:, 256:256 + TB - 1], p2[:, 1:TB], ALU.mult, [ccsb, p2b], [ub])
                    tt("dve", um[:, 1:TB], ccs[:, 0:TB - 1], p2[:, 0:TB - 1], ALU.mult, [ccsb, p2b], [umb])
                    if bi == 0:
                        P.op("pool", lambda E, um=um: E.memset(um[:, 0:1], 0.0), [], [umb])
                    else:
                        P.op("pool", lambda E, um=um, ft=ft: E.tensor_copy(out=um[:, 0:1], in_=ulast[:, ft:ft + 1]), [ulastb[ft]], [umb])
                    if bi == nblk[s] - 1:
                        P.op("pool", lambda E, u=u: E.memset(u[:, 256 + TB - 1:256 + TB], 0.0), [], [ub])
                    else:
                        hi = halo_idx[(s, bi)]
                        P.op("pool", lambda E, u=u, ft=ft, hi=hi: E.tensor_copy(out=u[:, 256 + TB - 1:256 + TB],
                                                                               in_=uhalo[:, ft * NHALO + hi:ft * NHALO + hi + 1]),
                             [uhalob], [ub])
                    P.op("pool", lambda E, u=u, ft=ft: E.tensor_copy(out=ulast[:, ft:ft + 1], in_=u[:, TB - 1:TB]), [ub], [ulastb[ft]])
                    stt("dve", ccs[:, 0:TB], um[:, 0:TB], cst[:, C_CW + ft:C_CW + ft + 1], cst[:, C_ZERO:C_ZERO + TB], ALU.mult, ALU.add,
                        [umb, ccsb, ub, cstb], [ccsb])
                    stt("dve", ccs[:, 256:512], u[:, 0:TB], cst[:, C_CW + 16 + ft:C_CW + 17 + ft], ccs[:, 0:TB], ALU.mult, ALU.add,
                        [ub, ccsb, cstb], [ccsb])
                    stt("dve", ccs[:, 0:TB], u[:, 256:512], cst[:, C_CW + 32 + ft:C_CW + 33 + ft], ccs[:, 256:512], ALU.mult, ALU.add,
                        [ub, ccsb, cstb], [ccsb])
                    yfin = ccs[:, 0:TB]
                else:
                    ccs, ccsb = tF.get()
                    act(ccs[:, 0:TB], p1[:, 256:512], AF.Copy, [p1b], [ccsb])
                    u, ub = tF.get()
                    tt("dve", u[:, 1:TB + 1], ccs[:, 0:TB], p2[:, 0:256], ALU.mult, [ccsb, p2b], [ub])
                    if bi == 0:
                        P.op("pool", lambda E, u=u: E.memset(u[:, 0:1], 0.0), [], [ub])
                    else:
                        P.op("pool", lambda E, u=u, ft=ft: E.tensor_copy(out=u[:, 0:1], in_=ulast[:, ft:ft + 1]), [ulastb[ft]], [ub])
                    if bi == nblk[s] - 1:
                        P.op("pool", lambda E, u=u: E.memset(u[:, TB + 1:TB + 2], 0.0), [], [ub])
                    else:
                        hi = halo_idx[(s, bi)]
                        P.op("pool", lambda E, u=u, ft=ft, hi=hi: E.tensor_copy(out=u[:, TB + 1:TB + 2],
                                                                               in_=uhalo[:, ft * NHALO + hi:ft * NHALO + hi + 1]),
                             [uhalob], [ub])
                    P.op("pool", lambda E, u=u, ft=ft: E.tensor_copy(out=ulast[:, ft:ft + 1], in_=u[:, TB:TB + 1]), [ub], [ulastb[ft]])
                    stt(CONV_ENG, ccs[:, 256:512], u[:, 0:TB], cst[:, C_CW + ft:C_CW + ft + 1], cst[:, C_ZERO:C_ZERO + TB], ALU.mult, ALU.add,
                        [ub, cstb], [ccsb])
                    stt(CONV_ENG, ccs[:, 0:256], u[:, 1:TB + 1], cst[:, C_CW + 16 + ft:C_CW + 17 + ft], ccs[:, 256:512], ALU.mult, ALU.add,
                        [ub, ccsb, cstb], [ccsb])
                    stt(CONV_ENG, ccs[:, 256:512], u[:, 2:TB + 2], cst[:, C_CW + 32 + ft:C_CW + 33 + ft], ccs[:, 0:256], ALU.mult, ALU.add,
                        [ub, ccsb, cstb], [ccsb])

                    yfin = ccs[:, 256:512]
                tz, tzb = tF.get()
                act(tz[:, 0:TB], p2[:, 256:512], AF.Tanh, [p2b], [tzb], scale=0.5)
                stt("dve", tz[:, 256:512], tz[:, 0:TB], 1.0, p2[:, 256:512], ALU.add, ALU.mult, [tzb, p2b], [tzb])
                tt("dve", tz[:, 0:256], tz[:, 256:512], p1[:, 0:256], ALU.mult, [tzb, p1b], [tzb])
                stt("dve", ycT[:, ft * TB:(ft + 1) * TB], yfin, 0.5, tz[:, 0:256], ALU.mult, ALU.mult, [ccsb, tzb], [ycTb[ft]])
                yield

        def c_proj(s, bi, h, c):
            km_, km_b = kmh.get()
            dma("sp", km_[:, :], km_d[s][:, 4 * h * NMEM:(4 * h + 4) * NMEM], [], [km_b], extra=km_ops[s])
            vm_, vm_b = vmh.get()
            for mt in range(2):
                dma("sp", vm_[:, mt * HDA:(mt + 1) * HDA], vm_d[s][:, mt * D + h * HDA:mt * D + (h + 1) * HDA], [], [vm_b], extra=vm_ops[s])
            qa, qab = qas.get()
            c.update(km_=km_, km_b=km_b, vm_=vm_, vm_b=vm_b, qa=qa, qab=qab)
            for j2 in range(2):
                w, wb_ = load_unit(("qa", h, j2))
                pa, pb = bank()
                for jj in range(2):
                    mm(pa[:, jj * 256:(jj + 1) * 256],
                       [(w[:, kt * 256 + jj * 128:kt * 256 + (jj + 1) * 128], hT[:, kt * TB:(kt + 1) * TB]) for kt in range(KT)],
                       [wb_, hTb], [pb])
                act(qa[:, j2 * 2 * TB:(j2 + 1) * 2 * TB], pa, AF.Copy, [pb], [qab])
                yield

        def c_attn(s, bi, h, c):
            sc = float(HDA ** -0.5)
            km_, km_b, vm_, vm_b, qa, qab = c["km_"], c["km_b"], c["vm_"], c["vm_b"], c["qa"], c["qab"]
            pT, pTb = pTs.get()
            pns = []
            for ci in range(NCH):
                pS, pSb = bank()
                mm(pS[:, 0:NMEM], [(qa[:, j * TB + ci * 128:j * TB + (ci + 1) * 128], km_[:, j * NMEM:(j + 1) * NMEM]) for j in range(4)],
                   [qab, km_b], [pSb])
                mx, mxb = sm.get()
                P.op("dve", lambda E, mx=mx, pS=pS: E.reduce_max(out=mx[:, 0:1], in_=pS[:, 0:NMEM], axis=mybir.AxisListType.X), [pSb], [mxb])
                nm, nmb = sm.get()
                act(nm[:, 0:1], mx[:, 0:1], AF.Copy, [mxb], [nmb], scale=-sc)
                pe_, peb = tF.get()
                rs_, rsb = sm.get()
                act(pe_[:, 0:NMEM], pS[:, 0:NMEM], AF.Exp, [pSb, nmb], [peb, rsb], scale=sc, bias=nm[:, 0:1], accum=rs_[:, 0:1])
                ri, rib = sm.get()
                P.op("dve", lambda E, ri=ri, rs_=rs_: E.reciprocal(out=ri[:, 0:1], in_=rs_[:, 0:1]), [rsb], [rib])
                pn, pnb = tH.get()
                stt("dve", pn[:, 0:NMEM], pe_[:, 0:NMEM], ri[:, 0:1], cst[:, C_ZERO:C_ZERO + NMEM], ALU.mult, ALU.add, [peb, rib, cstb], [pnb])
                pns.append((pn, pnb))
                yield
            for ci in range(NCH):
                pn, pnb = pns[ci]
                pt, ptb = bankb()
                for mt in range(2):
                    P.op("pe", lambda E, mt=mt, pt=pt, pn=pn: E.transpose(out=pt[:, mt * 128:(mt + 1) * 128],
                                                                         in_=pn[:, mt * 128:(mt + 1) * 128], identity=identb[:, :]),
                         [pnb, identbb], [ptb])
                for mt in range(2):
                    act(pT[:, mt * TB + ci * 128:mt * TB + (ci + 1) * 128], pt[:, mt * 128:(mt + 1) * 128], AF.Copy, [ptb], [pTb])
                yield
            for j2 in range(2):
                w, wb_ = load_unit(("za", h, j2))
                for jj in range(2):
                    j = 2 * j2 + jj
                    poz, pozb = bank()
                    mm(poz[:, 256:512], [(w[:, kt * 256 + jj * 128:kt * 256 + (jj + 1) * 128], hT[:, kt * TB:(kt + 1) * TB]) for kt in range(KT)],
                       [wb_, hTb], [pozb])
                    mm(poz[:, 0:256], [(vm_[:, mt * HDA + j * 128:mt * HDA + (j + 1) * 128], pT[:, mt * TB:(mt + 1) * TB]) for mt in range(2)],
                       [vm_b, pTb], [pozb])
                    tz, tzb = tF.get()
                    act(tz[:, 0:TB], poz[:, 256:512], AF.Tanh, [pozb], [tzb], scale=0.5)
                    stt("dve", tz[:, 256:512], tz[:, 0:TB], 1.0, poz[:, 256:512], ALU.add, ALU.mult, [tzb, pozb], [tzb])
                    ft = 4 * h + j
                    stt("dve", yaT[:, ft * TB:(ft + 1) * TB], poz[:, 0:256], 0.5, tz[:, 256:512], ALU.mult, ALU.mult, [pozb, tzb], [yaTb[h]])
                    yield

        def phase_c(s, bi):
            cc_ = [dict() for _ in range(NHA)]
            if not CPIPE:
                for h in range(NHA):
                    yield from c_proj(s, bi, h, cc_[h])
                    yield from c_attn(s, bi, h, cc_[h])
                return
            yield from c_proj(s, bi, 0, cc_[0])
            for h in range(NHA):
                fg = c_attn(s, bi, h, cc_[h])
                bg = c_proj(s, bi, h + 1, cc_[h + 1]) if h + 1 < NHA else iter(())
                for _ in fg:
                    step(bg, 1)
                    yield
                for _ in bg:
                    yield

        def phase_d(s, bi):
            tok0 = seq_off[s] + bi * TB
            ysrc = [(ymT, ymTb), (ycT, ycTb), (yaT, yaTb)]
            for pr in range(8):
                accs = [None, None]
                for b in range(3):
                    wm_, wmb = load_unit(("mg", pr, b))
                    wb2, wb2b = load_unit(("wb", pr, b))
                    yt_, ybl = ysrc[b]
                    for jj in range(2):
                        nt = 2 * pr + jj
                        pg, pgb = bank()
                        mm(pg[:, 0:256], [(wm_[:, kt * 256 + jj * 128:kt * 256 + (jj + 1) * 128], hT[:, kt * TB:(kt + 1) * TB]) for kt in range(KT)],
                           [wmb, hTb], [pgb])
                        mm(pg[:, 256:512], [(wb2[:, kt * 256 + jj * 128:kt * 256 + (jj + 1) * 128], yt_[:, kt * TB:(kt + 1) * TB]) for kt in range(KT)],
                           [wb2b] + ybl, [pgb])
                        tg, tgb = tF.get()
                        act(tg[:, 0:TB], pg[:, 0:256], AF.Tanh, [pgb], [tgb], scale=0.5)
                        stt("dve", tg[:, 256:512], tg[:, 0:TB], 1.0, pg[:, 256:512], ALU.add, ALU.mult, [tgb, pgb], [tgb])
                        if b == 0:
                            accs[jj] = (tg, tgb)
                        elif b == 1:
                            acc, accb = accs[jj]
                            tt("pool", tg[:, 0:256], acc[:, 256:512], tg[:, 256:512], ALU.add, [accb, tgb], [tgb])
                            accs[jj] = (tg, tgb)
                        else:
                            acc, accb = accs[jj]
                            tt("pool", mgT[:, nt * TB:(nt + 1) * TB], acc[:, 0:256], tg[:, 256:512], ALU.add, [accb, tgb], [mgTb[nt]])
            for u in range(8):
                w, wb_ = load_unit(("out", u))
                pa, pb = bank()
                for ci in range(NCH):
                    mm(pa[:, ci * 256:(ci + 1) * 256],
                       [(mgT[:, kt * TB + ci * 128:kt * TB + (ci + 1) * 128], w[:, kt * 256:(kt + 1) * 256]) for kt in range(KT)],
                       [wb_] + mgTb, [pb])
                for ci in range(NCH):
                    xo = xtok[ci][:, u * 256:(u + 1) * 256]
                    stt("dve", xo, pa[:, ci * 256:(ci + 1) * 256], 0.5, xo, ALU.mult, ALU.add, [pb, xtokb[ci]], [xtokb[ci]])
            for ci in range(NCH):
                act(junk[:, :], xtok[ci][:, :], AF.Square, [xtokb[ci]], [junkb, ssqb], accum=ssq[:, 2 + ci:3 + ci])
                act(lnt[:, 2 + ci:3 + ci], ssq[:, 2 + ci:3 + ci], AF.Ln, [ssqb], [lntb], bias=float(D * EPS))
                act(rstd[:, 2 + ci:3 + ci], lnt[:, 2 + ci:3 + ci], AF.Exp, [lntb], [rstdb], scale=-0.5, bias=float(0.5 * np.log(D)))
                if FIN == "split":
                    for hf in range(2):
                        xs = xtok[ci][:, hf * 1024:(hf + 1) * 1024]
                        stt("pool" if hf else "dve", xs, xs, rstd[:, 2 + ci:3 + ci], cst[:, C_FG + hf * 1024:C_FG + (hf + 1) * 1024],
                            ALU.mult, ALU.mult, [xtokb[ci], rstdb, cstb], [xtokb[ci]])
                else:
                    for hf in range(4):
                        xs = xtok[ci][:, hf * 512:(hf + 1) * 512]
                        stt("dve", xs, xs, rstd[:, 2 + ci:3 + ci], cst[:, C_FG + hf * 512:C_FG + (hf + 1) * 512],
                            ALU.mult, ALU.mult, [xtokb[ci], rstdb, cstb], [xtokb[ci]])
                o = dma("pool", y_out[tok0 + ci * 128:tok0 + (ci + 1) * 128, :], xtok[ci][:, :], [xtokb[ci]], [])
                out_ops.append(o)

        def drain(g):
            for _ in g:
                pass

        def step(g, n=1):
            for _ in range(n):
                try:
                    next(g)
                except StopIteration:
                    return False
            return True

        def interleave(fg, bg, k):
            for _ in fg:
                step(bg, k)
            drain(bg)

        def chain(*gs):
            for g in gs:
                yield from g

        def run_pass(d, full):
            for s in range(NSEQ):
                blocks = list(range(nblk[s]))
                if d == 1:
                    blocks = blocks[::-1]
                order = list(range(NCH)) if d == 0 else list(range(NCH))[::-1]
                for oi_b, bi in enumerate(blocks):
                    tok0 = seq_off[s] + bi * TB
                    norm_block([(x_tok[tok0 + c * 128:tok0 + (c + 1) * 128, :], 128) for c in range(NCH)],
                               x_T[blk_base[s] + bi], TB, C_G)
                    ctx = [mk_ctx(s, bi, d, h, order, oi_b == 0, oi_b == len(blocks) - 1, full) for h in range(NH)]
                    p0 = proj_a(ctx[0])
                    gg = chain(*[gates(ci, d, oi + 1) for oi, ci in enumerate(order)])
                    step(gg, 2)
                    step(p0, 1)
                    step(gg, 2)
                    step(p0, 1)
                    drain(gg)
                    drain(p0)
                    if full:
                        bc = chain(phase_b(s, bi), phase_c(s, bi))
                    for h in range(NH):
                        bgs = []
                        if h + 1 < NH:
                            bgs.append(proj_a(ctx[h + 1]))
                        bg = chain(*bgs)
                        fg = recur_a(ctx[h])
                        if full and not INTERLEAVE_BC:
                            interleave(fg, bg, 1)
                        elif full:
                            for _ in fg:
                                if not step(bg, 1):
                                    step(bc, 2)
                                else:
                                    step(bc, 1)
                            drain(bg)
                        else:
                            interleave(fg, bg, 1)
                    P.op("pool", lambda E: E.tensor_copy(out=GT[:, 16:24], in_=GT[:, NCH * 24 + 16:NCH * 24 + 24]),
                         [GTb[NCH]], [GTb[0]])
                    if full:
                        chk("p2a")
                        drain(bc)
                        chk("p2bc")
                        phase_d(s, bi)
                        chk("p2d")

        try:
            chk("cast")
            for s in range(NSEQ):
                kv_prologue(s)
            chk("kv")
            halo_prepass()
            chk("halo")
            run_pass(1, False)
            chk("pass1")
            run_pass(0, True)
        except _Stop:
            pass
        if stop is not None:
            out_ops = [o for e in P.ENG for o in P.ops[e] if o.dma]
        P.op("sp", [], [], [], extra=out_ops)
        P.emit(nc, stack)
    return nc


def _tile_T(a):
    T = a.shape[0]
    return np.ascontiguousarray(a.T.reshape(KT, 128, T).transpose(1, 0, 2)).reshape(128, KT * T)


def prepare_shared(norm_g, w_in, b_if, conv_w, mem_norm_g, w_kv_mem, mh_norm_g, w_branch, w_out, final_norm_g):
    units, uidx = unit_catalog()
    srcs = {"in": w_in[0], "kv": w_kv_mem[0], "br0": w_branch[0, 0], "br1": w_branch[0, 1], "br2": w_branch[0, 2], "out": w_out[0]}
    ws32 = np.empty((len(units), 128, KT * 256), np.float32)
    for i, (src, cols) in enumerate(units):
        blk = srcs[src][:, cols]
        ws32[i] = blk.reshape(KT, 128, 256).transpose(1, 0, 2).reshape(128, KT * 256)
    perm = np.concatenate([np.arange(0, 8), np.arange(16, 24), np.arange(8, 16), np.arange(24, 32)])
    wgc = w_in[0][:, 10240 + perm]
    wg32 = np.ascontiguousarray(wgc.reshape(KT, 128, 32).transpose(1, 0, 2)).reshape(128, KT * 32)
    cst = np.zeros((128, C_END), np.float32)
    cst[:, C_ID:C_ID + 128] = np.eye(128, dtype=np.float32)
    cst[:, C_ONE:C_ONE + 128] = 1.0
    ii = np.arange(128)
    cst[:, C_TF:C_TF + 128] = (ii[:, None] <= ii[None, :]).astype(np.float32)
    cst[:, C_TB:C_TB + 128] = (ii[:, None] >= ii[None, :]).astype(np.float32)
    cst[:, C_G:C_G + 16] = norm_g[0].reshape(KT, 128).T
    cst[:, C_GM:C_GM + 16] = mem_norm_g[0].reshape(KT, 128).T
    for j in range(3):
        cst[:, C_CW + 16 * j:C_CW + 16 * (j + 1)] = conv_w[0, j].reshape(KT, 128).T
    cst[:, C_BIF:C_BIF + 32] = b_if[0][perm][None, :]
    cst[:, C_MHG:C_MHG + 16] = mh_norm_g[0].reshape(KT, 128).T
    cst[:, C_FG:C_FG + D] = final_norm_g[None, :]
    return ws32, wg32, cst


def prepare_core(seqs, mems):
    x_tok = np.ascontiguousarray(np.concatenate(seqs, axis=0))
    blocks = []
    halos = []
    for x in seqs:
        nb = x.shape[0] // TB
        for bi in range(nb):
            blocks.append(_tile_T(x[bi * TB:(bi + 1) * TB]))
            if bi < nb - 1:
                halos.append(x[(bi + 1) * TB])
    x_T = np.stack(blocks, axis=0)
    h_tok = np.zeros((32, D), np.float32)
    if halos:
        h_tok[:len(halos)] = np.stack(halos, axis=0)
    h_tok[len(halos):] = 1.0
    h_T = _tile_T(h_tok)
    m_tok = np.ascontiguousarray(np.concatenate(mems, axis=0))
    m_T = np.stack([_tile_T(m) for m in mems], axis=0)
    return {"x_tok": x_tok, "x_T": x_T, "m_tok": m_tok, "m_T": m_T, "h_tok": h_tok, "h_T": h_T}


def kernel(x_prompt, x_sample, mem_prompt, mem_sample, norm_g, w_in, b_if, conv_w, mem_norm_g,
           w_kv_mem, mh_norm_g, w_branch, w_out, final_norm_g):
    f = lambda a: np.asarray(a, dtype=np.float32)
    x_prompt, x_sample, mem_prompt, mem_sample = f(x_prompt), f(x_sample), f(mem_prompt), f(mem_sample)
    ws32, wg32, cst = prepare_shared(f(norm_g), f(w_in), f(b_if), f(conv_w), f(mem_norm_g), f(w_kv_mem), f(mh_norm_g),
                                     f(w_branch), f(w_out), f(final_norm_g))
    n = 8
    SP, SS = x_prompt.shape[1], x_sample.shape[1]
    nc = build_program([SP, SS, SS])
    in_maps = []
    for i in range(n):
        m = prepare_core([x_prompt[i], x_sample[2 * i], x_sample[2 * i + 1]],
                         [mem_prompt[i], mem_sample[2 * i], mem_sample[2 * i + 1]])
        m.update({"ws32": ws32, "wg32": wg32, "cst_in": cst})
        in_maps.append(m)
    res = run_bass_kernel_spmd(nc, in_maps, core_ids=list(range(n)))
    y_prompt = np.empty_like(x_prompt)
    y_sample = np.empty_like(x_sample)
    for i in range(n):
        y = res.results[i]["y"]
        y_prompt[i] = y[0:SP]
        y_sample[2 * i] = y[SP:SP + SS]
        y_sample[2 * i + 1] = y[SP + SS:SP + 2 * SS]
    return (y_prompt, y_sample)
```
